# Optimizing a Trainium2 kernel written in Bass

```python
import jax, jax.numpy as jnp
from jax import lax
import numpy as np

D_MODEL = 1024
BATCH = 8
SEQ = 2048
DEPTH = 2
DEC_BATCH = 128
DEC_SEQ = 1
PAST_LEN = 16384
PAGE_SIZE = 128

N_MIXERS = 2
N_A_LAYERS = (DEPTH + N_MIXERS - 1) // N_MIXERS
N_B_LAYERS = DEPTH // N_MIXERS
SG_CHUNK = 128
SG_WIDTH = 2 * D_MODEL
SG_GROUPS = 8
SG_GROUP_DIM = SG_WIDTH // SG_GROUPS
GLA_HEADS = 4
GLA_KEY_DIM = D_MODEL // 2
GLA_VALUE_DIM = D_MODEL
GLA_HEAD_K = GLA_KEY_DIM // GLA_HEADS
GLA_HEAD_V = GLA_VALUE_DIM // GLA_HEADS
GLA_GATE_RANK = 16
GLA_GATE_NORMALIZER = 16.0
GLA_CHUNK = 64
GLA_IN_DIM = 2 * GLA_KEY_DIM + 2 * GLA_VALUE_DIM + GLA_GATE_RANK
D_FF = ((8 * D_MODEL + 3 * 256 - 1) // (3 * 256)) * 256
EPS = 1e-6

kernel_name = "hybrid_sgu_gla_decoder_step"


def rms_norm(x, g):
    xf = x.astype(jnp.float32)
    y = xf * lax.rsqrt(jnp.mean(xf * xf, axis=-1, keepdims=True) + EPS)
    return (y * g.astype(jnp.float32)).astype(x.dtype)


def layer_norm(x, g, b):
    xf = x.astype(jnp.float32)
    mu = jnp.mean(xf, axis=-1, keepdims=True)
    var = jnp.mean(jnp.square(xf - mu), axis=-1, keepdims=True)
    y = (xf - mu) * lax.rsqrt(var + EPS)
    return (y * g.astype(jnp.float32) + b.astype(jnp.float32)).astype(x.dtype)


def swiglu(x, w13, w2):
    a, b = jnp.split(x @ w13, 2, axis=-1)
    return (jax.nn.silu(a) * b) @ w2


def spatial_gate_mixer(x, w_in, ln_g, ln_b, w_s, b_s, w_out):
    B, L, _ = x.shape
    z = jax.nn.gelu(x @ w_in, approximate=False)
    u, v = jnp.split(z, 2, axis=-1)
    v = layer_norm(v, ln_g, ln_b)
    cw = min(SG_CHUNK, L)
    n_chunks = -(-L // cw)
    lp = n_chunks * cw
    vp = jnp.pad(v, ((0, 0), (0, lp - L), (0, 0))).reshape(B, n_chunks, cw, SG_GROUPS, SG_GROUP_DIM)
    mask = jnp.tril(jnp.ones((cw, cw), dtype=bool))
    ws = jnp.where(mask, w_s[:, :cw, :cw], 0).astype(v.dtype)
    bias = b_s[:, :cw].T[:, :, None].astype(v.dtype)
    mixed = jnp.einsum('gts,bnsgc->bntgc', ws, vp) + bias
    mixed = mixed.reshape(B, lp, SG_WIDTH)[:, :L]
    y = (u * mixed) @ w_out
    n_last = (L - 1) % SG_CHUNK + 1
    return y, v[:, L - n_last:]


def gla_chunk_scan(q, k, v, g, s0):
    B, L = q.shape[:2]
    c = min(GLA_CHUNK, L)
    n = -(-L // c)
    lp = n * c

    def blocks(a):
        a = jnp.pad(a, ((0, 0), (0, lp - L), (0, 0), (0, 0)))
        return a.reshape(B, n, c, a.shape[2], a.shape[3]).transpose(1, 0, 3, 2, 4)

    qc, kc, vc, gc = blocks(q), blocks(k), blocks(v), blocks(g)
    mask = jnp.tril(jnp.ones((c, c), dtype=bool))[:, :, None]

    def step(S, inp):
        qi, ki, vi, gi = inp
        b = jnp.cumsum(gi, axis=2)
        o_inter = jnp.einsum('bhtk,bhkv->bhtv', qi * jnp.exp(b), S)
        diff = b[:, :, :, None, :] - b[:, :, None, :, :]
        decay = jnp.exp(jnp.where(mask, diff, -jnp.inf))
        attn = jnp.einsum('bhtk,bhsk,bhtsk->bhts', qi, ki, decay)
        o = o_inter + jnp.einsum('bhts,bhsv->bhtv', attn, vi)
        b_last = b[:, :, -1:, :]
        k_dec = ki * jnp.exp(b_last - b)
        S = jnp.exp(b_last[:, :, 0, :])[..., None] * S + jnp.einsum('bhsk,bhsv->bhkv', k_dec, vi)
        return S, o

    S, o = lax.scan(step, s0, (qc, kc, vc, gc))
    o = o.transpose(1, 0, 3, 2, 4).reshape(B, lp, GLA_HEADS, GLA_HEAD_V)[:, :L]
    return o, S


def gla_mixer(x, w_in, w_gate_up, b_gate, out_norm_g, w_out, s0):
    B, L, _ = x.shape
    proj = x @ w_in
    q, k, v, g_out, r = jnp.split(
        proj, [GLA_KEY_DIM, 2 * GLA_KEY_DIM, 2 * GLA_KEY_DIM + GLA_VALUE_DIM, 2 * GLA_KEY_DIM + 2 * GLA_VALUE_DIM], axis=-1)
    log_a = jax.nn.log_sigmoid((r @ w_gate_up + b_gate).astype(jnp.float32)) / GLA_GATE_NORMALIZER
    heads_k = lambda a: a.astype(jnp.float32).reshape(B, L, GLA_HEADS, GLA_HEAD_K)
    qh = heads_k(q) * (GLA_HEAD_K ** -0.5)
    kh = heads_k(k)
    gh = heads_k(log_a)
    vh = v.astype(jnp.float32).reshape(B, L, GLA_HEADS, GLA_HEAD_V)
    o, S = gla_chunk_scan(qh, kh, vh, gh, s0)
    o = rms_norm(o, out_norm_g).reshape(B, L, GLA_VALUE_DIM).astype(x.dtype)
    o = o * jax.nn.silu(g_out)
    return o @ w_out, S


def trunk(x, gla_init, norm_g, ffn_w13, ffn_w2, sg_w_in, sg_ln_g, sg_ln_b, sg_w_s, sg_b_s, sg_w_out,
          gla_w_in, gla_w_gate_up, gla_b_gate, gla_out_norm_g, gla_w_out):
    v_rows, gla_states = [], []
    for i in range(DEPTH):
        j = i // N_MIXERS
        h = rms_norm(x, norm_g[i, 0])
        if i % N_MIXERS == 0:
            m, vr = spatial_gate_mixer(h, sg_w_in[j], sg_ln_g[j], sg_ln_b[j], sg_w_s[j], sg_b_s[j], sg_w_out[j])
            v_rows.append(vr)
        else:
            m, S = gla_mixer(h, gla_w_in[j], gla_w_gate_up[j], gla_b_gate[j], gla_out_norm_g[j], gla_w_out[j], gla_init[j])
            gla_states.append(S)
        x = x + rms_norm(m, norm_g[i, 1])
        h = rms_norm(x, norm_g[i, 2])
        x = x + rms_norm(swiglu(h, ffn_w13[i], ffn_w2[i]), norm_g[i, 3])
    return x, jnp.stack(v_rows), jnp.stack(gla_states)


def setup_inputs(seed: int = 0) -> dict:
    key = jax.random.key(seed)
    ks = jax.random.split(key, 20)
    nrm = lambda k, shape, s: jax.random.normal(k, shape, jnp.float32) * s
    return {
        "x_prompt": nrm(ks[0], (BATCH, SEQ, D_MODEL), 1.0),
        "x_sample": nrm(ks[1], (DEC_BATCH, DEC_SEQ, D_MODEL), 1.0),
        "state_gla": nrm(ks[2], (N_B_LAYERS, DEC_BATCH, GLA_HEADS, GLA_HEAD_K, GLA_HEAD_V), 1.0),
        "norm_g": 1.0 + nrm(ks[3], (DEPTH, 4, D_MODEL), 0.1),
        "ffn_w13": nrm(ks[4], (DEPTH, D_MODEL, 2 * D_FF), D_MODEL ** -0.5),
        "ffn_w2": nrm(ks[5], (DEPTH, D_FF, D_MODEL), D_FF ** -0.5),
        "sg_w_in": nrm(ks[6], (N_A_LAYERS, D_MODEL, 2 * SG_WIDTH), D_MODEL ** -0.5),
        "sg_ln_g": 1.0 + nrm(ks[7], (N_A_LAYERS, SG_WIDTH), 0.1),
        "sg_ln_b": nrm(ks[8], (N_A_LAYERS, SG_WIDTH), 0.02),
        "sg_w_s": nrm(ks[9], (N_A_LAYERS, SG_GROUPS, SG_CHUNK, SG_CHUNK), SG_CHUNK ** -0.5),
        "sg_b_s": 1.0 + nrm(ks[10], (N_A_LAYERS, SG_GROUPS, SG_CHUNK), 0.1),
        "sg_w_out": nrm(ks[11], (N_A_LAYERS, SG_WIDTH, D_MODEL), SG_WIDTH ** -0.5),
        "gla_w_in": nrm(ks[12], (N_B_LAYERS, D_MODEL, GLA_IN_DIM), D_MODEL ** -0.5),
        "gla_w_gate_up": nrm(ks[13], (N_B_LAYERS, GLA_GATE_RANK, GLA_KEY_DIM), GLA_GATE_RANK ** -0.5),
        "gla_b_gate": nrm(ks[14], (N_B_LAYERS, GLA_KEY_DIM), 0.1),
        "gla_out_norm_g": 1.0 + nrm(ks[15], (N_B_LAYERS, GLA_HEAD_V), 0.1),
        "gla_w_out": nrm(ks[16], (N_B_LAYERS, GLA_VALUE_DIM, D_MODEL), GLA_VALUE_DIM ** -0.5),
    }


def reference(x_prompt, x_sample, state_gla, norm_g, ffn_w13, ffn_w2, sg_w_in, sg_ln_g, sg_ln_b, sg_w_s,
              sg_b_s, sg_w_out, gla_w_in, gla_w_gate_up, gla_b_gate, gla_out_norm_g, gla_w_out):
    zero_state = jnp.zeros((N_B_LAYERS, x_prompt.shape[0], GLA_HEADS, GLA_HEAD_K, GLA_HEAD_V), jnp.float32)
    y_prompt, sg_v_prompt, gla_state_prompt = trunk(
        x_prompt, zero_state, norm_g, ffn_w13, ffn_w2, sg_w_in, sg_ln_g, sg_ln_b, sg_w_s, sg_b_s, sg_w_out,
        gla_w_in, gla_w_gate_up, gla_b_gate, gla_out_norm_g, gla_w_out)
    y_sample, sg_v_sample, gla_state_sample = trunk(
        x_sample, state_gla.astype(jnp.float32), norm_g, ffn_w13, ffn_w2, sg_w_in, sg_ln_g, sg_ln_b, sg_w_s, sg_b_s,
        sg_w_out, gla_w_in, gla_w_gate_up, gla_b_gate, gla_out_norm_g, gla_w_out)
    return (y_prompt, y_sample, sg_v_prompt, sg_v_sample,
            gla_state_prompt.astype(x_prompt.dtype), gla_state_sample.astype(state_gla.dtype))
```

```python
import numpy as np
import concourse.bass as bass
import concourse.mybir as mybir
from concourse.bass_utils import run_bass_kernel_spmd

F32 = mybir.dt.float32
BF16 = mybir.dt.bfloat16
AF = mybir.ActivationFunctionType
ALU = mybir.AluOpType
AX = mybir.AxisListType

D = 1024
SEQ = 2048
NCORE = 8
NS = 16
NPASS = 4
TPP = 4
PTOK = 512
TW = PTOK + NS
DFF = 2816
NFF = 22
EPS = 1e-6
NSLOT = 8
PE, ACT, DVE, POOL, SP = "pe", "act", "dve", "pool", "sp"


REGION_KEYS = {"gated", "v", "vhat", "ut", "lg_bc", "lb_bc", "mixs", "gTf", "sab", "V", "SG", "qT", "kT", "qTt", "kTt",
               "rT", "e32", "l", "og", "ogT", "Ktok_s", "EL", "Ep", "Em", "Ktok", "AT", "Kh", "S_in", "Snew", "Snbf", "Qm",
               "Kmask", "EGs", "Q32", "K32", "ws_tm", "ws_bf", "bs_bc", "hbx", "ystage", "xnext"}


class Plan:
    def __init__(self):
        self.ops = []
        self.last_w = {}
        self.readers = {}
        self.barrier = {}
        self.gbarrier = {}
        self.last_on = {}
        self.last_region_on = {}

    def _stream(self, eng, dma):
        return ("dma", dma) if dma is not None else eng

    def op(self, eng, fn, reads=(), writes=(), dma=None, nobarrier=False):
        idx = len(self.ops)
        deps = set()
        for k in reads:
            if k in self.last_w:
                deps.add(self.last_w[k])
        for k in writes:
            if k in self.last_w:
                deps.add(self.last_w[k])
            deps.update(self.readers.get(k, ()))
        region = any(k[0] in REGION_KEYS for k in reads) or any(k[0] in REGION_KEYS for k in writes)
        if not nobarrier and eng != PE:
            deps.update(self.gbarrier.values())
            if region:
                deps.update(self.barrier.values())
        self.ops.append(dict(eng=eng, fn=fn, deps=deps, dma=dma))
        for k in reads:
            self.readers.setdefault(k, []).append(idx)
        for k in writes:
            self.last_w[k] = idx
            self.readers[k] = []
        if not nobarrier:
            st = self._stream(eng, dma)
            self.last_on[st] = idx
            if region:
                self.last_region_on[st] = idx
        return idx

    def set_barrier(self):
        self.barrier = dict(self.last_region_on)

    def set_global_barrier(self):
        self.gbarrier = dict(self.last_on)


def build_program():
    specs = _build(None)
    return _build(specs)


def _build(schedule):
    nc = bass.Bass("TRN2", target_bir_lowering=False)
    P = Plan()
    specs = []

    def din(name, shape):
        return nc.dram_tensor(name, list(shape), F32, kind="ExternalInput").ap()

    def dout(name, shape):
        return nc.dram_tensor(name, list(shape), F32, kind="ExternalOutput").ap()

    xp = din("xp", [SEQ, D])
    xsam = din("xsam", [NS, D])
    st_in = din("st_in", [NS, 4, 128, 256])
    norm_g = din("norm_g", [2, 4, D])
    ffn_w13 = din("ffn_w13", [2, D, 2 * DFF])
    ffn_w2 = din("ffn_w2", [2, DFF, D])
    sg_w_in = din("sg_w_in", [1, D, 4096])
    sg_ln_g = din("sg_ln_g", [1, 2048])
    sg_ln_b = din("sg_ln_b", [1, 2048])
    sg_w_s = din("sg_w_s", [1, 8, 128, 128])
    sg_b_s = din("sg_b_s", [1, 8, 128])
    sg_w_out = din("sg_w_out", [1, 2048, D])
    gla_w_in = din("gla_w_in", [1, D, 3088])
    gla_w_gate_up = din("gla_w_gate_up", [1, 16, 512])
    gla_b_gate = din("gla_b_gate", [1, 512])
    gla_out_norm_g = din("gla_out_norm_g", [1, 256])
    gla_w_out = din("gla_w_out", [1, D, D])
    yp = dout("yp", [SEQ, D])
    ysam = dout("ysam", [NS, D])
    sgvp = dout("sgvp", [128, 2048])
    sgvs = dout("sgvs", [NS, 2048])
    stp = dout("stp", [4, 128, 256])
    sts = dout("sts", [NS, 4, 128, 256])

    base = 229376 - nc.sbuf_bytes_remaining
    base = (base + 63) // 64 * 64
    cur = [base]
    cnt = [0]

    def alloc(shape, dt, at=None):
        nbytes = int(np.prod(shape[1:])) * (4 if dt == F32 else 2)
        nbytes = (nbytes + 63) // 64 * 64
        if at is None:
            off = cur[0]
            cur[0] += nbytes
        else:
            off = at[0]
            at[0] += nbytes
        cnt[0] += 1
        assert off + nbytes <= 229344, ("sbuf overflow", off + nbytes)
        return nc.alloc_sbuf_tensor_at("t%d" % cnt[0], list(shape), dt, offset=off)

    xs = alloc([128, 5, D], F32)
    hT = alloc([128, 8, TW], BF16)
    slots = [alloc([128, 8, 512], BF16) for _ in range(NSLOT)]
    ident = alloc([128, 128], BF16)
    identf = alloc([16, 16], F32)
    LT = alloc([128, 128], BF16)
    ones = alloc([128, 128], BF16)
    colmask = alloc([128, 16, 16], BF16)
    wsT = alloc([128, 8, 128], BF16)
    Bc = alloc([128, 16, 128], BF16)
    gb = [alloc([128, D], F32) for _ in range(2)]
    gT = alloc([128, 4, 8], F32)
    lgT = alloc([128, 16], F32)
    lbT = alloc([128, 16], F32)
    gn_bc = alloc([128, 256], F32)
    w00 = alloc([16, 8], F32)
    b00 = alloc([16, 8], F32)
    mhalf = alloc([128, 8], F32)
    NST = 1280
    stat = alloc([128, NST], F32)
    hb = [alloc([128, D], BF16) for _ in range(2)]
    tmp = alloc([128, 2, D], F32)
    junk = alloc([128, 2048], BF16)
    jrot = [0]

    def jk(n):
        if n > 1024:
            return 0, [("junk", 0), ("junk", 1)]
        i_ = jrot[0] % 2
        jrot[0] += 1
        return i_ * 1024, [("junk", i_)]
    Sst = alloc([128, 4, 256], F32)
    Sbf = alloc([128, 4, 256], BF16)
    Wg = alloc([32, 512], BF16)
    ph0 = cur[0]

    HBX0 = (229344 - 4 * 2048) // 64 * 64

    def phase_alloc():
        return [ph0]

    a = phase_alloc()
    ws_tm = alloc([128, 8, 128], F32, a)
    ws_bf = alloc([128, 8, 128], BF16, a)
    bs_bc = alloc([128, 8, 128], F32, a)
    a = phase_alloc()
    gated = alloc([128, 16, TW], BF16, a)
    vbuf = [alloc([128, 2048], BF16, a) for _ in range(3)]
    vhat = [alloc([128, 2048], BF16, a) for _ in range(3)]
    ut = [alloc([128, 512], BF16, a) for _ in range(2)]
    lg_bc = alloc([128, 2048], F32, a)
    lb_bc = alloc([128, 2048], F32, a)
    mixs = alloc([16, 2048], BF16, a)
    assert a[0] <= HBX0, (a[0], HBX0)
    a = phase_alloc()
    gTf = alloc([128, NFF, TW], BF16, a)
    sab = [alloc([128, 512], BF16, a) for _ in range(2)]
    ystage = alloc([128, 4, D], F32, a)
    xnext = alloc([128, 4, D], F32, a)
    assert a[0] <= HBX0
    a_hbx = [HBX0]
    hbx = [alloc([128, D], BF16, a_hbx) for _ in range(4)]
    a = phase_alloc()
    Vt = alloc([128, 5, D], BF16, a)
    SG = alloc([128, 5, D], BF16, a)
    qT = alloc([128, 4, TW], BF16, a)
    kT = alloc([128, 4, TW], BF16, a)
    rT = alloc([32, TW], BF16, a)
    off_e32 = a[0]
    e32 = alloc([128, 512], F32, a)
    Lb = alloc([128, 5, 512], BF16, a)
    og = [alloc([128, D], BF16, a) for _ in range(2)]
    ogT = [alloc([128, 8, 128], BF16, a) for _ in range(2)]
    Ktok_s = alloc([16, 512], BF16, a)
    EL = alloc([128, 16], F32, a)
    a_mark = a[0]
    Ep = [alloc([128, 4, 128], F32, a)] * 2
    Em = [alloc([128, 4, 128], F32, a)] * 2
    Q32 = [alloc([128, 4, 128], F32, a) for _ in range(2)]
    K32 = [alloc([128, 4, 128], F32, a) for _ in range(2)]
    Ktok = [alloc([128, 4, 128], BF16, a) for _ in range(4)]
    ATb = [alloc([128, 4, 128], BF16, a) for _ in range(4)]
    Kh = [alloc([128, 4, 128], BF16, a) for _ in range(2)]
    a = [a_mark]
    S_in = [alloc([128, 4, 256], F32, a) for _ in range(2)] + [alloc([128, 4, 256], F32, [off_e32])]
    Snew = [alloc([128, 4, 256], F32, a) for _ in range(2)]
    Snbf = [alloc([128, 4, 256], BF16, a) for _ in range(2)]
    Qm = alloc([128, 4, 16, 16], BF16, a)
    Kmask = [alloc([16, 512], BF16, a) for _ in range(2)]
    EGs = alloc([128, 4, 16], F32, a)

    ps = nc.alloc_psum_tensor("ps", [128, 6 * 512], F32)
    pt = nc.alloc_psum_tensor("pt", [128, 2, 1024], BF16)

    def bank(b, np_=128, n=512, off=0):
        return ps[0:np_, b * 512 + off: b * 512 + off + n]

    stc = [0]

    def newstat(n=1):
        c = stc[0]
        stc[0] += n
        assert stc[0] <= NST
        return c

    def S(c, np_=128, n=1):
        return stat[0:np_, c:c + n]

    def mm_group(out_ap, pairs, skip=False):
        def fn(e):
            n = len(pairs)
            ins = None
            for i, (l, r) in enumerate(pairs):
                if skip:
                    ins = e.matmul(out_ap, l, r, start=(i == 0), stop=(i == n - 1), skip_group_check=True)
                else:
                    ins = e.matmul(out_ap, l, r, start=(i == 0), stop=(i == n - 1))
            return ins
        return fn

    slot_ctr = [0]
    issued = [0]
    released = set()
    wmap = dict(sg_w_in=sg_w_in, sg_w_out=sg_w_out, ffn_w13=ffn_w13, ffn_w2=ffn_w2, gla_w_in=gla_w_in,
                gla_w_out=gla_w_out)

    first_reads = [[("xs", t_) for t_ in range(TPP)]]

    def issue_slab(i, spec):
        wname, li_, r0, nkc, c0, w = spec
        s_ = i % NSLOT
        src = wmap[wname][li_][r0:r0 + nkc * 128, c0:c0 + w].rearrange("(kc p) n -> p kc n", p=128)
        dst = slots[s_][:, 0:nkc, 0:w]
        rd = first_reads[0] if (schedule is not None and i == 0) else []
        P.op(POOL, lambda e: e.dma_start(out=dst, in_=src), reads=rd, writes=[("slot", s_)], dma="slot%d" % s_,
             nobarrier=True)

    def pump(maxn=None):
        while issued[0] < len(schedule) and (issued[0] < NSLOT or (issued[0] - NSLOT) in released) and \
                (maxn is None or issued[0] < maxn):
            issue_slab(issued[0], schedule[issued[0]])
            issued[0] += 1

    def wslab(wname, li_, r0, nkc, c0, w):
        i = slot_ctr[0]
        slot_ctr[0] += 1
        s_ = i % NSLOT
        spec = (wname, li_, r0, nkc, c0, w)
        if schedule is None:
            specs.append(spec)
            issue_slab(i, spec)
        else:
            assert schedule[i] == spec, (i, schedule[i], spec)
            pump()
            assert issued[0] > i, ("slab not issuable (too many live slabs)", i)
        return slots[s_], ("slot", s_), i

    def wrelease(h):
        if schedule is None:
            return
        released.add(h[2])
        pump()

    def dma(out, in_, key, reads=(), writes=(), slow=False, eng=SP):
        if slow:
            fn = lambda e: e.dma_start(out=out, in_=in_, allow_slow_non_contiguous=True)
        else:
            fn = lambda e: e.dma_start(out=out, in_=in_)
        return P.op(eng, fn, reads=reads, writes=writes, dma=key)

    def rstd_from_ss(c_ss, np_, n, inv, cols=1):
        c1 = newstat(cols)
        c2 = newstat(cols)
        P.op(DVE, lambda e: e.tensor_scalar(S(c1, np_, cols), S(c_ss, np_, cols), inv, EPS, ALU.mult, ALU.add),
             reads=[("st", c_ss)], writes=[("st", c1)])
        P.op(POOL, lambda e: e.tensor_tensor(out=S(c2, np_, cols), in0=S(c1, np_, cols), in1=mhalf[0:np_, 0:cols], op=ALU.pow),
             reads=[("st", c1), ("c", "mhalf")], writes=[("st", c2)])
        return c2

    hbc = [0]
    ptc = [0]

    def xsrc(t, np_, nx):
        if nx:
            return xnext[0:np_, t, :], ("xnext", t)
        return xs[0:np_, t, :], ("xs", t)

    def norm_in_A_sq(t, np_, nx=False):
        c0 = newstat()
        j0, jkeys = jk(D)
        src, skey = xsrc(t, np_, nx)
        P.op(ACT, lambda e: e.activation(out=junk[0:np_, j0:j0 + D], in_=src, func=AF.Square,
                                         accum_out=S(c0, np_)),
             reads=[skey], writes=[("st", c0)] + jkeys)
        return c0

    def hb_of(hi):
        if isinstance(hi, tuple):
            return hbx[hi[1]], ("hbx", hi[1])
        return hb[hi], ("hb", hi)

    def norm_in_A_rest(t, np_, c0, xi=None, nx=False):
        c2 = rstd_from_ss(c0, np_, D, 1.0 / D)
        src, skey = xsrc(t, np_, nx)
        if xi is None:
            hi = hbc[0] % 2
            hbc[0] += 1
        else:
            hi = ("x", xi)
        buf, key = hb_of(hi)
        P.op(ACT, lambda e: e.activation(out=buf[0:np_, :], in_=src, func=AF.Copy,
                                         scale=S(c2, np_)),
             reads=[skey, ("st", c2)], writes=[key])
        return hi

    def norm_in_A(t, np_, xi=None, nx=False):
        return norm_in_A_rest(t, np_, norm_in_A_sq(t, np_, nx), xi, nx)

    def norm_in_B(t, np_, hi, gidx):
        pb = ptc[0] % 2
        ptc[0] += 1
        col0 = t * 128
        hbuf, hkey = hb_of(hi)

        def tr(e):
            ins = None
            for kc in range(8):
                ins = e.transpose(pt[:, pb, kc * 128: kc * 128 + np_], hbuf[0:np_, kc * 128:(kc + 1) * 128],
                                  ident[0:np_, 0:np_])
            return ins
        P.op(PE, tr, reads=[hkey, ("c", "ident")], writes=[("pt", pb)])
        src = pt[:, pb, :].rearrange("p (k c) -> p k c", c=128)[:, :, 0:np_]
        P.op(DVE, lambda e: e.tensor_tensor(out=hT[:, :, col0:col0 + np_], in0=src,
                                            in1=gT[:, gidx, :].unsqueeze(2).to_broadcast([128, 8, np_]),
                                            op=ALU.mult),
             reads=[("pt", pb), ("c", "gT", gidx)], writes=[("hT", t)])

    def norm_out_sq(t, np_, b0):
        m = ps[0:np_, b0 * 512:(b0 + 2) * 512]
        c0 = newstat()
        j0, jkeys = jk(D)
        P.op(ACT, lambda e: e.activation(out=junk[0:np_, j0:j0 + D], in_=m, func=AF.Square, accum_out=S(c0, np_)),
             reads=[("ps", b0), ("ps", b0 + 1)], writes=[("st", c0)] + jkeys)
        return c0

    def norm_out_rest(t, np_, b0, gbi, c0, ys=False):
        m = ps[0:np_, b0 * 512:(b0 + 2) * 512]
        c2 = rstd_from_ss(c0, np_, D, 1.0 / D)
        P.op(DVE, lambda e: e.scalar_tensor_tensor(out=tmp[0:np_, 0, :], in0=m, scalar=S(c2, np_),
                                                   in1=gb[gbi][0:np_, :], op0=ALU.mult, op1=ALU.mult),
             reads=[("ps", b0), ("ps", b0 + 1), ("st", c2), ("gb", gbi)], writes=[("tmp", 0)])
        if ys:
            P.op(DVE, lambda e: e.tensor_tensor(out=ystage[0:np_, t, :], in0=xs[0:np_, t, :], in1=tmp[0:np_, 0, :],
                                                op=ALU.add),
                 reads=[("tmp", 0), ("xs", t)], writes=[("ystage", t)])
        else:
            P.op(DVE, lambda e: e.tensor_tensor(out=xs[0:np_, t, :], in0=xs[0:np_, t, :], in1=tmp[0:np_, 0, :],
                                                op=ALU.add),
                 reads=[("tmp", 0), ("xs", t)], writes=[("xs", t)])

    def norm_out2(t, np_, bA, bB, gbi, ys=False):
        ca = newstat(2)
        for hh, bk in enumerate((bA, bB)):
            j0, jkeys = jk(512)
            P.op(ACT, lambda e, hh=hh, bk=bk, j0=j0: e.activation(out=junk[0:np_, j0:j0 + 512], in_=bank(bk, np_), func=AF.Square,
                                                                  accum_out=S(ca + hh, np_)),
                 reads=[("ps", bk)], writes=[("st", ca + hh)] + jkeys)
        c0 = newstat()
        P.op(DVE, lambda e: e.tensor_tensor(out=S(c0, np_), in0=S(ca, np_), in1=S(ca + 1, np_), op=ALU.add),
             reads=[("st", ca), ("st", ca + 1)], writes=[("st", c0)])
        c2 = rstd_from_ss(c0, np_, D, 1.0 / D)
        for hh, bk in enumerate((bA, bB)):
            P.op(DVE, lambda e, hh=hh, bk=bk: e.scalar_tensor_tensor(
                out=tmp[0:np_, 0, hh * 512:(hh + 1) * 512], in0=bank(bk, np_), scalar=S(c2, np_),
                in1=gb[gbi][0:np_, hh * 512:(hh + 1) * 512], op0=ALU.mult, op1=ALU.mult),
                reads=[("ps", bk), ("st", c2), ("gb", gbi)], writes=[("tmp", 0)])
        if ys:
            P.op(DVE, lambda e: e.tensor_tensor(out=ystage[0:np_, t, :], in0=xs[0:np_, t, :], in1=tmp[0:np_, 0, :],
                                                op=ALU.add),
                 reads=[("tmp", 0), ("xs", t)], writes=[("ystage", t)])
        else:
            P.op(DVE, lambda e: e.tensor_tensor(out=xs[0:np_, t, :], in0=xs[0:np_, t, :], in1=tmp[0:np_, 0, :],
                                                op=ALU.add),
                 reads=[("tmp", 0), ("xs", t)], writes=[("xs", t)])

    def norm_out(t, np_, b0, gbi, ys=False):
        norm_out_rest(t, np_, b0, gbi, norm_out_sq(t, np_, b0), ys)

    def mk_sel(tile_, fill0, pattern, cmp, fill, cm):
        def fn(e):
            e.memset(tile_[:], fill0)
            return e.affine_select(out=tile_[:], in_=tile_[:], pattern=pattern, compare_op=cmp, fill=fill,
                                   base=0, channel_multiplier=cm)
        return fn
    P.op(POOL, lambda e: e.memset(ident[:], 0.0), writes=[("c", "ident")])
    P.op(POOL, lambda e: e.affine_select(out=ident[:], in_=ident[:], pattern=[[-1, 128]], compare_op=ALU.not_equal,
                                         fill=1.0, base=0, channel_multiplier=1), reads=[("c", "ident")], writes=[("c", "ident")])
    P.op(POOL, lambda e: e.memset(identf[:], 0.0), writes=[("c", "identf")])
    P.op(POOL, lambda e: e.affine_select(out=identf[:], in_=identf[:], pattern=[[-1, 16]], compare_op=ALU.not_equal,
                                         fill=1.0, base=0, channel_multiplier=1), reads=[("c", "identf")], writes=[("c", "identf")])
    P.op(POOL, lambda e: e.memset(LT[:], 1.0), writes=[("c", "LT")])
    P.op(POOL, lambda e: e.affine_select(out=LT[:], in_=LT[:], pattern=[[1, 128]], compare_op=ALU.is_ge, fill=0.0,
                                         base=0, channel_multiplier=-1), reads=[("c", "LT")], writes=[("c", "LT")])
    P.op(POOL, lambda e: e.memset(ones[:], 1.0), writes=[("c", "ones")])
    P.op(POOL, lambda e: e.memset(mhalf[:], -0.5), writes=[("c", "mhalf")])
    P.op(POOL, lambda e: e.memset(colmask[:], 0.0), writes=[("c", "colmask")])
    P.op(POOL, lambda e: e.affine_select(out=colmask[:], in_=colmask[:], pattern=[[1, 16], [-1, 16]],
                                         compare_op=ALU.not_equal, fill=1.0, base=0, channel_multiplier=0),
         reads=[("c", "colmask")], writes=[("c", "colmask")])
    P.op(DVE, lambda e: e.memset(stat[:], 0.0), writes=[("c", "stat")])
    P.op(POOL, lambda e: e.dma_start(out=Wg[0:16, :], in_=gla_w_gate_up[0]), writes=[("Wg", 0)], dma="wg0")
    P.op(POOL, lambda e: e.dma_start(out=Wg[16:17, :], in_=gla_b_gate[0:1, :]), writes=[("Wg", 1)], dma="wg1")

    P.op(DVE, lambda e: e.memset(Sst[:], 0.0), writes=[("S", h) for h in range(4)])
    P.op(DVE, lambda e: e.memset(Sbf[:], 0.0), writes=[("Sbf",)])

    def load_x(p_, t_):
        r0_ = p_ * PTOK + t_ * 128
        dma(xs[:, t_, :], xp[r0_:r0_ + 128, :], "xl%d" % t_, writes=[("xs", t_)])
    for t_ in range(TPP):
        load_x(0, t_)
    P.set_global_barrier()
    for li in range(2):
        for jj, j in enumerate((0, 2)):
            dma(gT[:, li * 2 + jj, :], norm_g[li, j].rearrange("(k p) -> p k", p=128), "c2_%d" % (li * 2 + jj),
                writes=[("c", "gT", li * 2 + jj)], slow=True)

    if schedule is not None:
        pump(4)

    def late_setup():
        dma(ws_tm[:], sg_w_s[0].rearrange("g t s -> t g s"), "c0", writes=[("ws_tm",)])
        dma(bs_bc[:], sg_b_s[0].rearrange("g t -> (g t)").partition_broadcast(128).rearrange("p (g t) -> p g t", t=128),
            "c1", writes=[("bs_bc",)])
        dma(lgT[:], sg_ln_g[0].rearrange("(c p) -> p c", p=128), "c3", writes=[("c", "lgT")], slow=True)
        dma(lbT[:], sg_ln_b[0].rearrange("(c p) -> p c", p=128), "c4", writes=[("c", "lbT")], slow=True)
        dma(gn_bc[:], gla_out_norm_g[0].partition_broadcast(128), "c5", writes=[("c", "gn")])
        dma(w00[:], sg_w_s[0, :, 0, 0].partition_broadcast(16), "c6", writes=[("c", "w00")], slow=True)
        dma(b00[:], sg_b_s[0, :, 0].partition_broadcast(16), "c7", writes=[("c", "b00")], slow=True)

        P.op(POOL, lambda e: e.affine_select(out=ws_tm[:], in_=ws_tm[:], pattern=[[0, 8], [-1, 128]],
                                             compare_op=ALU.is_ge, fill=0.0, base=0, channel_multiplier=1),
             reads=[("ws_tm",)], writes=[("ws_tm",)])
        P.op(DVE, lambda e: e.tensor_copy(out=ws_bf[:], in_=ws_tm[:]), reads=[("ws_tm",)], writes=[("ws_bf",)])

        def tr_ws(e):
            ins = None
            for g in range(8):
                ins = e.transpose(pt[:, 0, g * 128:(g + 1) * 128], ws_bf[:, g, :], ident[:])
            return ins
        P.op(PE, tr_ws, reads=[("ws_bf",), ("c", "ident")], writes=[("pt", 0)])
        P.op(DVE, lambda e: e.tensor_copy(out=wsT[:], in_=pt[:, 0, :].rearrange("p (g t) -> p g t", t=128)),
             reads=[("pt", 0)], writes=[("c", "wsT")])
        for half in range(2):
            def rs(e, half=half):
                ins = None
                for gi in range(4):
                    g = half * 4 + gi
                    ins = e.matmul(bank(half, 128, 128, gi * 128), ones[:], wsT[:, g, :], start=True, stop=True)
                return ins
            P.op(PE, rs, reads=[("c", "wsT"), ("c", "ones")], writes=[("ps", half)])
            for gi in range(4):
                g = half * 4 + gi
                for cc in range(2):
                    c = 2 * g + cc
                    P.op(DVE, lambda e, half=half, gi=gi, g=g, c=c: e.scalar_tensor_tensor(
                        out=Bc[:, c, :], in0=bank(half, 128, 128, gi * 128), scalar=lbT[:, c:c + 1],
                        in1=bs_bc[:, g, :], op0=ALU.mult, op1=ALU.add),
                        reads=[("ps", half), ("c", "lbT"), ("bs_bc",)], writes=[("c", "Bc", c)])


    gbc = [0]
    setup_done = [False]
    carryB = []

    def pop_carry(tile=None):
        while carryB and (tile is None or carryB[0][0] == tile):
            norm_in_B(*carryB.pop(0))
            if tile is not None:
                break

    def load_gb(li, j):
        gi = gbc[0] % 2
        gbc[0] += 1
        dma(gb[gi][:], norm_g[li, j].partition_broadcast(128), "gb%d" % gi, writes=[("gb", gi)])
        return gi

    for p in range(NPASS):
        last = (p == NPASS - 1)
        tiles = [(t, 128) for t in range(TPP)] + ([(TPP, NS)] if last else [])
        groups = [(0, PTOK)] + ([(PTOK, TW)] if last else [])
        if p == 0:
            dma(xs[0:NS, TPP, :], xsam[:, :], "xl4", writes=[("xs", TPP)])
        todo = (tiles + [(TPP, NS)]) if p == 0 else []
        pend0 = None
        for (t, np_) in todo:
            hi = norm_in_A(t, np_)
            if pend0 is not None:
                norm_in_B(*pend0)
            pend0 = (t, np_, hi, 0)
        if pend0 is not None:
            norm_in_B(*pend0)

        for li in range(2):
            P.set_barrier()
            gbi_mix = load_gb(li, 1)
            if li == 0:
                if last:
                    dma(lg_bc[:], sg_ln_g[0].partition_broadcast(128), "c8", writes=[("lg_bc",)])
                    dma(lb_bc[:], sg_ln_b[0].partition_broadcast(128), "c9", writes=[("lb_bc",)])
                Wv = [wslab("sg_w_in", 0, 0, 8, 2048 + 512 * nb, 512) for nb in range(4)]
                vstate = {}

                def vA(t, np_):
                    vb = t % 3
                    cols = slice(t * 128, t * 128 + np_)
                    s1 = newstat(4)
                    for nb in range(4):
                        P.op(PE, mm_group(bank(nb, np_), [(hT[:, kc, cols], Wv[nb][0][:, kc, :]) for kc in range(8)]),
                             reads=[("hT", t), Wv[nb][1]], writes=[("ps", nb)])
                        P.op(ACT, lambda e, nb=nb: e.activation(
                            out=vbuf[vb][0:np_, nb * 512:(nb + 1) * 512], in_=bank(nb, np_), func=AF.Gelu,
                            accum_out=S(s1 + nb, np_)),
                            reads=[("ps", nb)], writes=[("v", vb, nb), ("st", s1 + nb)])
                    s2 = newstat()
                    P.op(ACT, lambda e: e.activation(
                        out=junk[0:np_, :], in_=vbuf[vb][0:np_, :], func=AF.Square, accum_out=S(s2, np_)),
                        reads=[("v", vb, nb) for nb in range(4)], writes=[("st", s2), ("junk", 0), ("junk", 1)])
                    vstate[t] = (s1, s2)

                def vA2(t, np_):
                    vb = t % 3
                    s1, s2 = vstate[t]
                    c = newstat(8)
                    rd = [("st", s1 + i) for i in range(4)] + [("st", s2)]
                    P.op(DVE, lambda e: e.reduce_sum(out=S(c, np_), in_=S(s1, np_, 4), axis=AX.X),
                         reads=rd, writes=[("st", c)])
                    P.op(DVE, lambda e: e.tensor_scalar(S(c + 1, np_), S(c, np_), 1.0 / 2048, None, ALU.mult),
                         reads=[("st", c)], writes=[("st", c + 1)])
                    P.op(DVE, lambda e: e.tensor_scalar(S(c + 2, np_), S(s2, np_), 1.0 / 2048, None, ALU.mult),
                         reads=[("st", s2)], writes=[("st", c + 2)])
                    P.op(DVE, lambda e: e.tensor_tensor(out=S(c + 3, np_), in0=S(c + 1, np_), in1=S(c + 1, np_), op=ALU.mult),
                         reads=[("st", c + 1)], writes=[("st", c + 3)])
                    P.op(DVE, lambda e: e.tensor_tensor(out=S(c + 4, np_), in0=S(c + 2, np_), in1=S(c + 3, np_), op=ALU.subtract),
                         reads=[("st", c + 2), ("st", c + 3)], writes=[("st", c + 4)])
                    P.op(DVE, lambda e: e.tensor_scalar(S(c + 5, np_), S(c + 4, np_), EPS, None, ALU.add),
                         reads=[("st", c + 4)], writes=[("st", c + 5)])
                    P.op(POOL, lambda e: e.tensor_tensor(out=S(c + 6, np_), in0=S(c + 5, np_), in1=mhalf[0:np_, 0:1], op=ALU.pow),
                         reads=[("st", c + 5), ("c", "mhalf")], writes=[("st", c + 6)])
                    P.op(DVE, lambda e: e.scalar_tensor_tensor(out=S(c + 7, np_), in0=S(c + 1, np_), scalar=-1.0,
                                                               in1=S(c + 6, np_), op0=ALU.mult, op1=ALU.mult),
                         reads=[("st", c + 1), ("st", c + 6)], writes=[("st", c + 7)])
                    vkeys = [("v", vb, nb) for nb in range(4)]
                    special = last and (t == TPP - 1 or np_ == NS)
                    if np_ == 128:
                        P.op(DVE, lambda e: e.tensor_scalar(
                            vhat[vb][0:np_, :], vbuf[vb][0:np_, :], S(c + 6, np_), S(c + 7, np_), ALU.mult, ALU.add),
                            reads=vkeys + [("st", c + 6), ("st", c + 7)], writes=[("vhat", vb)])
                    if special:
                        tA = tmp[0:np_].rearrange("p a b -> p (a b)")
                        P.op(DVE, lambda e: e.tensor_scalar(
                            tA, vbuf[vb][0:np_, :], S(c + 6, np_), S(c + 7, np_), ALU.mult, ALU.add),
                            reads=vkeys + [("st", c + 6), ("st", c + 7)], writes=[("tmp", 0), ("tmp", 1)])
                        P.op(DVE, lambda e: e.tensor_tensor(out=tA, in0=tA, in1=lg_bc[0:np_, :], op=ALU.mult),
                             reads=[("tmp", 0), ("tmp", 1), ("lg_bc",)], writes=[("tmp", 0), ("tmp", 1)])
                        P.op(DVE, lambda e: e.tensor_tensor(out=tA, in0=tA, in1=lb_bc[0:np_, :], op=ALU.add),
                             reads=[("tmp", 0), ("tmp", 1), ("lb_bc",)], writes=[("tmp", 0), ("tmp", 1)])
                        if np_ == 128:
                            dma(sgvp[:, :], tA, "sgvp", reads=[("tmp", 0), ("tmp", 1)])
                        else:
                            dma(sgvs[:, :], tA, "sgvs", reads=[("tmp", 0), ("tmp", 1)])
                            tA3 = tA.rearrange("p (g c) -> p g c", c=256)
                            P.op(DVE, lambda e: e.tensor_tensor(
                                out=tA3, in0=tA3, in1=w00[:, :].unsqueeze(2).to_broadcast([NS, 8, 256]), op=ALU.mult),
                                reads=[("tmp", 0), ("tmp", 1), ("c", "w00")], writes=[("tmp", 0), ("tmp", 1)])
                            P.op(DVE, lambda e: e.tensor_tensor(
                                out=mixs[:, :].rearrange("p (g c) -> p g c", c=256), in0=tA3,
                                in1=b00[:, :].unsqueeze(2).to_broadcast([NS, 8, 256]), op=ALU.add),
                                reads=[("tmp", 0), ("tmp", 1), ("c", "b00")], writes=[("mixs",)])

                def vB(t, np_):
                    vb = t % 3
                    cols = slice(t * 128, t * 128 + np_)
                    if np_ == 128:
                        for cq in range(4):
                            b = 4 + cq % 2

                            def sp_mm(e, cq=cq, b=b):
                                ins = None
                                for ci in range(4):
                                    cc = 4 * cq + ci
                                    ins = e.matmul(bank(b, 128, 128, ci * 128), vhat[vb][:, cc * 128:(cc + 1) * 128],
                                                   wsT[:, cc // 2, :], start=True, stop=True)
                                return ins
                            P.op(PE, sp_mm, reads=[("vhat", vb), ("c", "wsT")], writes=[("ps", b)])
                            for ci in range(4):
                                cc = 4 * cq + ci
                                P.op(DVE, lambda e, b=b, ci=ci, cc=cc: e.scalar_tensor_tensor(
                                    out=gated[:, cc, cols], in0=bank(b, 128, 128, ci * 128), scalar=lgT[:, cc:cc + 1],
                                    in1=Bc[:, cc, :], op0=ALU.mult, op1=ALU.add),
                                    reads=[("ps", b), ("c", "lgT"), ("c", "Bc", cc)], writes=[("gated", cc, t)])
                    else:
                        pb = ptc[0] % 2
                        ptc[0] += 1

                        def tr_mix(e):
                            ins = None
                            for cc in range(16):
                                ins = e.transpose(pt[:, pb, cc * 16:(cc + 1) * 16], mixs[:, cc * 128:(cc + 1) * 128],
                                                  ident[0:NS, 0:NS])
                            return ins
                        P.op(PE, tr_mix, reads=[("mixs",), ("c", "ident")], writes=[("pt", pb)])
                        P.op(DVE, lambda e: e.tensor_copy(
                            out=gated[:, :, PTOK:TW], in_=pt[:, pb, 0:256].rearrange("p (c j) -> p c j", j=16)),
                            reads=[("pt", pb)], writes=[("gated", cc, t) for cc in range(16)])

                pv = []
                pop_carry(0)
                vtiles = (tiles[:-2] + [tiles[-1], tiles[-2]]) if last else tiles
                for (t, np_) in vtiles:
                    vA(t, np_)
                    pop_carry(t + 1)
                    if len(pv) == 2:
                        if p == 0 and not setup_done[0]:
                            late_setup()
                            setup_done[0] = True
                        vB(*pv.pop(0))
                    vA2(t, np_)
                    pv.append((t, np_))
                while pv:
                    vB(*pv.pop(0))

                for w_ in Wv:
                    wrelease(w_)
                uc = 0
                for j in range(4):
                    Wu = wslab("sg_w_in", 0, 0, 8, 512 * j, 512)
                    for ci in range(4):
                        cc = 4 * j + ci
                        for (t0, t1) in groups:
                            n = t1 - t0
                            b = uc % 6
                            ub = uc % 2
                            uc += 1
                            tl = list(range(TPP)) if t0 == 0 else [TPP]
                            P.op(PE, mm_group(bank(b, 128, n), [(Wu[0][:, kc, ci * 128:(ci + 1) * 128], hT[:, kc, t0:t1])
                                                               for kc in range(8)]),
                                 reads=[("hT", t) for t in tl] + [Wu[1]], writes=[("ps", b)])
                            P.op(ACT, lambda e, b=b, n=n, ub=ub: e.activation(out=ut[ub][:, 0:n], in_=bank(b, 128, n), func=AF.Gelu),
                                 reads=[("ps", b)], writes=[("ut", ub)])
                            P.op(DVE, lambda e, cc=cc, t0=t0, t1=t1, ub=ub, n=n: e.tensor_tensor(
                                out=gated[:, cc, t0:t1], in0=gated[:, cc, t0:t1], in1=ut[ub][:, 0:n], op=ALU.mult),
                                reads=[("ut", ub)] + [("gated", cc, t) for t in tl],
                                writes=[("gated", cc, t) for t in tl])
                    wrelease(Wu)
                Wo = {}
                for kh in range(2):
                    for nn in range(2):
                        Wo[(kh, nn)] = wslab("sg_w_out", 0, kh * 1024, 8, nn * 512, 512)
                pend = None
                for i, (t, np_) in enumerate(tiles):
                    b0 = (i % 3) * 2
                    cols = slice(t * 128, t * 128 + np_)
                    for nn in range(2):
                        P.op(PE, mm_group(bank(b0 + nn, np_), [(gated[:, kc, cols], Wo[(kc // 8, nn)][0][:, kc % 8, :])
                                                               for kc in range(16)]),
                             reads=[("gated", kc, t) for kc in range(16)] + [Wo[(0, nn)][1], Wo[(1, nn)][1]],
                             writes=[("ps", b0 + nn)])
                    if pend is not None:
                        norm_in_B(*pend)
                    norm_out(t, np_, b0, gbi_mix)
                    hi = norm_in_A(t, np_)
                    pend = (t, np_, hi, li * 2 + 1)
                carryB.append(pend)
                for w_ in Wo.values():
                    wrelease(w_)
            else:
                bcx = [0]
                def proj_unit(which, j, Wv_, t, np_):
                    b = bcx[0] % 4
                    bcx[0] += 1
                    cols = slice(t * 128, t * 128 + np_)
                    P.op(PE, mm_group(bank(b, np_), [(hT[:, kc, cols], Wv_[0][:, kc, :]) for kc in range(8)]),
                         reads=[("hT", t), Wv_[1]], writes=[("ps", b)])
                    if which == 0:
                        P.op(ACT, lambda e: e.activation(
                            out=Vt[0:np_, t, j * 512:(j + 1) * 512], in_=bank(b, np_), func=AF.Copy),
                            reads=[("ps", b)], writes=[("V", t, j)])
                    else:
                        P.op(ACT, lambda e: e.activation(
                            out=SG[0:np_, t, j * 512:(j + 1) * 512], in_=bank(b, np_), func=AF.Silu),
                            reads=[("ps", b)], writes=[("SG", t, j)])

                def proj_tok(which, steps_):
                    if which == 0 and carryB:
                        Ws = [wslab("gla_w_in", 0, 0, 8, 1024 + 512 * j, 512) for j in range(2)]
                        for j in range(2):
                            for (t, np_) in tiles[:-1]:
                                proj_unit(which, j, Ws[j], t, np_)
                        pop_carry()
                        for j in range(2):
                            proj_unit(which, j, Ws[j], tiles[-1][0], tiles[-1][1])
                        for w_ in Ws:
                            wrelease(w_)
                        return
                    for j in range(2):
                        Wv_ = wslab("gla_w_in", 0, 0, 8, 1024 + 1024 * which + 512 * j, 512)
                        for (t, np_) in tiles:
                            pop_carry(t)
                            proj_unit(which, j, Wv_, t, np_)
                            for _ in range(2):
                                if steps_:
                                    f_, a_ = steps_.pop(0)
                                    f_(a_)
                        wrelease(Wv_)
                proj_tok(0, [])
                P.op(DVE, lambda e: e.memset(rT[:], 1.0), writes=[("rT",)])
                Wr = wslab("gla_w_in", 0, 0, 8, 3072, 16)
                for (t0, t1) in groups:
                    n = t1 - t0
                    tl = list(range(TPP)) if t0 == 0 else [TPP]
                    P.op(PE, mm_group(bank(0, 16, n), [(Wr[0][:, kc, 0:16], hT[:, kc, t0:t1]) for kc in range(8)]),
                         reads=[("hT", t) for t in tl] + [Wr[1]], writes=[("ps", 0)])
                    P.op(DVE, lambda e, t0=t0, t1=t1, n=n: e.tensor_copy(out=rT[0:16, t0:t1], in_=bank(0, 16, n)),
                         reads=[("ps", 0)], writes=[("rT",)])
                wrelease(Wr)
                for gi_, (t, np_) in enumerate(tiles):
                    gbk = 4 + gi_ % 2
                    cols = slice(t * 128, t * 128 + np_)
                    P.op(PE, lambda e, gbk=gbk, np_=np_, cols=cols: e.matmul(bank(gbk, np_), rT[0:17, cols], Wg[0:17, :],
                                                                            start=True, stop=True),
                         reads=[("rT",), ("Wg", 0), ("Wg", 1)], writes=[("ps", gbk)])
                    P.op(ACT, lambda e, gbk=gbk, np_=np_: e.activation(out=e32[0:np_, :], in_=bank(gbk, np_), func=AF.Exp, scale=-1.0),
                         reads=[("ps", gbk)], writes=[("e32",)])
                    P.op(ACT, lambda e, np_=np_, t=t: e.activation(out=Lb[0:np_, t, :], in_=e32[0:np_, :], func=AF.Ln, bias=1.0),
                         reads=[("e32",)], writes=[("l", t)])
                bc = bcx[0]
                for which in range(2):
                    Wq = wslab("gla_w_in", 0, 0, 8, 512 * which, 512)
                    for h in range(4):
                        for (t0, t1) in groups:
                            n = t1 - t0
                            b = bc % 4
                            bc += 1
                            tl = list(range(TPP)) if t0 == 0 else [TPP]
                            P.op(PE, mm_group(bank(b, 128, n), [(Wq[0][:, kc, h * 128:(h + 1) * 128], hT[:, kc, t0:t1])
                                                               for kc in range(8)]),
                                 reads=[("hT", t) for t in tl] + [Wq[1]], writes=[("ps", b)])
                            if which == 0:
                                P.op(ACT, lambda e, b=b, n=n, h=h, t0=t0, t1=t1: e.activation(
                                    out=qT[:, h, t0:t1], in_=bank(b, 128, n), func=AF.Copy, scale=float(128 ** -0.5)),
                                    reads=[("ps", b)], writes=[("qT", h, t0)])
                            else:
                                P.op(DVE, lambda e, b=b, n=n, h=h, t0=t0, t1=t1: e.tensor_copy(
                                    out=kT[:, h, t0:t1], in_=bank(b, 128, n)),
                                    reads=[("ps", b)], writes=[("kT", h, t0)])
                    if which == 1 and last:
                        b = bc % 4
                        bc += 1
                        P.op(PE, mm_group(bank(b, NS, 512), [(hT[:, kc, PTOK:TW], Wq[0][:, kc, :]) for kc in range(8)]),
                             reads=[("hT", TPP), Wq[1]], writes=[("ps", b)])
                        P.op(DVE, lambda e, b=b: e.tensor_copy(out=Ktok_s[:, :], in_=bank(b, NS, 512)),
                             reads=[("ps", b)], writes=[("Ktok_s",)])
                    wrelease(Wq)

                bcx[0] = bc
                def pre1(t):
                    eb = 0
                    cols = slice(t * 128, (t + 1) * 128)

                    def cum(e):
                        ins = None
                        for h in range(4):
                            ins = e.matmul(bank(4, 128, 128, h * 128), Lb[:, t, h * 128:(h + 1) * 128], LT[:, :],
                                           start=True, stop=True)
                        return ins
                    P.op(PE, cum, reads=[("l", t), ("c", "LT")], writes=[("ps", 4)])
                    P.op(ACT, lambda e: e.activation(out=Ep[eb][:].rearrange("p h t -> p (h t)"), in_=bank(4), func=AF.Exp,
                                                     scale=-1.0 / 16),
                         reads=[("ps", 4)], writes=[("Ep", eb)])
                    P.op(ACT, lambda e: e.activation(out=Em[eb][:].rearrange("p h t -> p (h t)"), in_=bank(4), func=AF.Exp,
                                                     scale=1.0 / 16),
                         reads=[("ps", 4)], writes=[("Em", eb)])
                    P.op(DVE, lambda e: e.tensor_tensor(out=Q32[t % 2][:], in0=qT[:, :, cols], in1=Ep[eb][:], op=ALU.mult),
                         reads=[("qT", h, 0) for h in range(4)] + [("Ep", eb), ("qTt", t)], writes=[("Q32", t % 2)])
                    P.op(DVE, lambda e: e.tensor_tensor(out=K32[t % 2][:], in0=kT[:, :, cols], in1=Em[eb][:], op=ALU.mult),
                         reads=[("kT", h, 0) for h in range(4)] + [("Em", eb), ("kTt", t)], writes=[("K32", t % 2)])
                    P.op(DVE, lambda e: e.tensor_tensor(out=qT[:, :, cols], in0=qT[:, :, cols], in1=Ep[eb][:], op=ALU.mult),
                         reads=[("qT", h, 0) for h in range(4)] + [("Ep", eb)], writes=[("qTt", t)])
                    P.op(DVE, lambda e: e.tensor_tensor(out=kT[:, :, cols], in0=kT[:, :, cols], in1=Em[eb][:], op=ALU.mult),
                         reads=[("kT", h, 0) for h in range(4)] + [("Em", eb)], writes=[("kTt", t)])
                    P.op(DVE, lambda e: e.tensor_copy(out=EL[:, t * 4:(t + 1) * 4], in_=Ep[eb][:, :, 127]),
                         reads=[("Ep", eb)], writes=[("EL", t)])
                    for h in range(4):
                        P.op(DVE, lambda e, h=h: e.tensor_scalar(Kh[t % 2][:, h, :], kT[:, h, cols], EL[:, t * 4 + h:t * 4 + h + 1],
                                                                 None, ALU.mult),
                             reads=[("kTt", t), ("EL", t)], writes=[("Kh", t % 2, h)])

                def pre2(t):
                    cols = slice(t * 128, (t + 1) * 128)
                    pb = ptc[0] % 2
                    ptc[0] += 1

                    def trk(e):
                        ins = None
                        for h in range(4):
                            ins = e.transpose(pt[:, pb, h * 128:(h + 1) * 128], Kh[t % 2][:, h, :], ident[:])
                        return ins
                    P.op(PE, trk, reads=[("Kh", t % 2, h) for h in range(4)] + [("c", "ident")], writes=[("pt", pb)])
                    P.op(ACT, lambda e: e.activation(out=Ktok[t][:].rearrange("p h k -> p (h k)"),
                                                     in_=pt[:, pb, 0:512], func=AF.Copy),
                         reads=[("pt", pb)], writes=[("Ktok", t)])

                def pre3(t):
                    cols = slice(t * 128, (t + 1) * 128)

                    def att(e):
                        ins = None
                        for h in range(4):
                            ins = e.matmul(bank(5, 128, 128, h * 128), K32[t % 2][:, h, :], Q32[t % 2][:, h, :], start=True, stop=True)
                        return ins
                    P.op(PE, att, reads=[("K32", t % 2), ("Q32", t % 2)], writes=[("ps", 5)])
                    P.op(DVE, lambda e: e.tensor_tensor(
                        out=ATb[t][:], in0=bank(5).rearrange("p (h t) -> p h t", t=128),
                        in1=LT[:, :].unsqueeze(1).to_broadcast([128, 4, 128]), op=ALU.mult),
                        reads=[("ps", 5), ("c", "LT")], writes=[("AT", t)])

                steps = [(pre1, 0), (pre1, 1), (pre2, 0), (pre3, 0), (pre1, 2), (pre2, 1), (pre3, 1), (pre1, 3),
                         (pre2, 2), (pre3, 2), (pre2, 3), (pre3, 3)]
                proj_tok(1, steps)
                while steps:
                    f_, a_ = steps.pop(0)
                    f_(a_)
                Wo2 = [wslab("gla_w_out", 0, 0, 8, nn * 512, 512) for nn in range(2)]

                def onorm(t, np_, ob):
                    c = newstat(4)
                    for h in range(4):
                        j0, jkeys = jk(256)
                        P.op(ACT, lambda e, h=h, j0=j0: e.activation(out=junk[0:np_, j0:j0 + 256], in_=bank(2 + h // 2, np_, 256, (h % 2) * 256),
                                                                     func=AF.Square, accum_out=S(c + h, np_)),
                             reads=[("ps", 2 + h // 2)], writes=[("st", c + h)] + jkeys)
                    c2 = newstat(4)
                    c3 = newstat(4)
                    P.op(DVE, lambda e: e.tensor_scalar(S(c2, np_, 4), S(c, np_, 4), 1.0 / 256, EPS, ALU.mult, ALU.add),
                         reads=[("st", c + h) for h in range(4)], writes=[("st", c2)])
                    P.op(POOL, lambda e: e.tensor_tensor(out=S(c3, np_, 4), in0=S(c2, np_, 4), in1=mhalf[0:np_, 0:4], op=ALU.pow),
                         reads=[("st", c2), ("c", "mhalf")], writes=[("st", c3)])
                    for h in range(4):
                        P.op(DVE, lambda e, h=h: e.scalar_tensor_tensor(
                            out=tmp[0:np_, 1, h * 256:(h + 1) * 256], in0=bank(2 + h // 2, np_, 256, (h % 2) * 256),
                            scalar=S(c3 + h, np_), in1=gn_bc[0:np_, :], op0=ALU.mult, op1=ALU.mult),
                            reads=[("ps", 2 + h // 2), ("st", c3), ("c", "gn")], writes=[("tmp", 1, h)])
                    P.op(DVE, lambda e: e.tensor_tensor(out=og[ob][0:np_, :], in0=tmp[0:np_, 1, :], in1=SG[0:np_, t, :],
                                                        op=ALU.mult),
                         reads=[("tmp", 1, h) for h in range(4)] + [("SG", t, 0), ("SG", t, 1)], writes=[("og", ob)])

                def trO(t, np_, ob):
                    pb = ptc[0] % 2
                    ptc[0] += 1

                    def tr(e):
                        ins = None
                        for kc in range(8):
                            ins = e.transpose(pt[:, pb, kc * 128: kc * 128 + np_], og[ob][0:np_, kc * 128:(kc + 1) * 128],
                                              ident[0:np_, 0:np_])
                        return ins
                    P.op(PE, tr, reads=[("og", ob), ("c", "ident")], writes=[("pt", pb)])
                    P.op(ACT, lambda e: e.activation(out=ogT[ob][:, :, 0:np_],
                                                     in_=pt[:, pb, :].rearrange("p (k c) -> p k c", c=128)[:, :, 0:np_],
                                                     func=AF.Copy),
                         reads=[("pt", pb)], writes=[("ogT", ob)])

                def wout_mm(t, np_, ob):
                    for nn in range(2):
                        P.op(PE, mm_group(bank(nn, np_), [(ogT[ob][:, kc, 0:np_], Wo2[nn][0][:, kc, :]) for kc in range(8)]),
                             reads=[("ogT", ob), Wo2[nn][1]], writes=[("ps", nn)])

                ptl = [t for (t, np_) in tiles if np_ == 128]
                Sflat = Sst[:].rearrange("p h v -> p (h v)")
                T_ = len(ptl)
                stt_ = {}
                m_ap = ps[:, 0:1024]
                for i in range(T_ + 4):
                    ta = ptl[i] if i < T_ else None
                    tb = ptl[i - 1] if 0 <= i - 1 < T_ else None
                    tc_ = ptl[i - 2] if 0 <= i - 2 < T_ else None
                    td = ptl[i - 3] if 0 <= i - 3 < T_ else None
                    te = ptl[i - 4] if 0 <= i - 4 < T_ else None
                    if tc_ is not None:
                        c2m = stt_[tc_]["m2"]
                        P.op(DVE, lambda e, c2m=c2m: e.scalar_tensor_tensor(out=tmp[:, 0, :], in0=m_ap, scalar=S(c2m),
                                                                            in1=gb[gbi_mix][:, :], op0=ALU.mult, op1=ALU.mult),
                             reads=[("ps", 0), ("ps", 1), ("st", c2m), ("gb", gbi_mix)], writes=[("tmp", 0)])
                    if ta is not None:
                        t = ta
                        cols = slice(t * 128, (t + 1) * 128)
                        ob = i % 2

                        def omm(e, t=t, cols=cols):
                            ins = None
                            for h in range(4):
                                o_ap = bank(2 + h // 2, 128, 256, (h % 2) * 256)
                                e.matmul(o_ap, qT[:, h, cols], Sbf[:, h, :], start=True, stop=False)
                                ins = e.matmul(o_ap, ATb[t][:, h, :], Vt[:, t, h * 256:(h + 1) * 256], start=False, stop=True)
                            return ins
                        P.op(PE, omm, reads=[("qTt", t), ("Sbf",), ("AT", t), ("V", t, 0), ("V", t, 1)],
                             writes=[("ps", 2), ("ps", 3)])

                        def smm(e, t=t):
                            ins = None
                            for h in range(4):
                                ins = e.matmul(bank(4 + h // 2, 128, 256, (h % 2) * 256), Ktok[t][:, h, :],
                                               Vt[:, t, h * 256:(h + 1) * 256], start=True, stop=True)
                            return ins
                        P.op(PE, smm, reads=[("Ktok", t), ("V", t, 0), ("V", t, 1)], writes=[("ps", 4), ("ps", 5)])
                        stt_[t] = dict(ob=ob)
                    if ta is not None:
                        c = newstat(4)
                        for h in range(4):
                            j0, jkeys = jk(256)
                            P.op(ACT, lambda e, h=h, c=c, j0=j0: e.activation(out=junk[:, j0:j0 + 256], in_=bank(2 + h // 2, 128, 256, (h % 2) * 256),
                                                                              func=AF.Square, accum_out=S(c + h)),
                                 reads=[("ps", 2 + h // 2)], writes=[("st", c + h)] + jkeys)
                        stt_[ta]["c"] = c
                        c2 = newstat(4)
                        c3 = newstat(4)
                        P.op(ACT, lambda e, c=c, c2=c2: e.activation(out=S(c2, 128, 4), in_=S(c, 128, 4), func=AF.Ln,
                                                                     bias=EPS, scale=1.0 / 256),
                             reads=[("st", c + h) for h in range(4)], writes=[("st", c2)])
                        P.op(ACT, lambda e, c2=c2, c3=c3: e.activation(out=S(c3, 128, 4), in_=S(c2, 128, 4), func=AF.Exp, scale=-0.5),
                             reads=[("st", c2)], writes=[("st", c3)])
                        stt_[ta]["c3"] = c3
                    if ta is not None:
                        t = ta
                        for h in range(4):
                            P.op(DVE, lambda e, h=h, t=t: e.scalar_tensor_tensor(
                                out=Sst[:, h, :], in0=Sst[:, h, :], scalar=EL[:, t * 4 + h:t * 4 + h + 1],
                                in1=bank(4 + h // 2, 128, 256, (h % 2) * 256), op0=ALU.mult, op1=ALU.add),
                                reads=[("S", h), ("EL", t), ("ps", 4 + h // 2)], writes=[("S", h)])
                        P.op(ACT, lambda e: e.activation(out=Sbf[:].rearrange("p h v -> p (h v)"), in_=Sflat, func=AF.Copy),
                             reads=[("S", h) for h in range(4)], writes=[("Sbf",)])
                    if tb is not None:
                        wout_mm(tb, 128, stt_[tb]["ob"])
                    if td is not None:
                        x0 = norm_in_A_sq(td, 128)
                        stt_[td]["x2"] = rstd_from_ss(x0, 128, D, 1.0 / D)
                    if te is not None:
                        if (not last) and te == ptl[-1]:
                            carryB.append((te, 128, stt_[te]["hi"], li * 2 + 1))
                        else:
                            norm_in_B(te, 128, stt_[te]["hi"], li * 2 + 1)
                    if ta is not None:
                        t = ta
                        c3 = stt_[t]["c3"]
                        ob = stt_[t]["ob"]
                        for h in range(4):
                            P.op(DVE, lambda e, h=h, c3=c3: e.scalar_tensor_tensor(
                                out=tmp[:, 1, h * 256:(h + 1) * 256], in0=bank(2 + h // 2, 128, 256, (h % 2) * 256),
                                scalar=S(c3 + h), in1=gn_bc[:, :], op0=ALU.mult, op1=ALU.mult),
                                reads=[("ps", 2 + h // 2), ("st", c3), ("c", "gn")], writes=[("tmp", 1, h)])
                        P.op(DVE, lambda e, t=t, ob=ob: e.tensor_tensor(out=og[ob][:, :], in0=tmp[:, 1, :], in1=SG[:, t, :], op=ALU.mult),
                             reads=[("tmp", 1, h) for h in range(4)] + [("SG", t, 0), ("SG", t, 1)], writes=[("og", ob)])
                    if tc_ is not None:
                        P.op(POOL, lambda e, tc_=tc_: e.tensor_tensor(out=xs[:, tc_, :], in0=xs[:, tc_, :], in1=tmp[:, 0, :], op=ALU.add),
                             reads=[("tmp", 0), ("xs", tc_)], writes=[("xs", tc_)])
                    if td is not None:
                        x2 = stt_[td]["x2"]
                        hi = hbc[0] % 2
                        hbc[0] += 1
                        P.op(ACT, lambda e, td=td, x2=x2, hi=hi: e.activation(out=hb[hi][:, :], in_=xs[:, td, :], func=AF.Copy,
                                                                              scale=S(x2)),
                             reads=[("xs", td), ("st", x2)], writes=[("hb", hi)])
                        stt_[td]["hi"] = hi
                    if tb is not None:
                        stt_[tb]["m0"] = norm_out_sq(tb, 128, 0)
                        stt_[tb]["m2"] = rstd_from_ss(stt_[tb]["m0"], 128, D, 1.0 / D)
                    if ta is not None:
                        trO(ta, 128, stt_[ta]["ob"])
                if last:
                    dma(stp.rearrange("h k v -> k h v"), Sst[:], "stp", reads=[("S", h) for h in range(4)])
                    P.set_barrier()
                    t = TPP

                    def cum_s(e):
                        ins = None
                        for h in range(4):
                            ins = e.matmul(bank(1, 128, NS, h * NS), Lb[0:NS, t, h * 128:(h + 1) * 128], ident[0:NS, 0:NS],
                                           start=True, stop=True)
                        return ins
                    P.op(PE, cum_s, reads=[("l", t), ("c", "ident")], writes=[("ps", 1)])
                    P.op(ACT, lambda e: e.activation(out=EGs[:].rearrange("p h j -> p (h j)"), in_=bank(1, 128, 4 * NS),
                                                     func=AF.Exp, scale=-1.0 / 16),
                         reads=[("ps", 1)], writes=[("EGs",)])
                    for h in range(4):
                        P.op(DVE, lambda e, h=h: e.tensor_tensor(
                            out=Qm[:, h, :, :], in0=qT[:, h, PTOK:TW].unsqueeze(1).to_broadcast([128, NS, NS]),
                            in1=colmask[:], op=ALU.mult),
                            reads=[("qT", h, PTOK), ("c", "colmask")], writes=[("Qm", h)])

                    def s_pre(j):
                        sb = j % 2
                        pbk = 0 if sb == 0 else 4
                        P.op(DVE, lambda e: e.tensor_scalar(Kmask[sb][:, :], Ktok_s[:, :], identf[:, j:j + 1], None, ALU.mult),
                             reads=[("Ktok_s",), ("c", "identf")], writes=[("Kmask", sb)])

                        def outer(e):
                            ins = None
                            for h in range(4):
                                ins = e.matmul(bank(pbk + h // 2, 128, 256, (h % 2) * 256), Kmask[sb][:, h * 128:(h + 1) * 128],
                                               Vt[0:NS, t, h * 256:(h + 1) * 256], start=True, stop=True)
                            return ins
                        P.op(PE, outer, reads=[("Kmask", sb), ("V", t, 0), ("V", t, 1)], writes=[("ps", pbk), ("ps", pbk + 1)])

                    def s_outer(j):
                        sb = j % 2
                        pbk = 0 if sb == 0 else 4
                        for h in range(4):
                            P.op(DVE, lambda e, h=h: e.scalar_tensor_tensor(
                                out=Snew[sb][:, h, :], in0=S_in[j % 3][:, h, :], scalar=EGs[:, h, j:j + 1],
                                in1=bank(pbk + h // 2, 128, 256, (h % 2) * 256), op0=ALU.mult, op1=ALU.add),
                                reads=[("S_in", j % 3), ("EGs",), ("ps", pbk + h // 2)], writes=[("Snew", sb, h)])
                        if j + 3 < NS:
                            dma(S_in[j % 3][:], st_in[j + 3].rearrange("h k v -> k h v"), "sin%d" % (j % 3),
                                writes=[("S_in", j % 3)], eng=ACT)
                        P.op(ACT, lambda e: e.activation(out=Snbf[sb][:].rearrange("p h v -> p (h v)"),
                                                         in_=Snew[sb][:].rearrange("p h v -> p (h v)"), func=AF.Copy),
                             reads=[("Snew", sb, h) for h in range(4)], writes=[("Snbf", sb)])
                        dma(sts[j].rearrange("h k v -> k h v"), Snew[sb][:], "sout%d" % sb,
                            reads=[("Snew", sb, h) for h in range(4)])

                    def s_o(j):
                        sb = j % 2

                        def osm(e):
                            ins = None
                            for h in range(4):
                                ins = e.matmul(bank(2 + h // 2, NS, 256, (h % 2) * 256), Qm[:, h, j, :], Snbf[sb][:, h, :],
                                               start=(j == 0 and h % 2 == 0), stop=(j == NS - 1), skip_group_check=True)
                            return ins
                        P.op(PE, osm, reads=[("Qm", h) for h in range(4)] + [("Snbf", sb)], writes=[("ps", 2), ("ps", 3)])

                    for j0 in range(3):
                        dma(S_in[j0][:], st_in[j0].rearrange("h k v -> k h v"), "sin%d" % j0, writes=[("S_in", j0)], eng=ACT)
                    s_pre(0)
                    for j in range(NS):
                        if j + 1 < NS:
                            s_pre(j + 1)
                        s_outer(j)
                        if j >= 1:
                            s_o(j - 1)
                    s_o(NS - 1)
                    onorm(t, NS, 0)
                    trO(t, NS, 0)
                    wout_mm(t, NS, 0)
                    norm_out(t, NS, 0, gbi_mix)
                    hi = norm_in_A(t, NS)
                    carryB.append((t, NS, hi, li * 2 + 1))

            if li == 1:
                for w_ in Wo2:
                    wrelease(w_)
            P.set_barrier()
            gbi_ffn = load_gb(li, 3)
            if li == 1 and p + 1 < NPASS:
                for t_ in range(TPP):
                    r0_ = (p + 1) * PTOK + t_ * 128
                    dma(xnext[:, t_, :], xp[r0_:r0_ + 128, :], "xn%d" % t_, writes=[("xnext", t_)])
            w13 = ffn_w13[li]
            fc = [0]

            def ab_unit(j, Wa, Wb, ci, t0, t1, tl):
                n = t1 - t0
                ba = (fc[0] % 3) * 2
                sb = fc[0] % 2
                fc[0] += 1
                rk = [("hT", t) for t in tl]
                P.op(PE, mm_group(bank(ba, 128, n), [(Wa[0][:, kc, ci * 128:(ci + 1) * 128], hT[:, kc, t0:t1])
                                                    for kc in range(8)]),
                     reads=rk + [Wa[1]], writes=[("ps", ba)])
                P.op(PE, mm_group(bank(ba + 1, 128, n), [(Wb[0][:, kc, ci * 128:(ci + 1) * 128], hT[:, kc, t0:t1])
                                                        for kc in range(8)]),
                     reads=rk + [Wb[1]], writes=[("ps", ba + 1)])
                P.op(ACT, lambda e: e.activation(out=sab[sb][:, 0:n], in_=bank(ba, 128, n), func=AF.Silu),
                     reads=[("ps", ba)], writes=[("sab", sb)])
                P.op(DVE, lambda e: e.tensor_tensor(
                    out=gTf[:, j, t0:t1], in0=sab[sb][:, 0:n], in1=bank(ba + 1, 128, n), op=ALU.mult),
                    reads=[("sab", sb), ("ps", ba + 1)], writes=[("gTf", j, 0 if t0 < PTOK else PTOK)])

            NSPLIT = 4
            for jj in range(6):
                w = 512 if jj < 5 else 256
                Wa = wslab("ffn_w13", li, 0, 8, jj * 512, w)
                Wb = wslab("ffn_w13", li, 0, 8, DFF + jj * 512, w)
                deferred = []
                for ci in range(w // 128):
                    j = 4 * jj + ci
                    for (t0, t1) in groups:
                        tl = list(range(TPP)) if t0 == 0 else [TPP]
                        if jj == 0 and ci < NSPLIT and t0 == 0:
                            ab_unit(j, Wa, Wb, ci, 0, 384, [0, 1, 2])
                            deferred.append((j, Wa, Wb, ci, 384, 512, [3]))
                        elif jj == 0 and ci < NSPLIT:
                            deferred.append((j, Wa, Wb, ci, t0, t1, tl))
                        else:
                            ab_unit(j, Wa, Wb, ci, t0, t1, tl)
                    if jj == 0 and ci == NSPLIT - 1:
                        pop_carry()
                        for d_ in deferred:
                            ab_unit(*d_)
                        deferred = []
                wrelease(Wa)
                wrelease(Wb)
            W2 = {}
            for nn in range(2):
                for kg in range(3):
                    nk = 8 if kg < 2 else 6
                    W2[(kg, nn)] = wslab("ffn_w2", li, kg * 1024, nk, nn * 512, 512)
            final = (li == 1)
            pendq = []
            chained = []
            split = not last
            prefx = final and (p + 1 < NPASS)
            if split:
                pq = []
                for (t, np_) in tiles:
                    cols = slice(t * 128, t * 128 + np_)
                    P.op(PE, mm_group(bank(t, np_), [(gTf[:, kc, cols], W2[(kc // 8, 0)][0][:, kc % 8, :])
                                                     for kc in range(NFF)]),
                         reads=[("gTf", kc, 0) for kc in range(NFF)] + [W2[(kg, 0)][1] for kg in range(3)],
                         writes=[("ps", t)])
                    if prefx:
                        if pq:
                            norm_in_B(*pq.pop(0))
                        hi = norm_in_A(t, np_, xi=t, nx=True)
                        pq.append((t, np_, hi, 0))
                while prefx and pq:
                    norm_in_B(*pq.pop(0))
                for kg in range(3):
                    wrelease(W2[(kg, 0)])
            for i, (t, np_) in enumerate(tiles):
                b0 = (i % 3) * 2
                cols = slice(t * 128, t * 128 + np_)
                t0k = 0 if np_ == 128 else PTOK
                lagn = (len(tiles) + 1) if final else (2 if split else 1)
                if split:
                    bB = 4 + i % 2
                    P.op(PE, mm_group(bank(bB, np_), [(gTf[:, kc, cols], W2[(kc // 8, 1)][0][:, kc % 8, :])
                                                      for kc in range(NFF)]),
                         reads=[("gTf", kc, t0k) for kc in range(NFF)] + [W2[(kg, 1)][1] for kg in range(3)],
                         writes=[("ps", bB)])
                    while len(pendq) >= lagn:
                        norm_in_B(*pendq.pop(0))
                    norm_out2(t, np_, t, bB, gbi_ffn, ys=(final and np_ == 128))
                else:
                    for nn in range(2):
                        P.op(PE, mm_group(bank(b0 + nn, np_), [(gTf[:, kc, cols], W2[(kc // 8, nn)][0][:, kc % 8, :])
                                                               for kc in range(NFF)]),
                             reads=[("gTf", kc, t0k) for kc in range(NFF)] + [W2[(kg, nn)][1] for kg in range(3)],
                             writes=[("ps", b0 + nn)])
                    while len(pendq) >= lagn:
                        norm_in_B(*pendq.pop(0))
                    norm_out(t, np_, b0, gbi_ffn, ys=(final and np_ == 128))
                if final:
                    if np_ == 128:
                        r0 = p * PTOK + t * 128
                        dma(yp[r0:r0 + 128, :], ystage[:, t, :], "ys%d" % t, reads=[("ystage", t)])
                        if p + 1 < NPASS:
                            P.op(POOL, lambda e, t=t: e.tensor_copy(out=xs[:, t, :], in_=xnext[:, t, :]),
                                 reads=[("xnext", t)], writes=[("xs", t)])
                    else:
                        dma(ysam[:, :], xs[0:NS, t, :], "ys4", reads=[("xs", t)])
                else:
                    hi = norm_in_A(t, np_)
                    pendq.append((t, np_, hi, 2))
            while pendq:
                if len(pendq) == 1:
                    carryB.append(pendq.pop(0))
                else:
                    norm_in_B(*pendq.pop(0))
            for (t, np_) in chained:
                hi = norm_in_A(t, np_, xi=t)
                carryB.append((t, np_, hi, 0))
            for (kg_, nn_), w_ in W2.items():
                if not (split and nn_ == 0):
                    wrelease(w_)

    if schedule is None:
        return specs
    ops = P.ops
    needed = [False] * len(ops)
    for i, o in enumerate(ops):
        for d in o["deps"]:
            od = ops[d]
            if od["dma"] is None and o["dma"] is None and od["eng"] == PE and o["eng"] == PE:
                continue
            needed[d] = True
    streams = {}
    ev = [None] * len(ops)
    for i, o in enumerate(ops):
        s = P._stream(o["eng"], o["dma"])
        if o["dma"] is not None:
            streams[s] = streams.get(s, 0) + 16
            ev[i] = (s, streams[s])
        elif needed[i]:
            streams[s] = streams.get(s, 0) + 1
            ev[i] = (s, streams[s])
        else:
            streams.setdefault(s, 0)
    for e_ in (PE, ACT, DVE, POOL):
        streams.setdefault(e_, 0)
    sem_names = list(streams.keys())
    import contextlib
    with contextlib.ExitStack() as es:
        sems = {}
        for k, s in enumerate(sem_names):
            sems[s] = es.enter_context(nc.semaphore("s%d" % k))
        block = es.enter_context(nc.Block())

        def emit(engname):
            def body(e):
                waited = {}
                for i, o in enumerate(ops):
                    if o["eng"] != engname:
                        continue
                    want = {}
                    for d in o["deps"]:
                        od = ops[d]
                        if od["dma"] is None and o["dma"] is None and od["eng"] == PE and engname == PE:
                            continue
                        s, c = ev[d]
                        if c > want.get(s, 0):
                            want[s] = c
                    for s, c in want.items():
                        if waited.get(s, 0) >= c:
                            continue
                        e.wait_ge(sems[s], c)
                        waited[s] = c
                    ins = o["fn"](e)
                    if ev[i] is not None:
                        s, c = ev[i]
                        ins.then_inc(sems[s], 16 if o["dma"] is not None else 1)
                if engname == SP:
                    for s, c in streams.items():
                        if isinstance(s, tuple) and c > 0:
                            e.wait_ge(sems[s], c)
            return body

        block.sync(emit(SP))
        block.gpsimd(emit(POOL))
        block.vector(emit(DVE))
        block.scalar(emit(ACT))
        block.tensor(emit(PE))
    return nc


_CACHE = {}


def kernel(x_prompt, x_sample, state_gla, norm_g, ffn_w13, ffn_w2, sg_w_in, sg_ln_g, sg_ln_b, sg_w_s,
           sg_b_s, sg_w_out, gla_w_in, gla_w_gate_up, gla_b_gate, gla_out_norm_g, gla_w_out):
    f = lambda a: np.ascontiguousarray(np.asarray(a, dtype=np.float32))
    if "nc" not in _CACHE:
        _CACHE["nc"] = build_program()
    nc = _CACHE["nc"]
    shared = dict(norm_g=f(norm_g), ffn_w13=f(ffn_w13), ffn_w2=f(ffn_w2), sg_w_in=f(sg_w_in), sg_ln_g=f(sg_ln_g),
                  sg_ln_b=f(sg_ln_b), sg_w_s=f(sg_w_s), sg_b_s=f(sg_b_s), sg_w_out=f(sg_w_out), gla_w_in=f(gla_w_in),
                  gla_w_gate_up=f(gla_w_gate_up), gla_b_gate=f(gla_b_gate), gla_out_norm_g=f(gla_out_norm_g),
                  gla_w_out=f(gla_w_out))
    xpr = f(x_prompt)
    xsa = f(x_sample).reshape(128, D)
    stg = f(state_gla)[0]
    in_maps = []
    for c in range(NCORE):
        m = dict(shared)
        m["xp"] = xpr[c]
        m["xsam"] = np.ascontiguousarray(xsa[c * NS:(c + 1) * NS])
        m["st_in"] = np.ascontiguousarray(stg[c * NS:(c + 1) * NS])
        in_maps.append(m)
    res = run_bass_kernel_spmd(nc, in_maps, core_ids=list(range(NCORE)))
    R = res.results
    y_prompt = np.stack([R[c]["yp"] for c in range(NCORE)], 0).astype(np.float32)
    y_sample = np.concatenate([R[c]["ysam"] for c in range(NCORE)], 0).reshape(128, 1, D).astype(np.float32)
    sgv_p = np.stack([R[c]["sgvp"] for c in range(NCORE)], 0)[None].astype(np.float32)
    sgv_s = np.concatenate([R[c]["sgvs"] for c in range(NCORE)], 0).reshape(1, 128, 1, 2048).astype(np.float32)
    st_p = np.stack([R[c]["stp"] for c in range(NCORE)], 0)[None].astype(np.float32)
    st_s = np.concatenate([R[c]["sts"] for c in range(NCORE)], 0)[None].astype(np.float32)
    return (y_prompt, y_sample, sgv_p, sgv_s, st_p, st_s)
```

```python
import numpy as np
import concourse.bass as bass
import concourse.mybir as mybir
from concourse.bass_utils import run_bass_kernel_spmd

F32 = mybir.dt.float32
BF16 = mybir.dt.bfloat16
AF = mybir.ActivationFunctionType
ALU = mybir.AluOpType
AX = mybir.AxisListType

D = 1024
SEQ = 2048
NCORE = 8
NS = 16
NPASS = 4
TPP = 4
PTOK = 512
TW = PTOK + NS
DFF = 2816
NFF = 22
EPS = 1e-6
NSLOT = 8
PE, ACT, DVE, POOL, SP = "pe", "act", "dve", "pool", "sp"


REGION_KEYS = {"gated", "v", "vhat", "ut", "lg_bc", "lb_bc", "mixs", "gTf", "sab", "V", "SG", "qT", "kT", "qTt", "kTt",
               "rT", "e32", "l", "og", "ogT", "Ktok_s", "EL", "Ep", "Em", "Ktok", "AT", "Kh", "S_in", "Snew", "Snbf", "Qm",
               "Kmask", "EGs", "Q32", "K32", "ws_tm", "ws_bf", "bs_bc", "hbx", "ystage", "xnext"}


class Plan:
    def __init__(self):
        self.ops = []
        self.last_w = {}
        self.readers = {}
        self.barrier = {}
        self.gbarrier = {}
        self.last_on = {}
        self.last_region_on = {}

    def _stream(self, eng, dma):
        return ("dma", dma) if dma is not None else eng

    def op(self, eng, fn, reads=(), writes=(), dma=None, nobarrier=False):
        idx = len(self.ops)
        deps = set()
        for k in reads:
            if k in self.last_w:
                deps.add(self.last_w[k])
        for k in writes:
            if k in self.last_w:
                deps.add(self.last_w[k])
            deps.update(self.readers.get(k, ()))
        region = any(k[0] in REGION_KEYS for k in reads) or any(k[0] in REGION_KEYS for k in writes)
        if not nobarrier and eng != PE:
            deps.update(self.gbarrier.values())
            if region:
                deps.update(self.barrier.values())
        self.ops.append(dict(eng=eng, fn=fn, deps=deps, dma=dma))
        for k in reads:
            self.readers.setdefault(k, []).append(idx)
        for k in writes:
            self.last_w[k] = idx
            self.readers[k] = []
        if not nobarrier:
            st = self._stream(eng, dma)
            self.last_on[st] = idx
            if region:
                self.last_region_on[st] = idx
        return idx

    def set_barrier(self):
        self.barrier = dict(self.last_region_on)

    def set_global_barrier(self):
        self.gbarrier = dict(self.last_on)


def build_program():
    specs = _build(None)
    return _build(specs)


def _build(schedule):
    nc = bass.Bass("TRN2", target_bir_lowering=False)
    P = Plan()
    specs = []

    def din(name, shape):
        return nc.dram_tensor(name, list(shape), F32, kind="ExternalInput").ap()

    def dout(name, shape):
        return nc.dram_tensor(name, list(shape), F32, kind="ExternalOutput").ap()

    xp = din("xp", [SEQ, D])
    xsam = din("xsam", [NS, D])
    st_in = din("st_in", [NS, 4, 128, 256])
    norm_g = din("norm_g", [2, 4, D])
    ffn_w13 = din("ffn_w13", [2, D, 2 * DFF])
    ffn_w2 = din("ffn_w2", [2, DFF, D])
    sg_w_in = din("sg_w_in", [1, D, 4096])
    sg_ln_g = din("sg_ln_g", [1, 2048])
    sg_ln_b = din("sg_ln_b", [1, 2048])
    sg_w_s = din("sg_w_s", [1, 8, 128, 128])
    sg_b_s = din("sg_b_s", [1, 8, 128])
    sg_w_out = din("sg_w_out", [1, 2048, D])
    gla_w_in = din("gla_w_in", [1, D, 3088])
    gla_w_gate_up = din("gla_w_gate_up", [1, 16, 512])
    gla_b_gate = din("gla_b_gate", [1, 512])
    gla_out_norm_g = din("gla_out_norm_g", [1, 256])
    gla_w_out = din("gla_w_out", [1, D, D])
    yp = dout("yp", [SEQ, D])
    ysam = dout("ysam", [NS, D])
    sgvp = dout("sgvp", [128, 2048])
    sgvs = dout("sgvs", [NS, 2048])
    stp = dout("stp", [4, 128, 256])
    sts = dout("sts", [NS, 4, 128, 256])

    base = 229376 - nc.sbuf_bytes_remaining
    base = (base + 63) // 64 * 64
    cur = [base]
    cnt = [0]

    def alloc(shape, dt, at=None):
        nbytes = int(np.prod(shape[1:])) * (4 if dt == F32 else 2)
        nbytes = (nbytes + 63) // 64 * 64
        if at is None:
            off = cur[0]
            cur[0] += nbytes
        else:
            off = at[0]
            at[0] += nbytes
        cnt[0] += 1
        assert off + nbytes <= 229344, ("sbuf overflow", off + nbytes)
        return nc.alloc_sbuf_tensor_at("t%d" % cnt[0], list(shape), dt, offset=off)

    xs = alloc([128, 5, D], F32)
    hT = alloc([128, 8, TW], BF16)
    slots = [alloc([128, 8, 512], BF16) for _ in range(NSLOT)]
    ident = alloc([128, 128], BF16)
    identf = alloc([16, 16], F32)
    LT = alloc([128, 128], BF16)
    ones = alloc([128, 128], BF16)
    colmask = alloc([128, 16, 16], BF16)
    wsT = alloc([128, 8, 128], BF16)
    Bc = alloc([128, 16, 128], BF16)
    gb = [alloc([128, D], F32) for _ in range(2)]
    gT = alloc([128, 4, 8], F32)
    lgT = alloc([128, 16], F32)
    lbT = alloc([128, 16], F32)
    gn_bc = alloc([128, 256], F32)
    w00 = alloc([16, 8], F32)
    b00 = alloc([16, 8], F32)
    mhalf = alloc([128, 8], F32)
    NST = 1280
    stat = alloc([128, NST], F32)
    hb = [alloc([128, D], BF16) for _ in range(2)]
    tmp = alloc([128, 2, D], F32)
    junk = alloc([128, 2048], BF16)
    jrot = [0]

    def jk(n):
        if n > 1024:
            return 0, [("junk", 0), ("junk", 1)]
        i_ = jrot[0] % 2
        jrot[0] += 1
        return i_ * 1024, [("junk", i_)]
    Sst = alloc([128, 4, 256], F32)
    Sbf = alloc([128, 4, 256], BF16)
    Wg = alloc([32, 512], BF16)
    ph0 = cur[0]

    HBX0 = (229344 - 4 * 2048) // 64 * 64

    def phase_alloc():
        return [ph0]

    a = phase_alloc()
    ws_tm = alloc([128, 8, 128], F32, a)
    ws_bf = alloc([128, 8, 128], BF16, a)
    bs_bc = alloc([128, 8, 128], F32, a)
    a = phase_alloc()
    gated = alloc([128, 16, TW], BF16, a)
    vbuf = [alloc([128, 2048], BF16, a) for _ in range(3)]
    vhat = [alloc([128, 2048], BF16, a) for _ in range(3)]
    ut = [alloc([128, 512], BF16, a) for _ in range(2)]
    lg_bc = alloc([128, 2048], F32, a)
    lb_bc = alloc([128, 2048], F32, a)
    mixs = alloc([16, 2048], BF16, a)
    assert a[0] <= HBX0, (a[0], HBX0)
    a = phase_alloc()
    gTf = alloc([128, NFF, TW], BF16, a)
    sab = [alloc([128, 512], BF16, a) for _ in range(2)]
    ystage = alloc([128, 4, D], F32, a)
    xnext = alloc([128, 4, D], F32, a)
    assert a[0] <= HBX0
    a_hbx = [HBX0]
    hbx = [alloc([128, D], BF16, a_hbx) for _ in range(4)]
    a = phase_alloc()
    Vt = alloc([128, 5, D], BF16, a)
    SG = alloc([128, 5, D], BF16, a)
    qT = alloc([128, 4, TW], BF16, a)
    kT = alloc([128, 4, TW], BF16, a)
    rT = alloc([32, TW], BF16, a)
    off_e32 = a[0]
    e32 = alloc([128, 512], F32, a)
    Lb = alloc([128, 5, 512], BF16, a)
    og = [alloc([128, D], BF16, a) for _ in range(2)]
    ogT = [alloc([128, 8, 128], BF16, a) for _ in range(2)]
    Ktok_s = alloc([16, 512], BF16, a)
    EL = alloc([128, 16], F32, a)
    a_mark = a[0]
    Ep = [alloc([128, 4, 128], F32, a)] * 2
    Em = [alloc([128, 4, 128], F32, a)] * 2
    Q32 = [alloc([128, 4, 128], F32, a) for _ in range(2)]
    K32 = [alloc([128, 4, 128], F32, a) for _ in range(2)]
    Ktok = [alloc([128, 4, 128], BF16, a) for _ in range(4)]
    ATb = [alloc([128, 4, 128], BF16, a) for _ in range(4)]
    Kh = [alloc([128, 4, 128], BF16, a) for _ in range(2)]
    a = [a_mark]
    S_in = [alloc([128, 4, 256], F32, a) for _ in range(2)] + [alloc([128, 4, 256], F32, [off_e32])]
    Snew = [alloc([128, 4, 256], F32, a) for _ in range(2)]
    Snbf = [alloc([128, 4, 256], BF16, a) for _ in range(2)]
    Qm = alloc([128, 4, 16, 16], BF16, a)
    Kmask = [alloc([16, 512], BF16, a) for _ in range(2)]
    EGs = alloc([128, 4, 16], F32, a)

    ps = nc.alloc_psum_tensor("ps", [128, 6 * 512], F32)
    pt = nc.alloc_psum_tensor("pt", [128, 2, 1024], BF16)

    def bank(b, np_=128, n=512, off=0):
        return ps[0:np_, b * 512 + off: b * 512 + off + n]

    stc = [0]

    def newstat(n=1):
        c = stc[0]
        stc[0] += n
        assert stc[0] <= NST
        return c

    def S(c, np_=128, n=1):
        return stat[0:np_, c:c + n]

    def mm_group(out_ap, pairs, skip=False):
        def fn(e):
            n = len(pairs)
            ins = None
            for i, (l, r) in enumerate(pairs):
                if skip:
                    ins = e.matmul(out_ap, l, r, start=(i == 0), stop=(i == n - 1), skip_group_check=True)
                else:
                    ins = e.matmul(out_ap, l, r, start=(i == 0), stop=(i == n - 1))
            return ins
        return fn

    slot_ctr = [0]
    issued = [0]
    released = set()
    wmap = dict(sg_w_in=sg_w_in, sg_w_out=sg_w_out, ffn_w13=ffn_w13, ffn_w2=ffn_w2, gla_w_in=gla_w_in,
                gla_w_out=gla_w_out)

    first_reads = [[("xs", t_) for t_ in range(TPP)]]

    def issue_slab(i, spec):
        wname, li_, r0, nkc, c0, w = spec
        s_ = i % NSLOT
        src = wmap[wname][li_][r0:r0 + nkc * 128, c0:c0 + w].rearrange("(kc p) n -> p kc n", p=128)
        dst = slots[s_][:, 0:nkc, 0:w]
        rd = first_reads[0] if (schedule is not None and i == 0) else []
        P.op(POOL, lambda e: e.dma_start(out=dst, in_=src), reads=rd, writes=[("slot", s_)], dma="slot%d" % s_,
             nobarrier=True)

    def pump(maxn=None):
        while issued[0] < len(schedule) and (issued[0] < NSLOT or (issued[0] - NSLOT) in released) and \
                (maxn is None or issued[0] < maxn):
            issue_slab(issued[0], schedule[issued[0]])
            issued[0] += 1

    def wslab(wname, li_, r0, nkc, c0, w):
        i = slot_ctr[0]
        slot_ctr[0] += 1
        s_ = i % NSLOT
        spec = (wname, li_, r0, nkc, c0, w)
        if schedule is None:
            specs.append(spec)
            issue_slab(i, spec)
        else:
            assert schedule[i] == spec, (i, schedule[i], spec)
            pump()
            assert issued[0] > i, ("slab not issuable (too many live slabs)", i)
        return slots[s_], ("slot", s_), i

    def wrelease(h):
        if schedule is None:
            return
        released.add(h[2])
        pump()

    def dma(out, in_, key, reads=(), writes=(), slow=False, eng=SP):
        if slow:
            fn = lambda e: e.dma_start(out=out, in_=in_, allow_slow_non_contiguous=True)
        else:
            fn = lambda e: e.dma_start(out=out, in_=in_)
        return P.op(eng, fn, reads=reads, writes=writes, dma=key)

    def rstd_from_ss(c_ss, np_, n, inv, cols=1):
        c1 = newstat(cols)
        c2 = newstat(cols)
        P.op(DVE, lambda e: e.tensor_scalar(S(c1, np_, cols), S(c_ss, np_, cols), inv, EPS, ALU.mult, ALU.add),
             reads=[("st", c_ss)], writes=[("st", c1)])
        P.op(POOL, lambda e: e.tensor_tensor(out=S(c2, np_, cols), in0=S(c1, np_, cols), in1=mhalf[0:np_, 0:cols], op=ALU.pow),
             reads=[("st", c1), ("c", "mhalf")], writes=[("st", c2)])
        return c2

    hbc = [0]
    ptc = [0]

    def xsrc(t, np_, nx):
        if nx:
            return xnext[0:np_, t, :], ("xnext", t)
        return xs[0:np_, t, :], ("xs", t)

    def norm_in_A_sq(t, np_, nx=False):
        c0 = newstat()
        j0, jkeys = jk(D)
        src, skey = xsrc(t, np_, nx)
        P.op(ACT, lambda e: e.activation(out=junk[0:np_, j0:j0 + D], in_=src, func=AF.Square,
                                         accum_out=S(c0, np_)),
             reads=[skey], writes=[("st", c0)] + jkeys)
        return c0

    def hb_of(hi):
        if isinstance(hi, tuple):
            return hbx[hi[1]], ("hbx", hi[1])
        return hb[hi], ("hb", hi)

    def norm_in_A_rest(t, np_, c0, xi=None, nx=False):
        c2 = rstd_from_ss(c0, np_, D, 1.0 / D)
        src, skey = xsrc(t, np_, nx)
        if xi is None:
            hi = hbc[0] % 2
            hbc[0] += 1
        else:
            hi = ("x", xi)
        buf, key = hb_of(hi)
        P.op(ACT, lambda e: e.activation(out=buf[0:np_, :], in_=src, func=AF.Copy,
                                         scale=S(c2, np_)),
             reads=[skey, ("st", c2)], writes=[key])
        return hi

    def norm_in_A(t, np_, xi=None, nx=False):
        return norm_in_A_rest(t, np_, norm_in_A_sq(t, np_, nx), xi, nx)

    def norm_in_B(t, np_, hi, gidx):
        pb = ptc[0] % 2
        ptc[0] += 1
        col0 = t * 128
        hbuf, hkey = hb_of(hi)

        def tr(e):
            ins = None
            for kc in range(8):
                ins = e.transpose(pt[:, pb, kc * 128: kc * 128 + np_], hbuf[0:np_, kc * 128:(kc + 1) * 128],
                                  ident[0:np_, 0:np_])
            return ins
        P.op(PE, tr, reads=[hkey, ("c", "ident")], writes=[("pt", pb)])
        src = pt[:, pb, :].rearrange("p (k c) -> p k c", c=128)[:, :, 0:np_]
        P.op(DVE, lambda e: e.tensor_tensor(out=hT[:, :, col0:col0 + np_], in0=src,
                                            in1=gT[:, gidx, :].unsqueeze(2).to_broadcast([128, 8, np_]),
                                            op=ALU.mult),
             reads=[("pt", pb), ("c", "gT", gidx)], writes=[("hT", t)])

    def norm_out_sq(t, np_, b0):
        m = ps[0:np_, b0 * 512:(b0 + 2) * 512]
        c0 = newstat()
        j0, jkeys = jk(D)
        P.op(ACT, lambda e: e.activation(out=junk[0:np_, j0:j0 + D], in_=m, func=AF.Square, accum_out=S(c0, np_)),
             reads=[("ps", b0), ("ps", b0 + 1)], writes=[("st", c0)] + jkeys)
        return c0

    def norm_out_rest(t, np_, b0, gbi, c0, ys=False):
        m = ps[0:np_, b0 * 512:(b0 + 2) * 512]
        c2 = rstd_from_ss(c0, np_, D, 1.0 / D)
        P.op(DVE, lambda e: e.scalar_tensor_tensor(out=tmp[0:np_, 0, :], in0=m, scalar=S(c2, np_),
                                                   in1=gb[gbi][0:np_, :], op0=ALU.mult, op1=ALU.mult),
             reads=[("ps", b0), ("ps", b0 + 1), ("st", c2), ("gb", gbi)], writes=[("tmp", 0)])
        if ys:
            P.op(DVE, lambda e: e.tensor_tensor(out=ystage[0:np_, t, :], in0=xs[0:np_, t, :], in1=tmp[0:np_, 0, :],
                                                op=ALU.add),
                 reads=[("tmp", 0), ("xs", t)], writes=[("ystage", t)])
        else:
            P.op(DVE, lambda e: e.tensor_tensor(out=xs[0:np_, t, :], in0=xs[0:np_, t, :], in1=tmp[0:np_, 0, :],
                                                op=ALU.add),
                 reads=[("tmp", 0), ("xs", t)], writes=[("xs", t)])

    def norm_out2(t, np_, bA, bB, gbi, ys=False):
        ca = newstat(2)
        for hh, bk in enumerate((bA, bB)):
            j0, jkeys = jk(512)
            P.op(ACT, lambda e, hh=hh, bk=bk, j0=j0: e.activation(out=junk[0:np_, j0:j0 + 512], in_=bank(bk, np_), func=AF.Square,
                                                                  accum_out=S(ca + hh, np_)),
                 reads=[("ps", bk)], writes=[("st", ca + hh)] + jkeys)
        c0 = newstat()
        P.op(DVE, lambda e: e.tensor_tensor(out=S(c0, np_), in0=S(ca, np_), in1=S(ca + 1, np_), op=ALU.add),
             reads=[("st", ca), ("st", ca + 1)], writes=[("st", c0)])
        c2 = rstd_from_ss(c0, np_, D, 1.0 / D)
        for hh, bk in enumerate((bA, bB)):
            P.op(DVE, lambda e, hh=hh, bk=bk: e.scalar_tensor_tensor(
                out=tmp[0:np_, 0, hh * 512:(hh + 1) * 512], in0=bank(bk, np_), scalar=S(c2, np_),
                in1=gb[gbi][0:np_, hh * 512:(hh + 1) * 512], op0=ALU.mult, op1=ALU.mult),
                reads=[("ps", bk), ("st", c2), ("gb", gbi)], writes=[("tmp", 0)])
        if ys:
            P.op(DVE, lambda e: e.tensor_tensor(out=ystage[0:np_, t, :], in0=xs[0:np_, t, :], in1=tmp[0:np_, 0, :],
                                                op=ALU.add),
                 reads=[("tmp", 0), ("xs", t)], writes=[("ystage", t)])
        else:
            P.op(DVE, lambda e: e.tensor_tensor(out=xs[0:np_, t, :], in0=xs[0:np_, t, :], in1=tmp[0:np_, 0, :],
                                                op=ALU.add),
                 reads=[("tmp", 0), ("xs", t)], writes=[("xs", t)])

    def norm_out(t, np_, b0, gbi, ys=False):
        norm_out_rest(t, np_, b0, gbi, norm_out_sq(t, np_, b0), ys)

    def mk_sel(tile_, fill0, pattern, cmp, fill, cm):
        def fn(e):
            e.memset(tile_[:], fill0)
            return e.affine_select(out=tile_[:], in_=tile_[:], pattern=pattern, compare_op=cmp, fill=fill,
                                   base=0, channel_multiplier=cm)
        return fn
    P.op(POOL, lambda e: e.memset(ident[:], 0.0), writes=[("c", "ident")])
    P.op(POOL, lambda e: e.affine_select(out=ident[:], in_=ident[:], pattern=[[-1, 128]], compare_op=ALU.not_equal,
                                         fill=1.0, base=0, channel_multiplier=1), reads=[("c", "ident")], writes=[("c", "ident")])
    P.op(POOL, lambda e: e.memset(identf[:], 0.0), writes=[("c", "identf")])
    P.op(POOL, lambda e: e.affine_select(out=identf[:], in_=identf[:], pattern=[[-1, 16]], compare_op=ALU.not_equal,
                                         fill=1.0, base=0, channel_multiplier=1), reads=[("c", "identf")], writes=[("c", "identf")])
    P.op(POOL, lambda e: e.memset(LT[:], 1.0), writes=[("c", "LT")])
    P.op(POOL, lambda e: e.affine_select(out=LT[:], in_=LT[:], pattern=[[1, 128]], compare_op=ALU.is_ge, fill=0.0,
                                         base=0, channel_multiplier=-1), reads=[("c", "LT")], writes=[("c", "LT")])
    P.op(POOL, lambda e: e.memset(ones[:], 1.0), writes=[("c", "ones")])
    P.op(POOL, lambda e: e.memset(mhalf[:], -0.5), writes=[("c", "mhalf")])
    P.op(POOL, lambda e: e.memset(colmask[:], 0.0), writes=[("c", "colmask")])
    P.op(POOL, lambda e: e.affine_select(out=colmask[:], in_=colmask[:], pattern=[[1, 16], [-1, 16]],
                                         compare_op=ALU.not_equal, fill=1.0, base=0, channel_multiplier=0),
         reads=[("c", "colmask")], writes=[("c", "colmask")])
    P.op(DVE, lambda e: e.memset(stat[:], 0.0), writes=[("c", "stat")])
    P.op(POOL, lambda e: e.dma_start(out=Wg[0:16, :], in_=gla_w_gate_up[0]), writes=[("Wg", 0)], dma="wg0")
    P.op(POOL, lambda e: e.dma_start(out=Wg[16:17, :], in_=gla_b_gate[0:1, :]), writes=[("Wg", 1)], dma="wg1")

    P.op(DVE, lambda e: e.memset(Sst[:], 0.0), writes=[("S", h) for h in range(4)])
    P.op(DVE, lambda e: e.memset(Sbf[:], 0.0), writes=[("Sbf",)])

    def load_x(p_, t_):
        r0_ = p_ * PTOK + t_ * 128
        dma(xs[:, t_, :], xp[r0_:r0_ + 128, :], "xl%d" % t_, writes=[("xs", t_)])
    for t_ in range(TPP):
        load_x(0, t_)
    P.set_global_barrier()
    for li in range(2):
        for jj, j in enumerate((0, 2)):
            dma(gT[:, li * 2 + jj, :], norm_g[li, j].rearrange("(k p) -> p k", p=128), "c2_%d" % (li * 2 + jj),
                writes=[("c", "gT", li * 2 + jj)], slow=True)

    if schedule is not None:
        pump(4)

    def late_setup():
        dma(ws_tm[:], sg_w_s[0].rearrange("g t s -> t g s"), "c0", writes=[("ws_tm",)])
        dma(bs_bc[:], sg_b_s[0].rearrange("g t -> (g t)").partition_broadcast(128).rearrange("p (g t) -> p g t", t=128),
            "c1", writes=[("bs_bc",)])
        dma(lgT[:], sg_ln_g[0].rearrange("(c p) -> p c", p=128), "c3", writes=[("c", "lgT")], slow=True)
        dma(lbT[:], sg_ln_b[0].rearrange("(c p) -> p c", p=128), "c4", writes=[("c", "lbT")], slow=True)
        dma(gn_bc[:], gla_out_norm_g[0].partition_broadcast(128), "c5", writes=[("c", "gn")])
        dma(w00[:], sg_w_s[0, :, 0, 0].partition_broadcast(16), "c6", writes=[("c", "w00")], slow=True)
        dma(b00[:], sg_b_s[0, :, 0].partition_broadcast(16), "c7", writes=[("c", "b00")], slow=True)

        P.op(POOL, lambda e: e.affine_select(out=ws_tm[:], in_=ws_tm[:], pattern=[[0, 8], [-1, 128]],
                                             compare_op=ALU.is_ge, fill=0.0, base=0, channel_multiplier=1),
             reads=[("ws_tm",)], writes=[("ws_tm",)])
        P.op(DVE, lambda e: e.tensor_copy(out=ws_bf[:], in_=ws_tm[:]), reads=[("ws_tm",)], writes=[("ws_bf",)])

        def tr_ws(e):
            ins = None
            for g in range(8):
                ins = e.transpose(pt[:, 0, g * 128:(g + 1) * 128], ws_bf[:, g, :], ident[:])
            return ins
        P.op(PE, tr_ws, reads=[("ws_bf",), ("c", "ident")], writes=[("pt", 0)])
        P.op(DVE, lambda e: e.tensor_copy(out=wsT[:], in_=pt[:, 0, :].rearrange("p (g t) -> p g t", t=128)),
             reads=[("pt", 0)], writes=[("c", "wsT")])
        for half in range(2):
            def rs(e, half=half):
                ins = None
                for gi in range(4):
                    g = half * 4 + gi
                    ins = e.matmul(bank(half, 128, 128, gi * 128), ones[:], wsT[:, g, :], start=True, stop=True)
                return ins
            P.op(PE, rs, reads=[("c", "wsT"), ("c", "ones")], writes=[("ps", half)])
            for gi in range(4):
                g = half * 4 + gi
                for cc in range(2):
                    c = 2 * g + cc
                    P.op(DVE, lambda e, half=half, gi=gi, g=g, c=c: e.scalar_tensor_tensor(
                        out=Bc[:, c, :], in0=bank(half, 128, 128, gi * 128), scalar=lbT[:, c:c + 1],
                        in1=bs_bc[:, g, :], op0=ALU.mult, op1=ALU.add),
                        reads=[("ps", half), ("c", "lbT"), ("bs_bc",)], writes=[("c", "Bc", c)])


    gbc = [0]
    setup_done = [False]
    carryB = []

    def pop_carry(tile=None):
        while carryB and (tile is None or carryB[0][0] == tile):
            norm_in_B(*carryB.pop(0))
            if tile is not None:
                break

    def load_gb(li, j):
        gi = gbc[0] % 2
        gbc[0] += 1
        dma(gb[gi][:], norm_g[li, j].partition_broadcast(128), "gb%d" % gi, writes=[("gb", gi)])
        return gi

    for p in range(NPASS):
        last = (p == NPASS - 1)
        tiles = [(t, 128) for t in range(TPP)] + ([(TPP, NS)] if last else [])
        groups = [(0, PTOK)] + ([(PTOK, TW)] if last else [])
        if p == 0:
            dma(xs[0:NS, TPP, :], xsam[:, :], "xl4", writes=[("xs", TPP)])
        todo = (tiles + [(TPP, NS)]) if p == 0 else []
        pend0 = None
        for (t, np_) in todo:
            hi = norm_in_A(t, np_)
            if pend0 is not None:
                norm_in_B(*pend0)
            pend0 = (t, np_, hi, 0)
        if pend0 is not None:
            norm_in_B(*pend0)

        for li in range(2):
            P.set_barrier()
            gbi_mix = load_gb(li, 1)
            if li == 0:
                if last:
                    dma(lg_bc[:], sg_ln_g[0].partition_broadcast(128), "c8", writes=[("lg_bc",)])
                    dma(lb_bc[:], sg_ln_b[0].partition_broadcast(128), "c9", writes=[("lb_bc",)])
                Wv = [wslab("sg_w_in", 0, 0, 8, 2048 + 512 * nb, 512) for nb in range(4)]
                vstate = {}

                def vA(t, np_):
                    vb = t % 3
                    cols = slice(t * 128, t * 128 + np_)
                    s1 = newstat(4)
                    for nb in range(4):
                        P.op(PE, mm_group(bank(nb, np_), [(hT[:, kc, cols], Wv[nb][0][:, kc, :]) for kc in range(8)]),
                             reads=[("hT", t), Wv[nb][1]], writes=[("ps", nb)])
                        P.op(ACT, lambda e, nb=nb: e.activation(
                            out=vbuf[vb][0:np_, nb * 512:(nb + 1) * 512], in_=bank(nb, np_), func=AF.Gelu,
                            accum_out=S(s1 + nb, np_)),
                            reads=[("ps", nb)], writes=[("v", vb, nb), ("st", s1 + nb)])
                    s2 = newstat()
                    P.op(ACT, lambda e: e.activation(
                        out=junk[0:np_, :], in_=vbuf[vb][0:np_, :], func=AF.Square, accum_out=S(s2, np_)),
                        reads=[("v", vb, nb) for nb in range(4)], writes=[("st", s2), ("junk", 0), ("junk", 1)])
                    vstate[t] = (s1, s2)

                def vA2(t, np_):
                    vb = t % 3
                    s1, s2 = vstate[t]
                    c = newstat(8)
                    rd = [("st", s1 + i) for i in range(4)] + [("st", s2)]
                    P.op(DVE, lambda e: e.reduce_sum(out=S(c, np_), in_=S(s1, np_, 4), axis=AX.X),
                         reads=rd, writes=[("st", c)])
                    P.op(DVE, lambda e: e.tensor_scalar(S(c + 1, np_), S(c, np_), 1.0 / 2048, None, ALU.mult),
                         reads=[("st", c)], writes=[("st", c + 1)])
                    P.op(DVE, lambda e: e.tensor_scalar(S(c + 2, np_), S(s2, np_), 1.0 / 2048, None, ALU.mult),
                         reads=[("st", s2)], writes=[("st", c + 2)])
                    P.op(DVE, lambda e: e.tensor_tensor(out=S(c + 3, np_), in0=S(c + 1, np_), in1=S(c + 1, np_), op=ALU.mult),
                         reads=[("st", c + 1)], writes=[("st", c + 3)])
                    P.op(DVE, lambda e: e.tensor_tensor(out=S(c + 4, np_), in0=S(c + 2, np_), in1=S(c + 3, np_), op=ALU.subtract),
                         reads=[("st", c + 2), ("st", c + 3)], writes=[("st", c + 4)])
                    P.op(DVE, lambda e: e.tensor_scalar(S(c + 5, np_), S(c + 4, np_), EPS, None, ALU.add),
                         reads=[("st", c + 4)], writes=[("st", c + 5)])
                    P.op(POOL, lambda e: e.tensor_tensor(out=S(c + 6, np_), in0=S(c + 5, np_), in1=mhalf[0:np_, 0:1], op=ALU.pow),
                         reads=[("st", c + 5), ("c", "mhalf")], writes=[("st", c + 6)])
                    P.op(DVE, lambda e: e.scalar_tensor_tensor(out=S(c + 7, np_), in0=S(c + 1, np_), scalar=-1.0,
                                                               in1=S(c + 6, np_), op0=ALU.mult, op1=ALU.mult),
                         reads=[("st", c + 1), ("st", c + 6)], writes=[("st", c + 7)])
                    vkeys = [("v", vb, nb) for nb in range(4)]
                    special = last and (t == TPP - 1 or np_ == NS)
                    if np_ == 128:
                        P.op(DVE, lambda e: e.tensor_scalar(
                            vhat[vb][0:np_, :], vbuf[vb][0:np_, :], S(c + 6, np_), S(c + 7, np_), ALU.mult, ALU.add),
                            reads=vkeys + [("st", c + 6), ("st", c + 7)], writes=[("vhat", vb)])
                    if special:
                        tA = tmp[0:np_].rearrange("p a b -> p (a b)")
                        P.op(DVE, lambda e: e.tensor_scalar(
                            tA, vbuf[vb][0:np_, :], S(c + 6, np_), S(c + 7, np_), ALU.mult, ALU.add),
                            reads=vkeys + [("st", c + 6), ("st", c + 7)], writes=[("tmp", 0), ("tmp", 1)])
                        P.op(DVE, lambda e: e.tensor_tensor(out=tA, in0=tA, in1=lg_bc[0:np_, :], op=ALU.mult),
                             reads=[("tmp", 0), ("tmp", 1), ("lg_bc",)], writes=[("tmp", 0), ("tmp", 1)])
                        P.op(DVE, lambda e: e.tensor_tensor(out=tA, in0=tA, in1=lb_bc[0:np_, :], op=ALU.add),
                             reads=[("tmp", 0), ("tmp", 1), ("lb_bc",)], writes=[("tmp", 0), ("tmp", 1)])
                        if np_ == 128:
                            dma(sgvp[:, :], tA, "sgvp", reads=[("tmp", 0), ("tmp", 1)])
                        else:
                            dma(sgvs[:, :], tA, "sgvs", reads=[("tmp", 0), ("tmp", 1)])
                            tA3 = tA.rearrange("p (g c) -> p g c", c=256)
                            P.op(DVE, lambda e: e.tensor_tensor(
                                out=tA3, in0=tA3, in1=w00[:, :].unsqueeze(2).to_broadcast([NS, 8, 256]), op=ALU.mult),
                                reads=[("tmp", 0), ("tmp", 1), ("c", "w00")], writes=[("tmp", 0), ("tmp", 1)])
                            P.op(DVE, lambda e: e.tensor_tensor(
                                out=mixs[:, :].rearrange("p (g c) -> p g c", c=256), in0=tA3,
                                in1=b00[:, :].unsqueeze(2).to_broadcast([NS, 8, 256]), op=ALU.add),
                                reads=[("tmp", 0), ("tmp", 1), ("c", "b00")], writes=[("mixs",)])

                def vB(t, np_):
                    vb = t % 3
                    cols = slice(t * 128, t * 128 + np_)
                    if np_ == 128:
                        for cq in range(4):
                            b = 4 + cq % 2

                            def sp_mm(e, cq=cq, b=b):
                                ins = None
                                for ci in range(4):
                                    cc = 4 * cq + ci
                                    ins = e.matmul(bank(b, 128, 128, ci * 128), vhat[vb][:, cc * 128:(cc + 1) * 128],
                                                   wsT[:, cc // 2, :], start=True, stop=True)
                                return ins
                            P.op(PE, sp_mm, reads=[("vhat", vb), ("c", "wsT")], writes=[("ps", b)])
                            for ci in range(4):
                                cc = 4 * cq + ci
                                P.op(DVE, lambda e, b=b, ci=ci, cc=cc: e.scalar_tensor_tensor(
                                    out=gated[:, cc, cols], in0=bank(b, 128, 128, ci * 128), scalar=lgT[:, cc:cc + 1],
                                    in1=Bc[:, cc, :], op0=ALU.mult, op1=ALU.add),
                                    reads=[("ps", b), ("c", "lgT"), ("c", "Bc", cc)], writes=[("gated", cc, t)])
                    else:
                        pb = ptc[0] % 2
                        ptc[0] += 1

                        def tr_mix(e):
                            ins = None
                            for cc in range(16):
                                ins = e.transpose(pt[:, pb, cc * 16:(cc + 1) * 16], mixs[:, cc * 128:(cc + 1) * 128],
                                                  ident[0:NS, 0:NS])
                            return ins
                        P.op(PE, tr_mix, reads=[("mixs",), ("c", "ident")], writes=[("pt", pb)])
                        P.op(DVE, lambda e: e.tensor_copy(
                            out=gated[:, :, PTOK:TW], in_=pt[:, pb, 0:256].rearrange("p (c j) -> p c j", j=16)),
                            reads=[("pt", pb)], writes=[("gated", cc, t) for cc in range(16)])

                pv = []
                pop_carry(0)
                vtiles = (tiles[:-2] + [tiles[-1], tiles[-2]]) if last else tiles
                for (t, np_) in vtiles:
                    vA(t, np_)
                    pop_carry(t + 1)
                    if len(pv) == 2:
                        if p == 0 and not setup_done[0]:
                            late_setup()
                            setup_done[0] = True
                        vB(*pv.pop(0))
                    vA2(t, np_)
                    pv.append((t, np_))
                while pv:
                    vB(*pv.pop(0))

                for w_ in Wv:
                    wrelease(w_)
                uc = 0
                for j in range(4):
                    Wu = wslab("sg_w_in", 0, 0, 8, 512 * j, 512)
                    for ci in range(4):
                        cc = 4 * j + ci
                        for (t0, t1) in groups:
                            n = t1 - t0
                            b = uc % 6
                            ub = uc % 2
                            uc += 1
                            tl = list(range(TPP)) if t0 == 0 else [TPP]
                            P.op(PE, mm_group(bank(b, 128, n), [(Wu[0][:, kc, ci * 128:(ci + 1) * 128], hT[:, kc, t0:t1])
                                                               for kc in range(8)]),
                                 reads=[("hT", t) for t in tl] + [Wu[1]], writes=[("ps", b)])
                            P.op(ACT, lambda e, b=b, n=n, ub=ub: e.activation(out=ut[ub][:, 0:n], in_=bank(b, 128, n), func=AF.Gelu),
                                 reads=[("ps", b)], writes=[("ut", ub)])
                            P.op(DVE, lambda e, cc=cc, t0=t0, t1=t1, ub=ub, n=n: e.tensor_tensor(
                                out=gated[:, cc, t0:t1], in0=gated[:, cc, t0:t1], in1=ut[ub][:, 0:n], op=ALU.mult),
                                reads=[("ut", ub)] + [("gated", cc, t) for t in tl],
                                writes=[("gated", cc, t) for t in tl])
                    wrelease(Wu)
                Wo = {}
                for kh in range(2):
                    for nn in range(2):
                        Wo[(kh, nn)] = wslab("sg_w_out", 0, kh * 1024, 8, nn * 512, 512)
                pend = None
                for i, (t, np_) in enumerate(tiles):
                    b0 = (i % 3) * 2
                    cols = slice(t * 128, t * 128 + np_)
                    for nn in range(2):
                        P.op(PE, mm_group(bank(b0 + nn, np_), [(gated[:, kc, cols], Wo[(kc // 8, nn)][0][:, kc % 8, :])
                                                               for kc in range(16)]),
                             reads=[("gated", kc, t) for kc in range(16)] + [Wo[(0, nn)][1], Wo[(1, nn)][1]],
                             writes=[("ps", b0 + nn)])
                    if pend is not None:
                        norm_in_B(*pend)
                    norm_out(t, np_, b0, gbi_mix)
                    hi = norm_in_A(t, np_)
                    pend = (t, np_, hi, li * 2 + 1)
                carryB.append(pend)
                for w_ in Wo.values():
                    wrelease(w_)
            else:
                bcx = [0]
                def proj_unit(which, j, Wv_, t, np_):
                    b = bcx[0] % 4
                    bcx[0] += 1
                    cols = slice(t * 128, t * 128 + np_)
                    P.op(PE, mm_group(bank(b, np_), [(hT[:, kc, cols], Wv_[0][:, kc, :]) for kc in range(8)]),
                         reads=[("hT", t), Wv_[1]], writes=[("ps", b)])
                    if which == 0:
                        P.op(ACT, lambda e: e.activation(
                            out=Vt[0:np_, t, j * 512:(j + 1) * 512], in_=bank(b, np_), func=AF.Copy),
                            reads=[("ps", b)], writes=[("V", t, j)])
                    else:
                        P.op(ACT, lambda e: e.activation(
                            out=SG[0:np_, t, j * 512:(j + 1) * 512], in_=bank(b, np_), func=AF.Silu),
                            reads=[("ps", b)], writes=[("SG", t, j)])

                def proj_tok(which, steps_):
                    if which == 0 and carryB:
                        Ws = [wslab("gla_w_in", 0, 0, 8, 1024 + 512 * j, 512) for j in range(2)]
                        for j in range(2):
                            for (t, np_) in tiles[:-1]:
                                proj_unit(which, j, Ws[j], t, np_)
                        pop_carry()
                        for j in range(2):
                            proj_unit(which, j, Ws[j], tiles[-1][0], tiles[-1][1])
                        for w_ in Ws:
                            wrelease(w_)
                        return
                    for j in range(2):
                        Wv_ = wslab("gla_w_in", 0, 0, 8, 1024 + 1024 * which + 512 * j, 512)
                        for (t, np_) in tiles:
                            pop_carry(t)
                            proj_unit(which, j, Wv_, t, np_)
                            for _ in range(2):
                                if steps_:
                                    f_, a_ = steps_.pop(0)
                                    f_(a_)
                        wrelease(Wv_)
                proj_tok(0, [])
                P.op(DVE, lambda e: e.memset(rT[:], 1.0), writes=[("rT",)])
                Wr = wslab("gla_w_in", 0, 0, 8, 3072, 16)
                for (t0, t1) in groups:
                    n = t1 - t0
                    tl = list(range(TPP)) if t0 == 0 else [TPP]
                    P.op(PE, mm_group(bank(0, 16, n), [(Wr[0][:, kc, 0:16], hT[:, kc, t0:t1]) for kc in range(8)]),
                         reads=[("hT", t) for t in tl] + [Wr[1]], writes=[("ps", 0)])
                    P.op(DVE, lambda e, t0=t0, t1=t1, n=n: e.tensor_copy(out=rT[0:16, t0:t1], in_=bank(0, 16, n)),
                         reads=[("ps", 0)], writes=[("rT",)])
                wrelease(Wr)
                for gi_, (t, np_) in enumerate(tiles):
                    gbk = 4 + gi_ % 2
                    cols = slice(t * 128, t * 128 + np_)
                    P.op(PE, lambda e, gbk=gbk, np_=np_, cols=cols: e.matmul(bank(gbk, np_), rT[0:17, cols], Wg[0:17, :],
                                                                            start=True, stop=True),
                         reads=[("rT",), ("Wg", 0), ("Wg", 1)], writes=[("ps", gbk)])
                    P.op(ACT, lambda e, gbk=gbk, np_=np_: e.activation(out=e32[0:np_, :], in_=bank(gbk, np_), func=AF.Exp, scale=-1.0),
                         reads=[("ps", gbk)], writes=[("e32",)])
                    P.op(ACT, lambda e, np_=np_, t=t: e.activation(out=Lb[0:np_, t, :], in_=e32[0:np_, :], func=AF.Ln, bias=1.0),
                         reads=[("e32",)], writes=[("l", t)])
                bc = bcx[0]
                for which in range(2):
                    Wq = wslab("gla_w_in", 0, 0, 8, 512 * which, 512)
                    for h in range(4):
                        for (t0, t1) in groups:
                            n = t1 - t0
                            b = bc % 4
                            bc += 1
                            tl = list(range(TPP)) if t0 == 0 else [TPP]
                            P.op(PE, mm_group(bank(b, 128, n), [(Wq[0][:, kc, h * 128:(h + 1) * 128], hT[:, kc, t0:t1])
                                                               for kc in range(8)]),
                                 reads=[("hT", t) for t in tl] + [Wq[1]], writes=[("ps", b)])
                            if which == 0:
                                P.op(ACT, lambda e, b=b, n=n, h=h, t0=t0, t1=t1: e.activation(
                                    out=qT[:, h, t0:t1], in_=bank(b, 128, n), func=AF.Copy, scale=float(128 ** -0.5)),
                                    reads=[("ps", b)], writes=[("qT", h, t0)])
                            else:
                                P.op(DVE, lambda e, b=b, n=n, h=h, t0=t0, t1=t1: e.tensor_copy(
                                    out=kT[:, h, t0:t1], in_=bank(b, 128, n)),
                                    reads=[("ps", b)], writes=[("kT", h, t0)])
                    if which == 1 and last:
                        b = bc % 4
                        bc += 1
                        P.op(PE, mm_group(bank(b, NS, 512), [(hT[:, kc, PTOK:TW], Wq[0][:, kc, :]) for kc in range(8)]),
                             reads=[("hT", TPP), Wq[1]], writes=[("ps", b)])
                        P.op(DVE, lambda e, b=b: e.tensor_copy(out=Ktok_s[:, :], in_=bank(b, NS, 512)),
                             reads=[("ps", b)], writes=[("Ktok_s",)])
                    wrelease(Wq)

                bcx[0] = bc
                def pre1(t):
                    eb = 0
                    cols = slice(t * 128, (t + 1) * 128)

                    def cum(e):
                        ins = None
                        for h in range(4):
                            ins = e.matmul(bank(4, 128, 128, h * 128), Lb[:, t, h * 128:(h + 1) * 128], LT[:, :],
                                           start=True, stop=True)
                        return ins
                    P.op(PE, cum, reads=[("l", t), ("c", "LT")], writes=[("ps", 4)])
                    P.op(ACT, lambda e: e.activation(out=Ep[eb][:].rearrange("p h t -> p (h t)"), in_=bank(4), func=AF.Exp,
                                                     scale=-1.0 / 16),
                         reads=[("ps", 4)], writes=[("Ep", eb)])
                    P.op(ACT, lambda e: e.activation(out=Em[eb][:].rearrange("p h t -> p (h t)"), in_=bank(4), func=AF.Exp,
                                                     scale=1.0 / 16),
                         reads=[("ps", 4)], writes=[("Em", eb)])
                    P.op(DVE, lambda e: e.tensor_tensor(out=Q32[t % 2][:], in0=qT[:, :, cols], in1=Ep[eb][:], op=ALU.mult),
                         reads=[("qT", h, 0) for h in range(4)] + [("Ep", eb), ("qTt", t)], writes=[("Q32", t % 2)])
                    P.op(DVE, lambda e: e.tensor_tensor(out=K32[t % 2][:], in0=kT[:, :, cols], in1=Em[eb][:], op=ALU.mult),
                         reads=[("kT", h, 0) for h in range(4)] + [("Em", eb), ("kTt", t)], writes=[("K32", t % 2)])
                    P.op(DVE, lambda e: e.tensor_tensor(out=qT[:, :, cols], in0=qT[:, :, cols], in1=Ep[eb][:], op=ALU.mult),
                         reads=[("qT", h, 0) for h in range(4)] + [("Ep", eb)], writes=[("qTt", t)])
                    P.op(DVE, lambda e: e.tensor_tensor(out=kT[:, :, cols], in0=kT[:, :, cols], in1=Em[eb][:], op=ALU.mult),
                         reads=[("kT", h, 0) for h in range(4)] + [("Em", eb)], writes=[("kTt", t)])
                    P.op(DVE, lambda e: e.tensor_copy(out=EL[:, t * 4:(t + 1) * 4], in_=Ep[eb][:, :, 127]),
                         reads=[("Ep", eb)], writes=[("EL", t)])
                    for h in range(4):
                        P.op(DVE, lambda e, h=h: e.tensor_scalar(Kh[t % 2][:, h, :], kT[:, h, cols], EL[:, t * 4 + h:t * 4 + h + 1],
                                                                 None, ALU.mult),
                             reads=[("kTt", t), ("EL", t)], writes=[("Kh", t % 2, h)])

                def pre2(t):
                    cols = slice(t * 128, (t + 1) * 128)
                    pb = ptc[0] % 2
                    ptc[0] += 1

                    def trk(e):
                        ins = None
                        for h in range(4):
                            ins = e.transpose(pt[:, pb, h * 128:(h + 1) * 128], Kh[t % 2][:, h, :], ident[:])
                        return ins
                    P.op(PE, trk, reads=[("Kh", t % 2, h) for h in range(4)] + [("c", "ident")], writes=[("pt", pb)])
                    P.op(ACT, lambda e: e.activation(out=Ktok[t][:].rearrange("p h k -> p (h k)"),
                                                     in_=pt[:, pb, 0:512], func=AF.Copy),
                         reads=[("pt", pb)], writes=[("Ktok", t)])

                def pre3(t):
                    cols = slice(t * 128, (t + 1) * 128)

                    def att(e):
                        ins = None
                        for h in range(4):
                            ins = e.matmul(bank(5, 128, 128, h * 128), K32[t % 2][:, h, :], Q32[t % 2][:, h, :], start=True, stop=True)
                        return ins
                    P.op(PE, att, reads=[("K32", t % 2), ("Q32", t % 2)], writes=[("ps", 5)])
                    P.op(DVE, lambda e: e.tensor_tensor(
                        out=ATb[t][:], in0=bank(5).rearrange("p (h t) -> p h t", t=128),
                        in1=LT[:, :].unsqueeze(1).to_broadcast([128, 4, 128]), op=ALU.mult),
                        reads=[("ps", 5), ("c", "LT")], writes=[("AT", t)])

                steps = [(pre1, 0), (pre1, 1), (pre2, 0), (pre3, 0), (pre1, 2), (pre2, 1), (pre3, 1), (pre1, 3),
                         (pre2, 2), (pre3, 2), (pre2, 3), (pre3, 3)]
                proj_tok(1, steps)
                while steps:
                    f_, a_ = steps.pop(0)
                    f_(a_)
                Wo2 = [wslab("gla_w_out", 0, 0, 8, nn * 512, 512) for nn in range(2)]

                def onorm(t, np_, ob):
                    c = newstat(4)
                    for h in range(4):
                        j0, jkeys = jk(256)
                        P.op(ACT, lambda e, h=h, j0=j0: e.activation(out=junk[0:np_, j0:j0 + 256], in_=bank(2 + h // 2, np_, 256, (h % 2) * 256),
                                                                     func=AF.Square, accum_out=S(c + h, np_)),
                             reads=[("ps", 2 + h // 2)], writes=[("st", c + h)] + jkeys)
                    c2 = newstat(4)
                    c3 = newstat(4)
                    P.op(DVE, lambda e: e.tensor_scalar(S(c2, np_, 4), S(c, np_, 4), 1.0 / 256, EPS, ALU.mult, ALU.add),
                         reads=[("st", c + h) for h in range(4)], writes=[("st", c2)])
                    P.op(POOL, lambda e: e.tensor_tensor(out=S(c3, np_, 4), in0=S(c2, np_, 4), in1=mhalf[0:np_, 0:4], op=ALU.pow),
                         reads=[("st", c2), ("c", "mhalf")], writes=[("st", c3)])
                    for h in range(4):
                        P.op(DVE, lambda e, h=h: e.scalar_tensor_tensor(
                            out=tmp[0:np_, 1, h * 256:(h + 1) * 256], in0=bank(2 + h // 2, np_, 256, (h % 2) * 256),
                            scalar=S(c3 + h, np_), in1=gn_bc[0:np_, :], op0=ALU.mult, op1=ALU.mult),
                            reads=[("ps", 2 + h // 2), ("st", c3), ("c", "gn")], writes=[("tmp", 1, h)])
                    P.op(DVE, lambda e: e.tensor_tensor(out=og[ob][0:np_, :], in0=tmp[0:np_, 1, :], in1=SG[0:np_, t, :],
                                                        op=ALU.mult),
                         reads=[("tmp", 1, h) for h in range(4)] + [("SG", t, 0), ("SG", t, 1)], writes=[("og", ob)])

                def trO(t, np_, ob):
                    pb = ptc[0] % 2
                    ptc[0] += 1

                    def tr(e):
                        ins = None
                        for kc in range(8):
                            ins = e.transpose(pt[:, pb, kc * 128: kc * 128 + np_], og[ob][0:np_, kc * 128:(kc + 1) * 128],
                                              ident[0:np_, 0:np_])
                        return ins
                    P.op(PE, tr, reads=[("og", ob), ("c", "ident")], writes=[("pt", pb)])
                    P.op(ACT, lambda e: e.activation(out=ogT[ob][:, :, 0:np_],
                                                     in_=pt[:, pb, :].rearrange("p (k c) -> p k c", c=128)[:, :, 0:np_],
                                                     func=AF.Copy),
                         reads=[("pt", pb)], writes=[("ogT", ob)])

                def wout_mm(t, np_, ob):
                    for nn in range(2):
                        P.op(PE, mm_group(bank(nn, np_), [(ogT[ob][:, kc, 0:np_], Wo2[nn][0][:, kc, :]) for kc in range(8)]),
                             reads=[("ogT", ob), Wo2[nn][1]], writes=[("ps", nn)])

                ptl = [t for (t, np_) in tiles if np_ == 128]
                Sflat = Sst[:].rearrange("p h v -> p (h v)")
                T_ = len(ptl)
                stt_ = {}
                m_ap = ps[:, 0:1024]
                for i in range(T_ + 4):
                    ta = ptl[i] if i < T_ else None
                    tb = ptl[i - 1] if 0 <= i - 1 < T_ else None
                    tc_ = ptl[i - 2] if 0 <= i - 2 < T_ else None
                    td = ptl[i - 3] if 0 <= i - 3 < T_ else None
                    te = ptl[i - 4] if 0 <= i - 4 < T_ else None
                    if tc_ is not None:
                        c2m = stt_[tc_]["m2"]
                        P.op(DVE, lambda e, c2m=c2m: e.scalar_tensor_tensor(out=tmp[:, 0, :], in0=m_ap, scalar=S(c2m),
                                                                            in1=gb[gbi_mix][:, :], op0=ALU.mult, op1=ALU.mult),
                             reads=[("ps", 0), ("ps", 1), ("st", c2m), ("gb", gbi_mix)], writes=[("tmp", 0)])
                    if ta is not None:
                        t = ta
                        cols = slice(t * 128, (t + 1) * 128)
                        ob = i % 2

                        def omm(e, t=t, cols=cols):
                            ins = None
                            for h in range(4):
                                o_ap = bank(2 + h // 2, 128, 256, (h % 2) * 256)
                                e.matmul(o_ap, qT[:, h, cols], Sbf[:, h, :], start=True, stop=False)
                                ins = e.matmul(o_ap, ATb[t][:, h, :], Vt[:, t, h * 256:(h + 1) * 256], start=False, stop=True)
                            return ins
                        P.op(PE, omm, reads=[("qTt", t), ("Sbf",), ("AT", t), ("V", t, 0), ("V", t, 1)],
                             writes=[("ps", 2), ("ps", 3)])

                        def smm(e, t=t):
                            ins = None
                            for h in range(4):
                                ins = e.matmul(bank(4 + h // 2, 128, 256, (h % 2) * 256), Ktok[t][:, h, :],
                                               Vt[:, t, h * 256:(h + 1) * 256], start=True, stop=True)
                            return ins
                        P.op(PE, smm, reads=[("Ktok", t), ("V", t, 0), ("V", t, 1)], writes=[("ps", 4), ("ps", 5)])
                        stt_[t] = dict(ob=ob)
                    if ta is not None:
                        c = newstat(4)
                        for h in range(4):
                            j0, jkeys = jk(256)
                            P.op(ACT, lambda e, h=h, c=c, j0=j0: e.activation(out=junk[:, j0:j0 + 256], in_=bank(2 + h // 2, 128, 256, (h % 2) * 256),
                                                                              func=AF.Square, accum_out=S(c + h)),
                                 reads=[("ps", 2 + h // 2)], writes=[("st", c + h)] + jkeys)
                        stt_[ta]["c"] = c
                    if ta is not None:
                        t = ta
                        for h in range(4):
                            P.op(DVE, lambda e, h=h, t=t: e.scalar_tensor_tensor(
                                out=Sst[:, h, :], in0=Sst[:, h, :], scalar=EL[:, t * 4 + h:t * 4 + h + 1],
                                in1=bank(4 + h // 2, 128, 256, (h % 2) * 256), op0=ALU.mult, op1=ALU.add),
                                reads=[("S", h), ("EL", t), ("ps", 4 + h // 2)], writes=[("S", h)])
                        P.op(ACT, lambda e: e.activation(out=Sbf[:].rearrange("p h v -> p (h v)"), in_=Sflat, func=AF.Copy),
                             reads=[("S", h) for h in range(4)], writes=[("Sbf",)])
                    if tb is not None:
                        wout_mm(tb, 128, stt_[tb]["ob"])
                    if ta is not None:
                        c = stt_[ta]["c"]
                        c2 = newstat(4)
                        c3 = newstat(4)
                        P.op(DVE, lambda e, c=c, c2=c2: e.tensor_scalar(S(c2, 128, 4), S(c, 128, 4), 1.0 / 256, EPS, ALU.mult, ALU.add),
                             reads=[("st", c + h) for h in range(4)], writes=[("st", c2)])
                        P.op(POOL, lambda e, c2=c2, c3=c3: e.tensor_tensor(out=S(c3, 128, 4), in0=S(c2, 128, 4), in1=mhalf[:, 0:4], op=ALU.pow),
                             reads=[("st", c2), ("c", "mhalf")], writes=[("st", c3)])
                        stt_[ta]["c3"] = c3
                    if td is not None:
                        x0 = norm_in_A_sq(td, 128)
                        stt_[td]["x2"] = rstd_from_ss(x0, 128, D, 1.0 / D)
                    if te is not None:
                        if (not last) and te == ptl[-1]:
                            carryB.append((te, 128, stt_[te]["hi"], li * 2 + 1))
                        else:
                            norm_in_B(te, 128, stt_[te]["hi"], li * 2 + 1)
                    if ta is not None:
                        t = ta
                        c3 = stt_[t]["c3"]
                        ob = stt_[t]["ob"]
                        for h in range(4):
                            P.op(DVE, lambda e, h=h, c3=c3: e.scalar_tensor_tensor(
                                out=tmp[:, 1, h * 256:(h + 1) * 256], in0=bank(2 + h // 2, 128, 256, (h % 2) * 256),
                                scalar=S(c3 + h), in1=gn_bc[:, :], op0=ALU.mult, op1=ALU.mult),
                                reads=[("ps", 2 + h // 2), ("st", c3), ("c", "gn")], writes=[("tmp", 1, h)])
                        P.op(DVE, lambda e, t=t, ob=ob: e.tensor_tensor(out=og[ob][:, :], in0=tmp[:, 1, :], in1=SG[:, t, :], op=ALU.mult),
                             reads=[("tmp", 1, h) for h in range(4)] + [("SG", t, 0), ("SG", t, 1)], writes=[("og", ob)])
                    if tc_ is not None:
                        P.op(DVE, lambda e, tc_=tc_: e.tensor_tensor(out=xs[:, tc_, :], in0=xs[:, tc_, :], in1=tmp[:, 0, :], op=ALU.add),
                             reads=[("tmp", 0), ("xs", tc_)], writes=[("xs", tc_)])
                    if td is not None:
                        x2 = stt_[td]["x2"]
                        hi = hbc[0] % 2
                        hbc[0] += 1
                        P.op(ACT, lambda e, td=td, x2=x2, hi=hi: e.activation(out=hb[hi][:, :], in_=xs[:, td, :], func=AF.Copy,
                                                                              scale=S(x2)),
                             reads=[("xs", td), ("st", x2)], writes=[("hb", hi)])
                        stt_[td]["hi"] = hi
                    if tb is not None:
                        stt_[tb]["m0"] = norm_out_sq(tb, 128, 0)
                        stt_[tb]["m2"] = rstd_from_ss(stt_[tb]["m0"], 128, D, 1.0 / D)
                    if ta is not None:
                        trO(ta, 128, stt_[ta]["ob"])
                if last:
                    dma(stp.rearrange("h k v -> k h v"), Sst[:], "stp", reads=[("S", h) for h in range(4)])
                    P.set_barrier()
                    t = TPP

                    def cum_s(e):
                        ins = None
                        for h in range(4):
                            ins = e.matmul(bank(1, 128, NS, h * NS), Lb[0:NS, t, h * 128:(h + 1) * 128], ident[0:NS, 0:NS],
                                           start=True, stop=True)
                        return ins
                    P.op(PE, cum_s, reads=[("l", t), ("c", "ident")], writes=[("ps", 1)])
                    P.op(ACT, lambda e: e.activation(out=EGs[:].rearrange("p h j -> p (h j)"), in_=bank(1, 128, 4 * NS),
                                                     func=AF.Exp, scale=-1.0 / 16),
                         reads=[("ps", 1)], writes=[("EGs",)])
                    for h in range(4):
                        P.op(DVE, lambda e, h=h: e.tensor_tensor(
                            out=Qm[:, h, :, :], in0=qT[:, h, PTOK:TW].unsqueeze(1).to_broadcast([128, NS, NS]),
                            in1=colmask[:], op=ALU.mult),
                            reads=[("qT", h, PTOK), ("c", "colmask")], writes=[("Qm", h)])

                    def s_pre(j):
                        sb = j % 2
                        pbk = 0 if sb == 0 else 4
                        P.op(DVE, lambda e: e.tensor_scalar(Kmask[sb][:, :], Ktok_s[:, :], identf[:, j:j + 1], None, ALU.mult),
                             reads=[("Ktok_s",), ("c", "identf")], writes=[("Kmask", sb)])

                        def outer(e):
                            ins = None
                            for h in range(4):
                                ins = e.matmul(bank(pbk + h // 2, 128, 256, (h % 2) * 256), Kmask[sb][:, h * 128:(h + 1) * 128],
                                               Vt[0:NS, t, h * 256:(h + 1) * 256], start=True, stop=True)
                            return ins
                        P.op(PE, outer, reads=[("Kmask", sb), ("V", t, 0), ("V", t, 1)], writes=[("ps", pbk), ("ps", pbk + 1)])

                    def s_outer(j):
                        sb = j % 2
                        pbk = 0 if sb == 0 else 4
                        for h in range(4):
                            P.op(DVE, lambda e, h=h: e.scalar_tensor_tensor(
                                out=Snew[sb][:, h, :], in0=S_in[j % 3][:, h, :], scalar=EGs[:, h, j:j + 1],
                                in1=bank(pbk + h // 2, 128, 256, (h % 2) * 256), op0=ALU.mult, op1=ALU.add),
                                reads=[("S_in", j % 3), ("EGs",), ("ps", pbk + h // 2)], writes=[("Snew", sb, h)])
                        if j + 3 < NS:
                            dma(S_in[j % 3][:], st_in[j + 3].rearrange("h k v -> k h v"), "sin%d" % (j % 3),
                                writes=[("S_in", j % 3)], eng=ACT)
                        P.op(ACT, lambda e: e.activation(out=Snbf[sb][:].rearrange("p h v -> p (h v)"),
                                                         in_=Snew[sb][:].rearrange("p h v -> p (h v)"), func=AF.Copy),
                             reads=[("Snew", sb, h) for h in range(4)], writes=[("Snbf", sb)])
                        dma(sts[j].rearrange("h k v -> k h v"), Snew[sb][:], "sout%d" % sb,
                            reads=[("Snew", sb, h) for h in range(4)])

                    def s_o(j):
                        sb = j % 2

                        def osm(e):
                            ins = None
                            for h in range(4):
                                ins = e.matmul(bank(2 + h // 2, NS, 256, (h % 2) * 256), Qm[:, h, j, :], Snbf[sb][:, h, :],
                                               start=(j == 0 and h % 2 == 0), stop=(j == NS - 1), skip_group_check=True)
                            return ins
                        P.op(PE, osm, reads=[("Qm", h) for h in range(4)] + [("Snbf", sb)], writes=[("ps", 2), ("ps", 3)])

                    for j0 in range(3):
                        dma(S_in[j0][:], st_in[j0].rearrange("h k v -> k h v"), "sin%d" % j0, writes=[("S_in", j0)], eng=ACT)
                    s_pre(0)
                    for j in range(NS):
                        if j + 1 < NS:
                            s_pre(j + 1)
                        s_outer(j)
                        if j >= 1:
                            s_o(j - 1)
                    s_o(NS - 1)
                    onorm(t, NS, 0)
                    trO(t, NS, 0)
                    wout_mm(t, NS, 0)
                    norm_out(t, NS, 0, gbi_mix)
                    hi = norm_in_A(t, NS)
                    carryB.append((t, NS, hi, li * 2 + 1))

            if li == 1:
                for w_ in Wo2:
                    wrelease(w_)
            P.set_barrier()
            gbi_ffn = load_gb(li, 3)
            if li == 1 and p + 1 < NPASS:
                for t_ in range(TPP):
                    r0_ = (p + 1) * PTOK + t_ * 128
                    dma(xnext[:, t_, :], xp[r0_:r0_ + 128, :], "xn%d" % t_, writes=[("xnext", t_)])
            w13 = ffn_w13[li]
            fc = [0]

            def ab_unit(j, Wa, Wb, ci, t0, t1, tl):
                n = t1 - t0
                ba = (fc[0] % 3) * 2
                sb = fc[0] % 2
                fc[0] += 1
                rk = [("hT", t) for t in tl]
                P.op(PE, mm_group(bank(ba, 128, n), [(Wa[0][:, kc, ci * 128:(ci + 1) * 128], hT[:, kc, t0:t1])
                                                    for kc in range(8)]),
                     reads=rk + [Wa[1]], writes=[("ps", ba)])
                P.op(PE, mm_group(bank(ba + 1, 128, n), [(Wb[0][:, kc, ci * 128:(ci + 1) * 128], hT[:, kc, t0:t1])
                                                        for kc in range(8)]),
                     reads=rk + [Wb[1]], writes=[("ps", ba + 1)])
                P.op(ACT, lambda e: e.activation(out=sab[sb][:, 0:n], in_=bank(ba, 128, n), func=AF.Silu),
                     reads=[("ps", ba)], writes=[("sab", sb)])
                P.op(DVE, lambda e: e.tensor_tensor(
                    out=gTf[:, j, t0:t1], in0=sab[sb][:, 0:n], in1=bank(ba + 1, 128, n), op=ALU.mult),
                    reads=[("sab", sb), ("ps", ba + 1)], writes=[("gTf", j, 0 if t0 < PTOK else PTOK)])

            NSPLIT = 4
            for jj in range(6):
                w = 512 if jj < 5 else 256
                Wa = wslab("ffn_w13", li, 0, 8, jj * 512, w)
                Wb = wslab("ffn_w13", li, 0, 8, DFF + jj * 512, w)
                deferred = []
                for ci in range(w // 128):
                    j = 4 * jj + ci
                    for (t0, t1) in groups:
                        tl = list(range(TPP)) if t0 == 0 else [TPP]
                        if jj == 0 and ci < NSPLIT and t0 == 0:
                            ab_unit(j, Wa, Wb, ci, 0, 384, [0, 1, 2])
                            deferred.append((j, Wa, Wb, ci, 384, 512, [3]))
                        elif jj == 0 and ci < NSPLIT:
                            deferred.append((j, Wa, Wb, ci, t0, t1, tl))
                        else:
                            ab_unit(j, Wa, Wb, ci, t0, t1, tl)
                    if jj == 0 and ci == NSPLIT - 1:
                        pop_carry()
                        for d_ in deferred:
                            ab_unit(*d_)
                        deferred = []
                wrelease(Wa)
                wrelease(Wb)
            W2 = {}
            for nn in range(2):
                for kg in range(3):
                    nk = 8 if kg < 2 else 6
                    W2[(kg, nn)] = wslab("ffn_w2", li, kg * 1024, nk, nn * 512, 512)
            final = (li == 1)
            pendq = []
            chained = []
            split = not last
            prefx = final and (p + 1 < NPASS)
            if split:
                pq = []
                for (t, np_) in tiles:
                    cols = slice(t * 128, t * 128 + np_)
                    P.op(PE, mm_group(bank(t, np_), [(gTf[:, kc, cols], W2[(kc // 8, 0)][0][:, kc % 8, :])
                                                     for kc in range(NFF)]),
                         reads=[("gTf", kc, 0) for kc in range(NFF)] + [W2[(kg, 0)][1] for kg in range(3)],
                         writes=[("ps", t)])
                    if prefx:
                        if pq:
                            norm_in_B(*pq.pop(0))
                        hi = norm_in_A(t, np_, xi=t, nx=True)
                        pq.append((t, np_, hi, 0))
                while prefx and pq:
                    norm_in_B(*pq.pop(0))
                for kg in range(3):
                    wrelease(W2[(kg, 0)])
            for i, (t, np_) in enumerate(tiles):
                b0 = (i % 3) * 2
                cols = slice(t * 128, t * 128 + np_)
                t0k = 0 if np_ == 128 else PTOK
                lagn = (len(tiles) + 1) if final else (2 if split else 1)
                if split:
                    bB = 4 + i % 2
                    P.op(PE, mm_group(bank(bB, np_), [(gTf[:, kc, cols], W2[(kc // 8, 1)][0][:, kc % 8, :])
                                                      for kc in range(NFF)]),
                         reads=[("gTf", kc, t0k) for kc in range(NFF)] + [W2[(kg, 1)][1] for kg in range(3)],
                         writes=[("ps", bB)])
                    while len(pendq) >= lagn:
                        norm_in_B(*pendq.pop(0))
                    norm_out2(t, np_, t, bB, gbi_ffn, ys=(final and np_ == 128))
                else:
                    for nn in range(2):
                        P.op(PE, mm_group(bank(b0 + nn, np_), [(gTf[:, kc, cols], W2[(kc // 8, nn)][0][:, kc % 8, :])
                                                               for kc in range(NFF)]),
                             reads=[("gTf", kc, t0k) for kc in range(NFF)] + [W2[(kg, nn)][1] for kg in range(3)],
                             writes=[("ps", b0 + nn)])
                    while len(pendq) >= lagn:
                        norm_in_B(*pendq.pop(0))
                    norm_out(t, np_, b0, gbi_ffn, ys=(final and np_ == 128))
                if final:
                    if np_ == 128:
                        r0 = p * PTOK + t * 128
                        dma(yp[r0:r0 + 128, :], ystage[:, t, :], "ys%d" % t, reads=[("ystage", t)])
                        if p + 1 < NPASS:
                            P.op(POOL, lambda e, t=t: e.tensor_copy(out=xs[:, t, :], in_=xnext[:, t, :]),
                                 reads=[("xnext", t)], writes=[("xs", t)])
                    else:
                        dma(ysam[:, :], xs[0:NS, t, :], "ys4", reads=[("xs", t)])
                else:
                    hi = norm_in_A(t, np_)
                    pendq.append((t, np_, hi, 2))
            while pendq:
                if len(pendq) == 1:
                    carryB.append(pendq.pop(0))
                else:
                    norm_in_B(*pendq.pop(0))
            for (t, np_) in chained:
                hi = norm_in_A(t, np_, xi=t)
                carryB.append((t, np_, hi, 0))
            for (kg_, nn_), w_ in W2.items():
                if not (split and nn_ == 0):
                    wrelease(w_)

    if schedule is None:
        return specs
    ops = P.ops
    needed = [False] * len(ops)
    for i, o in enumerate(ops):
        for d in o["deps"]:
            od = ops[d]
            if od["dma"] is None and o["dma"] is None and od["eng"] == PE and o["eng"] == PE:
                continue
            needed[d] = True
    streams = {}
    ev = [None] * len(ops)
    for i, o in enumerate(ops):
        s = P._stream(o["eng"], o["dma"])
        if o["dma"] is not None:
            streams[s] = streams.get(s, 0) + 16
            ev[i] = (s, streams[s])
        elif needed[i]:
            streams[s] = streams.get(s, 0) + 1
            ev[i] = (s, streams[s])
        else:
            streams.setdefault(s, 0)
    for e_ in (PE, ACT, DVE, POOL):
        streams.setdefault(e_, 0)
    sem_names = list(streams.keys())
    import contextlib
    with contextlib.ExitStack() as es:
        sems = {}
        for k, s in enumerate(sem_names):
            sems[s] = es.enter_context(nc.semaphore("s%d" % k))
        block = es.enter_context(nc.Block())

        def emit(engname):
            def body(e):
                waited = {}
                for i, o in enumerate(ops):
                    if o["eng"] != engname:
                        continue
                    want = {}
                    for d in o["deps"]:
                        od = ops[d]
                        if od["dma"] is None and o["dma"] is None and od["eng"] == PE and engname == PE:
                            continue
                        s, c = ev[d]
                        if c > want.get(s, 0):
                            want[s] = c
                    for s, c in want.items():
                        if waited.get(s, 0) >= c:
                            continue
                        e.wait_ge(sems[s], c)
                        waited[s] = c
                    ins = o["fn"](e)
                    if ev[i] is not None:
                        s, c = ev[i]
                        ins.then_inc(sems[s], 16 if o["dma"] is not None else 1)
                if engname == SP:
                    for s, c in streams.items():
                        if isinstance(s, tuple) and c > 0:
                            e.wait_ge(sems[s], c)
            return body

        block.sync(emit(SP))
        block.gpsimd(emit(POOL))
        block.vector(emit(DVE))
        block.scalar(emit(ACT))
        block.tensor(emit(PE))
    return nc


_CACHE = {}


def kernel(x_prompt, x_sample, state_gla, norm_g, ffn_w13, ffn_w2, sg_w_in, sg_ln_g, sg_ln_b, sg_w_s,
           sg_b_s, sg_w_out, gla_w_in, gla_w_gate_up, gla_b_gate, gla_out_norm_g, gla_w_out):
    f = lambda a: np.ascontiguousarray(np.asarray(a, dtype=np.float32))
    if "nc" not in _CACHE:
        _CACHE["nc"] = build_program()
    nc = _CACHE["nc"]
    shared = dict(norm_g=f(norm_g), ffn_w13=f(ffn_w13), ffn_w2=f(ffn_w2), sg_w_in=f(sg_w_in), sg_ln_g=f(sg_ln_g),
                  sg_ln_b=f(sg_ln_b), sg_w_s=f(sg_w_s), sg_b_s=f(sg_b_s), sg_w_out=f(sg_w_out), gla_w_in=f(gla_w_in),
                  gla_w_gate_up=f(gla_w_gate_up), gla_b_gate=f(gla_b_gate), gla_out_norm_g=f(gla_out_norm_g),
                  gla_w_out=f(gla_w_out))
    xpr = f(x_prompt)
    xsa = f(x_sample).reshape(128, D)
    stg = f(state_gla)[0]
    in_maps = []
    for c in range(NCORE):
        m = dict(shared)
        m["xp"] = xpr[c]
        m["xsam"] = np.ascontiguousarray(xsa[c * NS:(c + 1) * NS])
        m["st_in"] = np.ascontiguousarray(stg[c * NS:(c + 1) * NS])
        in_maps.append(m)
    res = run_bass_kernel_spmd(nc, in_maps, core_ids=list(range(NCORE)))
    R = res.results
    y_prompt = np.stack([R[c]["yp"] for c in range(NCORE)], 0).astype(np.float32)
    y_sample = np.concatenate([R[c]["ysam"] for c in range(NCORE)], 0).reshape(128, 1, D).astype(np.float32)
    sgv_p = np.stack([R[c]["sgvp"] for c in range(NCORE)], 0)[None].astype(np.float32)
    sgv_s = np.concatenate([R[c]["sgvs"] for c in range(NCORE)], 0).reshape(1, 128, 1, 2048).astype(np.float32)
    st_p = np.stack([R[c]["stp"] for c in range(NCORE)], 0)[None].astype(np.float32)
    st_s = np.concatenate([R[c]["sts"] for c in range(NCORE)], 0)[None].astype(np.float32)
    return (y_prompt, y_sample, sgv_p, sgv_s, st_p, st_s)
```

```python
import numpy as np
import concourse.bass as bass
import concourse.mybir as mybir
from concourse.bass_utils import run_bass_kernel_spmd

F32 = mybir.dt.float32
BF16 = mybir.dt.bfloat16
AF = mybir.ActivationFunctionType
ALU = mybir.AluOpType
AX = mybir.AxisListType

D = 1024
SEQ = 2048
NCORE = 8
NS = 16
NPASS = 4
TPP = 4
PTOK = 512
TW = PTOK + NS
DFF = 2816
NFF = 22
EPS = 1e-6
NSLOT = 8
PE, ACT, DVE, POOL, SP = "pe", "act", "dve", "pool", "sp"


REGION_KEYS = {"gated", "v", "vhat", "ut", "lg_bc", "lb_bc", "mixs", "gTf", "sab", "V", "SG", "qT", "kT", "qTt", "kTt",
               "rT", "e32", "l", "og", "ogT", "Ktok_s", "EL", "Ep", "Em", "Ktok", "AT", "Kh", "S_in", "Snew", "Snbf", "Qm",
               "Kmask", "EGs", "Q32", "K32", "ws_tm", "ws_bf", "bs_bc", "hbx", "ystage", "xnext"}


class Plan:
    def __init__(self):
        self.ops = []
        self.last_w = {}
        self.readers = {}
        self.barrier = {}
        self.gbarrier = {}
        self.last_on = {}
        self.last_region_on = {}

    def _stream(self, eng, dma):
        return ("dma", dma) if dma is not None else eng

    def op(self, eng, fn, reads=(), writes=(), dma=None, nobarrier=False):
        idx = len(self.ops)
        deps = set()
        for k in reads:
            if k in self.last_w:
                deps.add(self.last_w[k])
        for k in writes:
            if k in self.last_w:
                deps.add(self.last_w[k])
            deps.update(self.readers.get(k, ()))
        region = any(k[0] in REGION_KEYS for k in reads) or any(k[0] in REGION_KEYS for k in writes)
        if not nobarrier and eng != PE:
            deps.update(self.gbarrier.values())
            if region:
                deps.update(self.barrier.values())
        self.ops.append(dict(eng=eng, fn=fn, deps=deps, dma=dma))
        for k in reads:
            self.readers.setdefault(k, []).append(idx)
        for k in writes:
            self.last_w[k] = idx
            self.readers[k] = []
        if not nobarrier:
            st = self._stream(eng, dma)
            self.last_on[st] = idx
            if region:
                self.last_region_on[st] = idx
        return idx

    def set_barrier(self):
        self.barrier = dict(self.last_region_on)

    def set_global_barrier(self):
        self.gbarrier = dict(self.last_on)


def build_program():
    specs = _build(None)
    return _build(specs)


def _build(schedule):
    nc = bass.Bass("TRN2", target_bir_lowering=False)
    P = Plan()
    specs = []

    def din(name, shape):
        return nc.dram_tensor(name, list(shape), F32, kind="ExternalInput").ap()

    def dout(name, shape):
        return nc.dram_tensor(name, list(shape), F32, kind="ExternalOutput").ap()

    xp = din("xp", [SEQ, D])
    xsam = din("xsam", [NS, D])
    st_in = din("st_in", [NS, 4, 128, 256])
    norm_g = din("norm_g", [2, 4, D])
    ffn_w13 = din("ffn_w13", [2, D, 2 * DFF])
    ffn_w2 = din("ffn_w2", [2, DFF, D])
    sg_w_in = din("sg_w_in", [1, D, 4096])
    sg_ln_g = din("sg_ln_g", [1, 2048])
    sg_ln_b = din("sg_ln_b", [1, 2048])
    sg_w_s = din("sg_w_s", [1, 8, 128, 128])
    sg_b_s = din("sg_b_s", [1, 8, 128])
    sg_w_out = din("sg_w_out", [1, 2048, D])
    gla_w_in = din("gla_w_in", [1, D, 3088])
    gla_w_gate_up = din("gla_w_gate_up", [1, 16, 512])
    gla_b_gate = din("gla_b_gate", [1, 512])
    gla_out_norm_g = din("gla_out_norm_g", [1, 256])
    gla_w_out = din("gla_w_out", [1, D, D])
    yp = dout("yp", [SEQ, D])
    ysam = dout("ysam", [NS, D])
    sgvp = dout("sgvp", [128, 2048])
    sgvs = dout("sgvs", [NS, 2048])
    stp = dout("stp", [4, 128, 256])
    sts = dout("sts", [NS, 4, 128, 256])

    base = 229376 - nc.sbuf_bytes_remaining
    base = (base + 63) // 64 * 64
    cur = [base]
    cnt = [0]

    def alloc(shape, dt, at=None):
        nbytes = int(np.prod(shape[1:])) * (4 if dt == F32 else 2)
        nbytes = (nbytes + 63) // 64 * 64
        if at is None:
            off = cur[0]
            cur[0] += nbytes
        else:
            off = at[0]
            at[0] += nbytes
        cnt[0] += 1
        assert off + nbytes <= 229344, ("sbuf overflow", off + nbytes)
        return nc.alloc_sbuf_tensor_at("t%d" % cnt[0], list(shape), dt, offset=off)

    xs = alloc([128, 5, D], F32)
    hT = alloc([128, 8, TW], BF16)
    slots = [alloc([128, 8, 512], BF16) for _ in range(NSLOT)]
    ident = alloc([128, 128], BF16)
    identf = alloc([16, 16], F32)
    LT = alloc([128, 128], BF16)
    ones = alloc([128, 128], BF16)
    colmask = alloc([128, 16, 16], BF16)
    wsT = alloc([128, 8, 128], BF16)
    Bc = alloc([128, 16, 128], BF16)
    gb = [alloc([128, D], F32) for _ in range(2)]
    gT = alloc([128, 4, 8], F32)
    lgT = alloc([128, 16], F32)
    lbT = alloc([128, 16], F32)
    gn_bc = alloc([128, 256], F32)
    w00 = alloc([16, 8], F32)
    b00 = alloc([16, 8], F32)
    mhalf = alloc([128, 8], F32)
    NST = 1280
    stat = alloc([128, NST], F32)
    hb = [alloc([128, D], BF16) for _ in range(2)]
    tmp = alloc([128, 2, D], F32)
    junk = alloc([128, 2048], BF16)
    jrot = [0]

    def jk(n):
        if n > 1024:
            return 0, [("junk", 0), ("junk", 1)]
        i_ = jrot[0] % 2
        jrot[0] += 1
        return i_ * 1024, [("junk", i_)]
    Sst = alloc([128, 4, 256], F32)
    Sbf = alloc([128, 4, 256], BF16)
    Wg = alloc([32, 512], BF16)
    ph0 = cur[0]

    HBX0 = (229344 - 4 * 2048) // 64 * 64

    def phase_alloc():
        return [ph0]

    a = phase_alloc()
    ws_tm = alloc([128, 8, 128], F32, a)
    ws_bf = alloc([128, 8, 128], BF16, a)
    bs_bc = alloc([128, 8, 128], F32, a)
    a = phase_alloc()
    gated = alloc([128, 16, TW], BF16, a)
    vbuf = [alloc([128, 2048], BF16, a) for _ in range(3)]
    vhat = [alloc([128, 2048], BF16, a) for _ in range(3)]
    ut = [alloc([128, 512], BF16, a) for _ in range(2)]
    lg_bc = alloc([128, 2048], F32, a)
    lb_bc = alloc([128, 2048], F32, a)
    mixs = alloc([16, 2048], BF16, a)
    assert a[0] <= HBX0, (a[0], HBX0)
    a = phase_alloc()
    gTf = alloc([128, NFF, TW], BF16, a)
    sab = [alloc([128, 512], BF16, a) for _ in range(2)]
    ystage = alloc([128, 4, D], F32, a)
    xnext = alloc([128, 4, D], F32, a)
    assert a[0] <= HBX0
    a_hbx = [HBX0]
    hbx = [alloc([128, D], BF16, a_hbx) for _ in range(4)]
    a = phase_alloc()
    Vt = alloc([128, 5, D], BF16, a)
    SG = alloc([128, 5, D], BF16, a)
    qT = alloc([128, 4, TW], BF16, a)
    kT = alloc([128, 4, TW], BF16, a)
    rT = alloc([32, TW], BF16, a)
    off_e32 = a[0]
    e32 = alloc([128, 512], F32, a)
    Lb = alloc([128, 5, 512], BF16, a)
    og = [alloc([128, D], BF16, a) for _ in range(2)]
    ogT = [alloc([128, 8, 128], BF16, a) for _ in range(2)]
    Ktok_s = alloc([16, 512], BF16, a)
    EL = alloc([128, 16], F32, a)
    a_mark = a[0]
    Ep = [alloc([128, 4, 128], F32, a)] * 2
    Em = [alloc([128, 4, 128], F32, a)] * 2
    Q32 = [alloc([128, 4, 128], F32, a) for _ in range(2)]
    K32 = [alloc([128, 4, 128], F32, a) for _ in range(2)]
    Ktok = [alloc([128, 4, 128], BF16, a) for _ in range(4)]
    ATb = [alloc([128, 4, 128], BF16, a) for _ in range(4)]
    Kh = [alloc([128, 4, 128], BF16, a) for _ in range(2)]
    a = [a_mark]
    S_in = [alloc([128, 4, 256], F32, a) for _ in range(2)] + [alloc([128, 4, 256], F32, [off_e32])]
    Snew = [alloc([128, 4, 256], F32, a) for _ in range(2)]
    Snbf = [alloc([128, 4, 256], BF16, a) for _ in range(2)]
    Qm = alloc([128, 4, 16, 16], BF16, a)
    Kmask = [alloc([16, 512], BF16, a) for _ in range(2)]
    EGs = alloc([128, 4, 16], F32, a)

    ps = nc.alloc_psum_tensor("ps", [128, 6 * 512], F32)
    pt = nc.alloc_psum_tensor("pt", [128, 2, 1024], BF16)

    def bank(b, np_=128, n=512, off=0):
        return ps[0:np_, b * 512 + off: b * 512 + off + n]

    stc = [0]

    def newstat(n=1):
        c = stc[0]
        stc[0] += n
        assert stc[0] <= NST
        return c

    def S(c, np_=128, n=1):
        return stat[0:np_, c:c + n]

    def mm_group(out_ap, pairs, skip=False):
        def fn(e):
            n = len(pairs)
            ins = None
            for i, (l, r) in enumerate(pairs):
                if skip:
                    ins = e.matmul(out_ap, l, r, start=(i == 0), stop=(i == n - 1), skip_group_check=True)
                else:
                    ins = e.matmul(out_ap, l, r, start=(i == 0), stop=(i == n - 1))
            return ins
        return fn

    slot_ctr = [0]
    issued = [0]
    released = set()
    wmap = dict(sg_w_in=sg_w_in, sg_w_out=sg_w_out, ffn_w13=ffn_w13, ffn_w2=ffn_w2, gla_w_in=gla_w_in,
                gla_w_out=gla_w_out)

    first_reads = [[("xs", t_) for t_ in range(TPP)]]

    def issue_slab(i, spec):
        wname, li_, r0, nkc, c0, w = spec
        s_ = i % NSLOT
        src = wmap[wname][li_][r0:r0 + nkc * 128, c0:c0 + w].rearrange("(kc p) n -> p kc n", p=128)
        dst = slots[s_][:, 0:nkc, 0:w]
        rd = first_reads[0] if (schedule is not None and i == 0) else []
        P.op(POOL, lambda e: e.dma_start(out=dst, in_=src), reads=rd, writes=[("slot", s_)], dma="slot%d" % s_,
             nobarrier=True)

    def pump(maxn=None):
        while issued[0] < len(schedule) and (issued[0] < NSLOT or (issued[0] - NSLOT) in released) and \
                (maxn is None or issued[0] < maxn):
            issue_slab(issued[0], schedule[issued[0]])
            issued[0] += 1

    def wslab(wname, li_, r0, nkc, c0, w):
        i = slot_ctr[0]
        slot_ctr[0] += 1
        s_ = i % NSLOT
        spec = (wname, li_, r0, nkc, c0, w)
        if schedule is None:
            specs.append(spec)
            issue_slab(i, spec)
        else:
            assert schedule[i] == spec, (i, schedule[i], spec)
            pump()
            assert issued[0] > i, ("slab not issuable (too many live slabs)", i)
        return slots[s_], ("slot", s_), i

    def wrelease(h):
        if schedule is None:
            return
        released.add(h[2])
        pump()

    def dma(out, in_, key, reads=(), writes=(), slow=False, eng=SP):
        if slow:
            fn = lambda e: e.dma_start(out=out, in_=in_, allow_slow_non_contiguous=True)
        else:
            fn = lambda e: e.dma_start(out=out, in_=in_)
        return P.op(eng, fn, reads=reads, writes=writes, dma=key)

    def rstd_from_ss(c_ss, np_, n, inv, cols=1):
        c1 = newstat(cols)
        c2 = newstat(cols)
        P.op(DVE, lambda e: e.tensor_scalar(S(c1, np_, cols), S(c_ss, np_, cols), inv, EPS, ALU.mult, ALU.add),
             reads=[("st", c_ss)], writes=[("st", c1)])
        P.op(POOL, lambda e: e.tensor_tensor(out=S(c2, np_, cols), in0=S(c1, np_, cols), in1=mhalf[0:np_, 0:cols], op=ALU.pow),
             reads=[("st", c1), ("c", "mhalf")], writes=[("st", c2)])
        return c2

    hbc = [0]
    ptc = [0]

    def xsrc(t, np_, nx):
        if nx:
            return xnext[0:np_, t, :], ("xnext", t)
        return xs[0:np_, t, :], ("xs", t)

    def norm_in_A_sq(t, np_, nx=False):
        c0 = newstat()
        j0, jkeys = jk(D)
        src, skey = xsrc(t, np_, nx)
        P.op(ACT, lambda e: e.activation(out=junk[0:np_, j0:j0 + D], in_=src, func=AF.Square,
                                         accum_out=S(c0, np_)),
             reads=[skey], writes=[("st", c0)] + jkeys)
        return c0

    def hb_of(hi):
        if isinstance(hi, tuple):
            return hbx[hi[1]], ("hbx", hi[1])
        return hb[hi], ("hb", hi)

    def norm_in_A_rest(t, np_, c0, xi=None, nx=False):
        c2 = rstd_from_ss(c0, np_, D, 1.0 / D)
        src, skey = xsrc(t, np_, nx)
        if xi is None:
            hi = hbc[0] % 2
            hbc[0] += 1
        else:
            hi = ("x", xi)
        buf, key = hb_of(hi)
        P.op(ACT, lambda e: e.activation(out=buf[0:np_, :], in_=src, func=AF.Copy,
                                         scale=S(c2, np_)),
             reads=[skey, ("st", c2)], writes=[key])
        return hi

    def norm_in_A(t, np_, xi=None, nx=False):
        return norm_in_A_rest(t, np_, norm_in_A_sq(t, np_, nx), xi, nx)

    def norm_in_B(t, np_, hi, gidx):
        pb = ptc[0] % 2
        ptc[0] += 1
        col0 = t * 128
        hbuf, hkey = hb_of(hi)

        def tr(e):
            ins = None
            for kc in range(8):
                ins = e.transpose(pt[:, pb, kc * 128: kc * 128 + np_], hbuf[0:np_, kc * 128:(kc + 1) * 128],
                                  ident[0:np_, 0:np_])
            return ins
        P.op(PE, tr, reads=[hkey, ("c", "ident")], writes=[("pt", pb)])
        src = pt[:, pb, :].rearrange("p (k c) -> p k c", c=128)[:, :, 0:np_]
        P.op(DVE, lambda e: e.tensor_tensor(out=hT[:, :, col0:col0 + np_], in0=src,
                                            in1=gT[:, gidx, :].unsqueeze(2).to_broadcast([128, 8, np_]),
                                            op=ALU.mult),
             reads=[("pt", pb), ("c", "gT", gidx)], writes=[("hT", t)])

    def norm_out_sq(t, np_, b0):
        m = ps[0:np_, b0 * 512:(b0 + 2) * 512]
        c0 = newstat()
        j0, jkeys = jk(D)
        P.op(ACT, lambda e: e.activation(out=junk[0:np_, j0:j0 + D], in_=m, func=AF.Square, accum_out=S(c0, np_)),
             reads=[("ps", b0), ("ps", b0 + 1)], writes=[("st", c0)] + jkeys)
        return c0

    def norm_out_rest(t, np_, b0, gbi, c0, ys=False):
        m = ps[0:np_, b0 * 512:(b0 + 2) * 512]
        c2 = rstd_from_ss(c0, np_, D, 1.0 / D)
        P.op(DVE, lambda e: e.scalar_tensor_tensor(out=tmp[0:np_, 0, :], in0=m, scalar=S(c2, np_),
                                                   in1=gb[gbi][0:np_, :], op0=ALU.mult, op1=ALU.mult),
             reads=[("ps", b0), ("ps", b0 + 1), ("st", c2), ("gb", gbi)], writes=[("tmp", 0)])
        if ys:
            P.op(DVE, lambda e: e.tensor_tensor(out=ystage[0:np_, t, :], in0=xs[0:np_, t, :], in1=tmp[0:np_, 0, :],
                                                op=ALU.add),
                 reads=[("tmp", 0), ("xs", t)], writes=[("ystage", t)])
        else:
            P.op(DVE, lambda e: e.tensor_tensor(out=xs[0:np_, t, :], in0=xs[0:np_, t, :], in1=tmp[0:np_, 0, :],
                                                op=ALU.add),
                 reads=[("tmp", 0), ("xs", t)], writes=[("xs", t)])

    def norm_out2(t, np_, bA, bB, gbi, ys=False):
        ca = newstat(2)
        for hh, bk in enumerate((bA, bB)):
            j0, jkeys = jk(512)
            P.op(ACT, lambda e, hh=hh, bk=bk, j0=j0: e.activation(out=junk[0:np_, j0:j0 + 512], in_=bank(bk, np_), func=AF.Square,
                                                                  accum_out=S(ca + hh, np_)),
                 reads=[("ps", bk)], writes=[("st", ca + hh)] + jkeys)
        c0 = newstat()
        P.op(DVE, lambda e: e.tensor_tensor(out=S(c0, np_), in0=S(ca, np_), in1=S(ca + 1, np_), op=ALU.add),
             reads=[("st", ca), ("st", ca + 1)], writes=[("st", c0)])
        c2 = rstd_from_ss(c0, np_, D, 1.0 / D)
        for hh, bk in enumerate((bA, bB)):
            P.op(DVE, lambda e, hh=hh, bk=bk: e.scalar_tensor_tensor(
                out=tmp[0:np_, 0, hh * 512:(hh + 1) * 512], in0=bank(bk, np_), scalar=S(c2, np_),
                in1=gb[gbi][0:np_, hh * 512:(hh + 1) * 512], op0=ALU.mult, op1=ALU.mult),
                reads=[("ps", bk), ("st", c2), ("gb", gbi)], writes=[("tmp", 0)])
        if ys:
            P.op(DVE, lambda e: e.tensor_tensor(out=ystage[0:np_, t, :], in0=xs[0:np_, t, :], in1=tmp[0:np_, 0, :],
                                                op=ALU.add),
                 reads=[("tmp", 0), ("xs", t)], writes=[("ystage", t)])
        else:
            P.op(DVE, lambda e: e.tensor_tensor(out=xs[0:np_, t, :], in0=xs[0:np_, t, :], in1=tmp[0:np_, 0, :],
                                                op=ALU.add),
                 reads=[("tmp", 0), ("xs", t)], writes=[("xs", t)])

    def norm_out(t, np_, b0, gbi, ys=False):
        norm_out_rest(t, np_, b0, gbi, norm_out_sq(t, np_, b0), ys)

    def mk_sel(tile_, fill0, pattern, cmp, fill, cm):
        def fn(e):
            e.memset(tile_[:], fill0)
            return e.affine_select(out=tile_[:], in_=tile_[:], pattern=pattern, compare_op=cmp, fill=fill,
                                   base=0, channel_multiplier=cm)
        return fn
    P.op(POOL, lambda e: e.memset(ident[:], 0.0), writes=[("c", "ident")])
    P.op(POOL, lambda e: e.affine_select(out=ident[:], in_=ident[:], pattern=[[-1, 128]], compare_op=ALU.not_equal,
                                         fill=1.0, base=0, channel_multiplier=1), reads=[("c", "ident")], writes=[("c", "ident")])
    P.op(POOL, lambda e: e.memset(identf[:], 0.0), writes=[("c", "identf")])
    P.op(POOL, lambda e: e.affine_select(out=identf[:], in_=identf[:], pattern=[[-1, 16]], compare_op=ALU.not_equal,
                                         fill=1.0, base=0, channel_multiplier=1), reads=[("c", "identf")], writes=[("c", "identf")])
    P.op(POOL, lambda e: e.memset(LT[:], 1.0), writes=[("c", "LT")])
    P.op(POOL, lambda e: e.affine_select(out=LT[:], in_=LT[:], pattern=[[1, 128]], compare_op=ALU.is_ge, fill=0.0,
                                         base=0, channel_multiplier=-1), reads=[("c", "LT")], writes=[("c", "LT")])
    P.op(POOL, lambda e: e.memset(ones[:], 1.0), writes=[("c", "ones")])
    P.op(POOL, lambda e: e.memset(mhalf[:], -0.5), writes=[("c", "mhalf")])
    P.op(POOL, lambda e: e.memset(colmask[:], 0.0), writes=[("c", "colmask")])
    P.op(POOL, lambda e: e.affine_select(out=colmask[:], in_=colmask[:], pattern=[[1, 16], [-1, 16]],
                                         compare_op=ALU.not_equal, fill=1.0, base=0, channel_multiplier=0),
         reads=[("c", "colmask")], writes=[("c", "colmask")])
    P.op(DVE, lambda e: e.memset(stat[:], 0.0), writes=[("c", "stat")])
    P.op(POOL, lambda e: e.dma_start(out=Wg[0:16, :], in_=gla_w_gate_up[0]), writes=[("Wg", 0)], dma="wg0")
    P.op(POOL, lambda e: e.dma_start(out=Wg[16:17, :], in_=gla_b_gate[0:1, :]), writes=[("Wg", 1)], dma="wg1")

    P.op(DVE, lambda e: e.memset(Sst[:], 0.0), writes=[("S", h) for h in range(4)])
    P.op(DVE, lambda e: e.memset(Sbf[:], 0.0), writes=[("Sbf",)])

    def load_x(p_, t_):
        r0_ = p_ * PTOK + t_ * 128
        dma(xs[:, t_, :], xp[r0_:r0_ + 128, :], "xl%d" % t_, writes=[("xs", t_)])
    for t_ in range(TPP):
        load_x(0, t_)
    P.set_global_barrier()
    for li in range(2):
        for jj, j in enumerate((0, 2)):
            dma(gT[:, li * 2 + jj, :], norm_g[li, j].rearrange("(k p) -> p k", p=128), "c2_%d" % (li * 2 + jj),
                writes=[("c", "gT", li * 2 + jj)], slow=True)

    if schedule is not None:
        pump(4)

    def late_setup():
        dma(ws_tm[:], sg_w_s[0].rearrange("g t s -> t g s"), "c0", writes=[("ws_tm",)])
        dma(bs_bc[:], sg_b_s[0].rearrange("g t -> (g t)").partition_broadcast(128).rearrange("p (g t) -> p g t", t=128),
            "c1", writes=[("bs_bc",)])
        dma(lgT[:], sg_ln_g[0].rearrange("(c p) -> p c", p=128), "c3", writes=[("c", "lgT")], slow=True)
        dma(lbT[:], sg_ln_b[0].rearrange("(c p) -> p c", p=128), "c4", writes=[("c", "lbT")], slow=True)
        dma(gn_bc[:], gla_out_norm_g[0].partition_broadcast(128), "c5", writes=[("c", "gn")])
        dma(w00[:], sg_w_s[0, :, 0, 0].partition_broadcast(16), "c6", writes=[("c", "w00")], slow=True)
        dma(b00[:], sg_b_s[0, :, 0].partition_broadcast(16), "c7", writes=[("c", "b00")], slow=True)

        P.op(POOL, lambda e: e.affine_select(out=ws_tm[:], in_=ws_tm[:], pattern=[[0, 8], [-1, 128]],
                                             compare_op=ALU.is_ge, fill=0.0, base=0, channel_multiplier=1),
             reads=[("ws_tm",)], writes=[("ws_tm",)])
        P.op(DVE, lambda e: e.tensor_copy(out=ws_bf[:], in_=ws_tm[:]), reads=[("ws_tm",)], writes=[("ws_bf",)])

        def tr_ws(e):
            ins = None
            for g in range(8):
                ins = e.transpose(pt[:, 0, g * 128:(g + 1) * 128], ws_bf[:, g, :], ident[:])
            return ins
        P.op(PE, tr_ws, reads=[("ws_bf",), ("c", "ident")], writes=[("pt", 0)])
        P.op(DVE, lambda e: e.tensor_copy(out=wsT[:], in_=pt[:, 0, :].rearrange("p (g t) -> p g t", t=128)),
             reads=[("pt", 0)], writes=[("c", "wsT")])
        for half in range(2):
            def rs(e, half=half):
                ins = None
                for gi in range(4):
                    g = half * 4 + gi
                    ins = e.matmul(bank(half, 128, 128, gi * 128), ones[:], wsT[:, g, :], start=True, stop=True)
                return ins
            P.op(PE, rs, reads=[("c", "wsT"), ("c", "ones")], writes=[("ps", half)])
            for gi in range(4):
                g = half * 4 + gi
                for cc in range(2):
                    c = 2 * g + cc
                    P.op(DVE, lambda e, half=half, gi=gi, g=g, c=c: e.scalar_tensor_tensor(
                        out=Bc[:, c, :], in0=bank(half, 128, 128, gi * 128), scalar=lbT[:, c:c + 1],
                        in1=bs_bc[:, g, :], op0=ALU.mult, op1=ALU.add),
                        reads=[("ps", half), ("c", "lbT"), ("bs_bc",)], writes=[("c", "Bc", c)])


    gbc = [0]
    setup_done = [False]
    carryB = []

    def pop_carry(tile=None):
        while carryB and (tile is None or carryB[0][0] == tile):
            norm_in_B(*carryB.pop(0))
            if tile is not None:
                break

    def load_gb(li, j):
        gi = gbc[0] % 2
        gbc[0] += 1
        dma(gb[gi][:], norm_g[li, j].partition_broadcast(128), "gb%d" % gi, writes=[("gb", gi)])
        return gi

    for p in range(NPASS):
        last = (p == NPASS - 1)
        tiles = [(t, 128) for t in range(TPP)] + ([(TPP, NS)] if last else [])
        groups = [(0, PTOK)] + ([(PTOK, TW)] if last else [])
        if p == 0:
            dma(xs[0:NS, TPP, :], xsam[:, :], "xl4", writes=[("xs", TPP)])
        todo = (tiles + [(TPP, NS)]) if p == 0 else []
        pend0 = None
        for (t, np_) in todo:
            hi = norm_in_A(t, np_)
            if pend0 is not None:
                norm_in_B(*pend0)
            pend0 = (t, np_, hi, 0)
        if pend0 is not None:
            norm_in_B(*pend0)

        for li in range(2):
            P.set_barrier()
            gbi_mix = load_gb(li, 1)
            if li == 0:
                if last:
                    dma(lg_bc[:], sg_ln_g[0].partition_broadcast(128), "c8", writes=[("lg_bc",)])
                    dma(lb_bc[:], sg_ln_b[0].partition_broadcast(128), "c9", writes=[("lb_bc",)])
                Wv = [wslab("sg_w_in", 0, 0, 8, 2048 + 512 * nb, 512) for nb in range(4)]
                vstate = {}

                def vA(t, np_):
                    vb = t % 3
                    cols = slice(t * 128, t * 128 + np_)
                    s1 = newstat(4)
                    for nb in range(4):
                        P.op(PE, mm_group(bank(nb, np_), [(hT[:, kc, cols], Wv[nb][0][:, kc, :]) for kc in range(8)]),
                             reads=[("hT", t), Wv[nb][1]], writes=[("ps", nb)])
                        P.op(ACT, lambda e, nb=nb: e.activation(
                            out=vbuf[vb][0:np_, nb * 512:(nb + 1) * 512], in_=bank(nb, np_), func=AF.Gelu,
                            accum_out=S(s1 + nb, np_)),
                            reads=[("ps", nb)], writes=[("v", vb, nb), ("st", s1 + nb)])
                    s2 = newstat()
                    P.op(ACT, lambda e: e.activation(
                        out=junk[0:np_, :], in_=vbuf[vb][0:np_, :], func=AF.Square, accum_out=S(s2, np_)),
                        reads=[("v", vb, nb) for nb in range(4)], writes=[("st", s2), ("junk", 0), ("junk", 1)])
                    vstate[t] = (s1, s2)

                def vA2(t, np_):
                    vb = t % 3
                    s1, s2 = vstate[t]
                    c = newstat(8)
                    rd = [("st", s1 + i) for i in range(4)] + [("st", s2)]
                    P.op(DVE, lambda e: e.reduce_sum(out=S(c, np_), in_=S(s1, np_, 4), axis=AX.X),
                         reads=rd, writes=[("st", c)])
                    P.op(DVE, lambda e: e.scalar_tensor_tensor(out=S(c + 3, np_), in0=S(c, np_), scalar=1.0 / (2048.0 * 2048.0),
                                                               in1=S(c, np_), op0=ALU.mult, op1=ALU.mult),
                         reads=[("st", c)], writes=[("st", c + 3)])
                    P.op(DVE, lambda e: e.scalar_tensor_tensor(out=S(c + 4, np_), in0=S(s2, np_), scalar=1.0 / 2048,
                                                               in1=S(c + 3, np_), op0=ALU.mult, op1=ALU.subtract),
                         reads=[("st", s2), ("st", c + 3)], writes=[("st", c + 4)])
                    P.op(DVE, lambda e: e.tensor_scalar(S(c + 5, np_), S(c + 4, np_), EPS, None, ALU.add),
                         reads=[("st", c + 4)], writes=[("st", c + 5)])
                    P.op(POOL, lambda e: e.tensor_tensor(out=S(c + 6, np_), in0=S(c + 5, np_), in1=mhalf[0:np_, 0:1], op=ALU.pow),
                         reads=[("st", c + 5), ("c", "mhalf")], writes=[("st", c + 6)])
                    P.op(DVE, lambda e: e.scalar_tensor_tensor(out=S(c + 7, np_), in0=S(c, np_), scalar=-1.0 / 2048,
                                                               in1=S(c + 6, np_), op0=ALU.mult, op1=ALU.mult),
                         reads=[("st", c), ("st", c + 6)], writes=[("st", c + 7)])
                    vkeys = [("v", vb, nb) for nb in range(4)]
                    special = last and (t == TPP - 1 or np_ == NS)
                    if np_ == 128:
                        P.op(DVE, lambda e: e.tensor_scalar(
                            vhat[vb][0:np_, :], vbuf[vb][0:np_, :], S(c + 6, np_), S(c + 7, np_), ALU.mult, ALU.add),
                            reads=vkeys + [("st", c + 6), ("st", c + 7)], writes=[("vhat", vb)])
                    if special:
                        tA = tmp[0:np_].rearrange("p a b -> p (a b)")
                        P.op(DVE, lambda e: e.tensor_scalar(
                            tA, vbuf[vb][0:np_, :], S(c + 6, np_), S(c + 7, np_), ALU.mult, ALU.add),
                            reads=vkeys + [("st", c + 6), ("st", c + 7)], writes=[("tmp", 0), ("tmp", 1)])
                        P.op(DVE, lambda e: e.tensor_tensor(out=tA, in0=tA, in1=lg_bc[0:np_, :], op=ALU.mult),
                             reads=[("tmp", 0), ("tmp", 1), ("lg_bc",)], writes=[("tmp", 0), ("tmp", 1)])
                        P.op(DVE, lambda e: e.tensor_tensor(out=tA, in0=tA, in1=lb_bc[0:np_, :], op=ALU.add),
                             reads=[("tmp", 0), ("tmp", 1), ("lb_bc",)], writes=[("tmp", 0), ("tmp", 1)])
                        if np_ == 128:
                            dma(sgvp[:, :], tA, "sgvp", reads=[("tmp", 0), ("tmp", 1)])
                        else:
                            dma(sgvs[:, :], tA, "sgvs", reads=[("tmp", 0), ("tmp", 1)])
                            tA3 = tA.rearrange("p (g c) -> p g c", c=256)
                            P.op(DVE, lambda e: e.tensor_tensor(
                                out=tA3, in0=tA3, in1=w00[:, :].unsqueeze(2).to_broadcast([NS, 8, 256]), op=ALU.mult),
                                reads=[("tmp", 0), ("tmp", 1), ("c", "w00")], writes=[("tmp", 0), ("tmp", 1)])
                            P.op(DVE, lambda e: e.tensor_tensor(
                                out=mixs[:, :].rearrange("p (g c) -> p g c", c=256), in0=tA3,
                                in1=b00[:, :].unsqueeze(2).to_broadcast([NS, 8, 256]), op=ALU.add),
                                reads=[("tmp", 0), ("tmp", 1), ("c", "b00")], writes=[("mixs",)])

                def vB(t, np_):
                    vb = t % 3
                    cols = slice(t * 128, t * 128 + np_)
                    if np_ == 128:
                        for cq in range(4):
                            b = 4 + cq % 2

                            def sp_mm(e, cq=cq, b=b):
                                ins = None
                                for ci in range(4):
                                    cc = 4 * cq + ci
                                    ins = e.matmul(bank(b, 128, 128, ci * 128), vhat[vb][:, cc * 128:(cc + 1) * 128],
                                                   wsT[:, cc // 2, :], start=True, stop=True)
                                return ins
                            P.op(PE, sp_mm, reads=[("vhat", vb), ("c", "wsT")], writes=[("ps", b)])
                            for ci in range(4):
                                cc = 4 * cq + ci
                                P.op(DVE, lambda e, b=b, ci=ci, cc=cc: e.scalar_tensor_tensor(
                                    out=gated[:, cc, cols], in0=bank(b, 128, 128, ci * 128), scalar=lgT[:, cc:cc + 1],
                                    in1=Bc[:, cc, :], op0=ALU.mult, op1=ALU.add),
                                    reads=[("ps", b), ("c", "lgT"), ("c", "Bc", cc)], writes=[("gated", cc, t)])
                    else:
                        pb = ptc[0] % 2
                        ptc[0] += 1

                        def tr_mix(e):
                            ins = None
                            for cc in range(16):
                                ins = e.transpose(pt[:, pb, cc * 16:(cc + 1) * 16], mixs[:, cc * 128:(cc + 1) * 128],
                                                  ident[0:NS, 0:NS])
                            return ins
                        P.op(PE, tr_mix, reads=[("mixs",), ("c", "ident")], writes=[("pt", pb)])
                        P.op(DVE, lambda e: e.tensor_copy(
                            out=gated[:, :, PTOK:TW], in_=pt[:, pb, 0:256].rearrange("p (c j) -> p c j", j=16)),
                            reads=[("pt", pb)], writes=[("gated", cc, t) for cc in range(16)])

                pv = []
                pop_carry(0)
                vtiles = (tiles[:-2] + [tiles[-1], tiles[-2]]) if last else tiles
                for (t, np_) in vtiles:
                    vA(t, np_)
                    pop_carry(t + 1)
                    if len(pv) == 2:
                        if p == 0 and not setup_done[0]:
                            late_setup()
                            setup_done[0] = True
                        vB(*pv.pop(0))
                    vA2(t, np_)
                    pv.append((t, np_))
                while pv:
                    vB(*pv.pop(0))

                for w_ in Wv:
                    wrelease(w_)
                uc = 0
                for j in range(4):
                    Wu = wslab("sg_w_in", 0, 0, 8, 512 * j, 512)
                    for ci in range(4):
                        cc = 4 * j + ci
                        for (t0, t1) in groups:
                            n = t1 - t0
                            b = uc % 6
                            ub = uc % 2
                            uc += 1
                            tl = list(range(TPP)) if t0 == 0 else [TPP]
                            P.op(PE, mm_group(bank(b, 128, n), [(Wu[0][:, kc, ci * 128:(ci + 1) * 128], hT[:, kc, t0:t1])
                                                               for kc in range(8)]),
                                 reads=[("hT", t) for t in tl] + [Wu[1]], writes=[("ps", b)])
                            P.op(ACT, lambda e, b=b, n=n, ub=ub: e.activation(out=ut[ub][:, 0:n], in_=bank(b, 128, n), func=AF.Gelu),
                                 reads=[("ps", b)], writes=[("ut", ub)])
                            P.op(DVE, lambda e, cc=cc, t0=t0, t1=t1, ub=ub, n=n: e.tensor_tensor(
                                out=gated[:, cc, t0:t1], in0=gated[:, cc, t0:t1], in1=ut[ub][:, 0:n], op=ALU.mult),
                                reads=[("ut", ub)] + [("gated", cc, t) for t in tl],
                                writes=[("gated", cc, t) for t in tl])
                    wrelease(Wu)
                Wo = {}
                for kh in range(2):
                    for nn in range(2):
                        Wo[(kh, nn)] = wslab("sg_w_out", 0, kh * 1024, 8, nn * 512, 512)
                pend = None
                for i, (t, np_) in enumerate(tiles):
                    b0 = (i % 3) * 2
                    cols = slice(t * 128, t * 128 + np_)
                    for nn in range(2):
                        P.op(PE, mm_group(bank(b0 + nn, np_), [(gated[:, kc, cols], Wo[(kc // 8, nn)][0][:, kc % 8, :])
                                                               for kc in range(16)]),
                             reads=[("gated", kc, t) for kc in range(16)] + [Wo[(0, nn)][1], Wo[(1, nn)][1]],
                             writes=[("ps", b0 + nn)])
                    if pend is not None:
                        norm_in_B(*pend)
                    norm_out(t, np_, b0, gbi_mix)
                    hi = norm_in_A(t, np_)
                    pend = (t, np_, hi, li * 2 + 1)
                carryB.append(pend)
                for w_ in Wo.values():
                    wrelease(w_)
            else:
                bcx = [0]
                def proj_unit(which, j, Wv_, t, np_):
                    b = bcx[0] % 4
                    bcx[0] += 1
                    cols = slice(t * 128, t * 128 + np_)
                    P.op(PE, mm_group(bank(b, np_), [(hT[:, kc, cols], Wv_[0][:, kc, :]) for kc in range(8)]),
                         reads=[("hT", t), Wv_[1]], writes=[("ps", b)])
                    if which == 0:
                        P.op(ACT, lambda e: e.activation(
                            out=Vt[0:np_, t, j * 512:(j + 1) * 512], in_=bank(b, np_), func=AF.Copy),
                            reads=[("ps", b)], writes=[("V", t, j)])
                    else:
                        P.op(ACT, lambda e: e.activation(
                            out=SG[0:np_, t, j * 512:(j + 1) * 512], in_=bank(b, np_), func=AF.Silu),
                            reads=[("ps", b)], writes=[("SG", t, j)])
                        sgv = SG[0:np_, t, j * 512:(j + 1) * 512].rearrange("p (h v) -> p h v", v=256)
                        P.op(POOL, lambda e: e.tensor_tensor(out=sgv, in0=sgv,
                                                             in1=gn_bc[0:np_, :].unsqueeze(1).to_broadcast([np_, 2, 256]),
                                                             op=ALU.mult),
                             reads=[("SG", t, j), ("c", "gn")], writes=[("SG", t, j)])

                def proj_tok(which, steps_):
                    if which == 0 and carryB:
                        Ws = [wslab("gla_w_in", 0, 0, 8, 1024 + 512 * j, 512) for j in range(2)]
                        for j in range(2):
                            for (t, np_) in tiles[:-1]:
                                proj_unit(which, j, Ws[j], t, np_)
                        pop_carry()
                        for j in range(2):
                            proj_unit(which, j, Ws[j], tiles[-1][0], tiles[-1][1])
                        for w_ in Ws:
                            wrelease(w_)
                        return
                    for j in range(2):
                        Wv_ = wslab("gla_w_in", 0, 0, 8, 1024 + 1024 * which + 512 * j, 512)
                        for (t, np_) in tiles:
                            pop_carry(t)
                            proj_unit(which, j, Wv_, t, np_)
                            for _ in range(2):
                                if steps_:
                                    f_, a_ = steps_.pop(0)
                                    f_(a_)
                        wrelease(Wv_)
                proj_tok(0, [])
                P.op(DVE, lambda e: e.memset(rT[:], 1.0), writes=[("rT",)])
                Wr = wslab("gla_w_in", 0, 0, 8, 3072, 16)
                for (t0, t1) in groups:
                    n = t1 - t0
                    tl = list(range(TPP)) if t0 == 0 else [TPP]
                    P.op(PE, mm_group(bank(0, 16, n), [(Wr[0][:, kc, 0:16], hT[:, kc, t0:t1]) for kc in range(8)]),
                         reads=[("hT", t) for t in tl] + [Wr[1]], writes=[("ps", 0)])
                    P.op(DVE, lambda e, t0=t0, t1=t1, n=n: e.tensor_copy(out=rT[0:16, t0:t1], in_=bank(0, 16, n)),
                         reads=[("ps", 0)], writes=[("rT",)])
                wrelease(Wr)
                for gi_, (t, np_) in enumerate(tiles):
                    gbk = 4 + gi_ % 2
                    cols = slice(t * 128, t * 128 + np_)
                    P.op(PE, lambda e, gbk=gbk, np_=np_, cols=cols: e.matmul(bank(gbk, np_), rT[0:17, cols], Wg[0:17, :],
                                                                            start=True, stop=True),
                         reads=[("rT",), ("Wg", 0), ("Wg", 1)], writes=[("ps", gbk)])
                    P.op(ACT, lambda e, gbk=gbk, np_=np_: e.activation(out=e32[0:np_, :], in_=bank(gbk, np_), func=AF.Exp, scale=-1.0),
                         reads=[("ps", gbk)], writes=[("e32",)])
                    P.op(ACT, lambda e, np_=np_, t=t: e.activation(out=Lb[0:np_, t, :], in_=e32[0:np_, :], func=AF.Ln, bias=1.0),
                         reads=[("e32",)], writes=[("l", t)])
                bc = bcx[0]
                for which in range(2):
                    Wq = wslab("gla_w_in", 0, 0, 8, 512 * which, 512)
                    for h in range(4):
                        for (t0, t1) in groups:
                            n = t1 - t0
                            b = bc % 4
                            bc += 1
                            tl = list(range(TPP)) if t0 == 0 else [TPP]
                            P.op(PE, mm_group(bank(b, 128, n), [(Wq[0][:, kc, h * 128:(h + 1) * 128], hT[:, kc, t0:t1])
                                                               for kc in range(8)]),
                                 reads=[("hT", t) for t in tl] + [Wq[1]], writes=[("ps", b)])
                            if which == 0:
                                P.op(ACT, lambda e, b=b, n=n, h=h, t0=t0, t1=t1: e.activation(
                                    out=qT[:, h, t0:t1], in_=bank(b, 128, n), func=AF.Copy, scale=float(128 ** -0.5)),
                                    reads=[("ps", b)], writes=[("qT", h, t0)])
                            else:
                                P.op(DVE, lambda e, b=b, n=n, h=h, t0=t0, t1=t1: e.tensor_copy(
                                    out=kT[:, h, t0:t1], in_=bank(b, 128, n)),
                                    reads=[("ps", b)], writes=[("kT", h, t0)])
                    if which == 1 and last:
                        b = bc % 4
                        bc += 1
                        P.op(PE, mm_group(bank(b, NS, 512), [(hT[:, kc, PTOK:TW], Wq[0][:, kc, :]) for kc in range(8)]),
                             reads=[("hT", TPP), Wq[1]], writes=[("ps", b)])
                        P.op(DVE, lambda e, b=b: e.tensor_copy(out=Ktok_s[:, :], in_=bank(b, NS, 512)),
                             reads=[("ps", b)], writes=[("Ktok_s",)])
                    wrelease(Wq)

                bcx[0] = bc
                def pre1(t):
                    eb = 0
                    cols = slice(t * 128, (t + 1) * 128)

                    def cum(e):
                        ins = None
                        for h in range(4):
                            ins = e.matmul(bank(4, 128, 128, h * 128), Lb[:, t, h * 128:(h + 1) * 128], LT[:, :],
                                           start=True, stop=True)
                        return ins
                    P.op(PE, cum, reads=[("l", t), ("c", "LT")], writes=[("ps", 4)])
                    P.op(ACT, lambda e: e.activation(out=Ep[eb][:].rearrange("p h t -> p (h t)"), in_=bank(4), func=AF.Exp,
                                                     scale=-1.0 / 16),
                         reads=[("ps", 4)], writes=[("Ep", eb)])
                    P.op(ACT, lambda e: e.activation(out=Em[eb][:].rearrange("p h t -> p (h t)"), in_=bank(4), func=AF.Exp,
                                                     scale=1.0 / 16),
                         reads=[("ps", 4)], writes=[("Em", eb)])
                    P.op(DVE, lambda e: e.tensor_tensor(out=Q32[t % 2][:], in0=qT[:, :, cols], in1=Ep[eb][:], op=ALU.mult),
                         reads=[("qT", h, 0) for h in range(4)] + [("Ep", eb), ("qTt", t)], writes=[("Q32", t % 2)])
                    P.op(DVE, lambda e: e.tensor_tensor(out=K32[t % 2][:], in0=kT[:, :, cols], in1=Em[eb][:], op=ALU.mult),
                         reads=[("kT", h, 0) for h in range(4)] + [("Em", eb), ("kTt", t)], writes=[("K32", t % 2)])
                    P.op(DVE, lambda e: e.tensor_tensor(out=qT[:, :, cols], in0=qT[:, :, cols], in1=Ep[eb][:], op=ALU.mult),
                         reads=[("qT", h, 0) for h in range(4)] + [("Ep", eb)], writes=[("qTt", t)])
                    P.op(DVE, lambda e: e.tensor_tensor(out=kT[:, :, cols], in0=kT[:, :, cols], in1=Em[eb][:], op=ALU.mult),
                         reads=[("kT", h, 0) for h in range(4)] + [("Em", eb)], writes=[("kTt", t)])
                    P.op(DVE, lambda e: e.tensor_copy(out=EL[:, t * 4:(t + 1) * 4], in_=Ep[eb][:, :, 127]),
                         reads=[("Ep", eb)], writes=[("EL", t)])
                    for h in range(4):
                        P.op(DVE, lambda e, h=h: e.tensor_scalar(Kh[t % 2][:, h, :], kT[:, h, cols], EL[:, t * 4 + h:t * 4 + h + 1],
                                                                 None, ALU.mult),
                             reads=[("kTt", t), ("EL", t)], writes=[("Kh", t % 2, h)])

                def pre2(t):
                    cols = slice(t * 128, (t + 1) * 128)
                    pb = ptc[0] % 2
                    ptc[0] += 1

                    def trk(e):
                        ins = None
                        for h in range(4):
                            ins = e.transpose(pt[:, pb, h * 128:(h + 1) * 128], Kh[t % 2][:, h, :], ident[:])
                        return ins
                    P.op(PE, trk, reads=[("Kh", t % 2, h) for h in range(4)] + [("c", "ident")], writes=[("pt", pb)])
                    P.op(ACT, lambda e: e.activation(out=Ktok[t][:].rearrange("p h k -> p (h k)"),
                                                     in_=pt[:, pb, 0:512], func=AF.Copy),
                         reads=[("pt", pb)], writes=[("Ktok", t)])

                def pre3(t):
                    cols = slice(t * 128, (t + 1) * 128)

                    def att(e):
                        ins = None
                        for h in range(4):
                            ins = e.matmul(bank(5, 128, 128, h * 128), K32[t % 2][:, h, :], Q32[t % 2][:, h, :], start=True, stop=True)
                        return ins
                    P.op(PE, att, reads=[("K32", t % 2), ("Q32", t % 2)], writes=[("ps", 5)])
                    P.op(DVE, lambda e: e.tensor_tensor(
                        out=ATb[t][:], in0=bank(5).rearrange("p (h t) -> p h t", t=128),
                        in1=LT[:, :].unsqueeze(1).to_broadcast([128, 4, 128]), op=ALU.mult),
                        reads=[("ps", 5), ("c", "LT")], writes=[("AT", t)])

                steps = [(pre1, 0), (pre1, 1), (pre2, 0), (pre3, 0), (pre1, 2), (pre2, 1), (pre3, 1), (pre1, 3),
                         (pre2, 2), (pre3, 2), (pre2, 3), (pre3, 3)]
                proj_tok(1, steps)
                while steps:
                    f_, a_ = steps.pop(0)
                    f_(a_)
                Wo2 = [wslab("gla_w_out", 0, 0, 8, nn * 512, 512) for nn in range(2)]

                def onorm(t, np_, ob):
                    c = newstat(4)
                    for h in range(4):
                        j0, jkeys = jk(256)
                        P.op(ACT, lambda e, h=h, j0=j0: e.activation(out=junk[0:np_, j0:j0 + 256], in_=bank(2 + h // 2, np_, 256, (h % 2) * 256),
                                                                     func=AF.Square, accum_out=S(c + h, np_)),
                             reads=[("ps", 2 + h // 2)], writes=[("st", c + h)] + jkeys)
                    c2 = newstat(4)
                    c3 = newstat(4)
                    P.op(DVE, lambda e: e.tensor_scalar(S(c2, np_, 4), S(c, np_, 4), 1.0 / 256, EPS, ALU.mult, ALU.add),
                         reads=[("st", c + h) for h in range(4)], writes=[("st", c2)])
                    P.op(POOL, lambda e: e.tensor_tensor(out=S(c3, np_, 4), in0=S(c2, np_, 4), in1=mhalf[0:np_, 0:4], op=ALU.pow),
                         reads=[("st", c2), ("c", "mhalf")], writes=[("st", c3)])
                    for h in range(4):
                        P.op(DVE, lambda e, h=h: e.scalar_tensor_tensor(
                            out=og[ob][0:np_, h * 256:(h + 1) * 256], in0=bank(2 + h // 2, np_, 256, (h % 2) * 256),
                            scalar=S(c3 + h, np_), in1=SG[0:np_, t, h * 256:(h + 1) * 256], op0=ALU.mult, op1=ALU.mult),
                            reads=[("ps", 2 + h // 2), ("st", c3), ("SG", t, h // 2)], writes=[("og", ob)])

                def trO(t, np_, ob):
                    pb = ptc[0] % 2
                    ptc[0] += 1

                    def tr(e):
                        ins = None
                        for kc in range(8):
                            ins = e.transpose(pt[:, pb, kc * 128: kc * 128 + np_], og[ob][0:np_, kc * 128:(kc + 1) * 128],
                                              ident[0:np_, 0:np_])
                        return ins
                    P.op(PE, tr, reads=[("og", ob), ("c", "ident")], writes=[("pt", pb)])
                    P.op(ACT, lambda e: e.activation(out=ogT[ob][:, :, 0:np_],
                                                     in_=pt[:, pb, :].rearrange("p (k c) -> p k c", c=128)[:, :, 0:np_],
                                                     func=AF.Copy),
                         reads=[("pt", pb)], writes=[("ogT", ob)])

                def wout_mm(t, np_, ob):
                    for nn in range(2):
                        P.op(PE, mm_group(bank(nn, np_), [(ogT[ob][:, kc, 0:np_], Wo2[nn][0][:, kc, :]) for kc in range(8)]),
                             reads=[("ogT", ob), Wo2[nn][1]], writes=[("ps", nn)])

                ptl = [t for (t, np_) in tiles if np_ == 128]
                Sflat = Sst[:].rearrange("p h v -> p (h v)")
                T_ = len(ptl)
                stt_ = {}
                m_ap = ps[:, 0:1024]
                for i in range(T_ + 4):
                    ta = ptl[i] if i < T_ else None
                    tb = ptl[i - 1] if 0 <= i - 1 < T_ else None
                    tc_ = ptl[i - 2] if 0 <= i - 2 < T_ else None
                    td = ptl[i - 3] if 0 <= i - 3 < T_ else None
                    te = ptl[i - 4] if 0 <= i - 4 < T_ else None
                    if tc_ is not None:
                        c2m = stt_[tc_]["m2"]
                        P.op(DVE, lambda e, c2m=c2m: e.scalar_tensor_tensor(out=tmp[:, 0, :], in0=m_ap, scalar=S(c2m),
                                                                            in1=gb[gbi_mix][:, :], op0=ALU.mult, op1=ALU.mult),
                             reads=[("ps", 0), ("ps", 1), ("st", c2m), ("gb", gbi_mix)], writes=[("tmp", 0)])
                    if ta is not None:
                        t = ta
                        cols = slice(t * 128, (t + 1) * 128)
                        ob = i % 2

                        def omm(e, t=t, cols=cols):
                            ins = None
                            for h in range(4):
                                o_ap = bank(2 + h // 2, 128, 256, (h % 2) * 256)
                                e.matmul(o_ap, qT[:, h, cols], Sbf[:, h, :], start=True, stop=False)
                                ins = e.matmul(o_ap, ATb[t][:, h, :], Vt[:, t, h * 256:(h + 1) * 256], start=False, stop=True)
                            return ins
                        P.op(PE, omm, reads=[("qTt", t), ("Sbf",), ("AT", t), ("V", t, 0), ("V", t, 1)],
                             writes=[("ps", 2), ("ps", 3)])

                        def smm(e, t=t):
                            ins = None
                            for h in range(4):
                                ins = e.matmul(bank(4 + h // 2, 128, 256, (h % 2) * 256), Ktok[t][:, h, :],
                                               Vt[:, t, h * 256:(h + 1) * 256], start=True, stop=True)
                            return ins
                        P.op(PE, smm, reads=[("Ktok", t), ("V", t, 0), ("V", t, 1)], writes=[("ps", 4), ("ps", 5)])
                        stt_[t] = dict(ob=ob)
                    if ta is not None:
                        c = newstat(4)
                        for h in range(4):
                            j0, jkeys = jk(256)
                            P.op(ACT, lambda e, h=h, c=c, j0=j0: e.activation(out=junk[:, j0:j0 + 256], in_=bank(2 + h // 2, 128, 256, (h % 2) * 256),
                                                                              func=AF.Square, accum_out=S(c + h)),
                                 reads=[("ps", 2 + h // 2)], writes=[("st", c + h)] + jkeys)
                        stt_[ta]["c"] = c
                    if ta is not None:
                        t = ta
                        for h in range(4):
                            P.op(DVE, lambda e, h=h, t=t: e.scalar_tensor_tensor(
                                out=Sst[:, h, :], in0=Sst[:, h, :], scalar=EL[:, t * 4 + h:t * 4 + h + 1],
                                in1=bank(4 + h // 2, 128, 256, (h % 2) * 256), op0=ALU.mult, op1=ALU.add),
                                reads=[("S", h), ("EL", t), ("ps", 4 + h // 2)], writes=[("S", h)])
                        P.op(ACT, lambda e: e.activation(out=Sbf[:].rearrange("p h v -> p (h v)"), in_=Sflat, func=AF.Copy),
                             reads=[("S", h) for h in range(4)], writes=[("Sbf",)])
                    if tb is not None:
                        wout_mm(tb, 128, stt_[tb]["ob"])
                    if ta is not None:
                        c = stt_[ta]["c"]
                        c2 = newstat(4)
                        c3 = newstat(4)
                        P.op(DVE, lambda e, c=c, c2=c2: e.tensor_scalar(S(c2, 128, 4), S(c, 128, 4), 1.0 / 256, EPS, ALU.mult, ALU.add),
                             reads=[("st", c + h) for h in range(4)], writes=[("st", c2)])
                        P.op(POOL, lambda e, c2=c2, c3=c3: e.tensor_tensor(out=S(c3, 128, 4), in0=S(c2, 128, 4), in1=mhalf[:, 0:4], op=ALU.pow),
                             reads=[("st", c2), ("c", "mhalf")], writes=[("st", c3)])
                        stt_[ta]["c3"] = c3
                    if td is not None:
                        x0 = norm_in_A_sq(td, 128)
                        stt_[td]["x2"] = rstd_from_ss(x0, 128, D, 1.0 / D)
                    if te is not None:
                        if (not last) and te == ptl[-1]:
                            carryB.append((te, 128, stt_[te]["hi"], li * 2 + 1))
                        else:
                            norm_in_B(te, 128, stt_[te]["hi"], li * 2 + 1)
                    if ta is not None:
                        t = ta
                        c3 = stt_[t]["c3"]
                        ob = stt_[t]["ob"]
                        for h in range(4):
                            P.op(DVE, lambda e, h=h, c3=c3, t=t, ob=ob: e.scalar_tensor_tensor(
                                out=og[ob][:, h * 256:(h + 1) * 256], in0=bank(2 + h // 2, 128, 256, (h % 2) * 256),
                                scalar=S(c3 + h), in1=SG[:, t, h * 256:(h + 1) * 256], op0=ALU.mult, op1=ALU.mult),
                                reads=[("ps", 2 + h // 2), ("st", c3), ("SG", t, h // 2)], writes=[("og", ob)])
                    if tc_ is not None:
                        P.op(POOL, lambda e, tc_=tc_: e.tensor_tensor(out=xs[:, tc_, :], in0=xs[:, tc_, :], in1=tmp[:, 0, :], op=ALU.add),
                             reads=[("tmp", 0), ("xs", tc_)], writes=[("xs", tc_)])
                    if td is not None:
                        x2 = stt_[td]["x2"]
                        hi = hbc[0] % 2
                        hbc[0] += 1
                        P.op(ACT, lambda e, td=td, x2=x2, hi=hi: e.activation(out=hb[hi][:, :], in_=xs[:, td, :], func=AF.Copy,
                                                                              scale=S(x2)),
                             reads=[("xs", td), ("st", x2)], writes=[("hb", hi)])
                        stt_[td]["hi"] = hi
                    if tb is not None:
                        stt_[tb]["m0"] = norm_out_sq(tb, 128, 0)
                        stt_[tb]["m2"] = rstd_from_ss(stt_[tb]["m0"], 128, D, 1.0 / D)
                    if ta is not None:
                        trO(ta, 128, stt_[ta]["ob"])
                if last:
                    dma(stp.rearrange("h k v -> k h v"), Sst[:], "stp", reads=[("S", h) for h in range(4)])
                    P.set_barrier()
                    t = TPP

                    def cum_s(e):
                        ins = None
                        for h in range(4):
                            ins = e.matmul(bank(1, 128, NS, h * NS), Lb[0:NS, t, h * 128:(h + 1) * 128], ident[0:NS, 0:NS],
                                           start=True, stop=True)
                        return ins
                    P.op(PE, cum_s, reads=[("l", t), ("c", "ident")], writes=[("ps", 1)])
                    P.op(ACT, lambda e: e.activation(out=EGs[:].rearrange("p h j -> p (h j)"), in_=bank(1, 128, 4 * NS),
                                                     func=AF.Exp, scale=-1.0 / 16),
                         reads=[("ps", 1)], writes=[("EGs",)])
                    for h in range(4):
                        P.op(DVE, lambda e, h=h: e.tensor_tensor(
                            out=Qm[:, h, :, :], in0=qT[:, h, PTOK:TW].unsqueeze(1).to_broadcast([128, NS, NS]),
                            in1=colmask[:], op=ALU.mult),
                            reads=[("qT", h, PTOK), ("c", "colmask")], writes=[("Qm", h)])

                    def s_pre(j):
                        sb = j % 2
                        pbk = 0 if sb == 0 else 4
                        P.op(DVE, lambda e: e.tensor_scalar(Kmask[sb][:, :], Ktok_s[:, :], identf[:, j:j + 1], None, ALU.mult),
                             reads=[("Ktok_s",), ("c", "identf")], writes=[("Kmask", sb)])

                        def outer(e):
                            ins = None
                            for h in range(4):
                                ins = e.matmul(bank(pbk + h // 2, 128, 256, (h % 2) * 256), Kmask[sb][:, h * 128:(h + 1) * 128],
                                               Vt[0:NS, t, h * 256:(h + 1) * 256], start=True, stop=True)
                            return ins
                        P.op(PE, outer, reads=[("Kmask", sb), ("V", t, 0), ("V", t, 1)], writes=[("ps", pbk), ("ps", pbk + 1)])

                    def s_outer(j):
                        sb = j % 2
                        pbk = 0 if sb == 0 else 4
                        for h in range(4):
                            P.op(DVE, lambda e, h=h: e.scalar_tensor_tensor(
                                out=Snew[sb][:, h, :], in0=S_in[j % 3][:, h, :], scalar=EGs[:, h, j:j + 1],
                                in1=bank(pbk + h // 2, 128, 256, (h % 2) * 256), op0=ALU.mult, op1=ALU.add),
                                reads=[("S_in", j % 3), ("EGs",), ("ps", pbk + h // 2)], writes=[("Snew", sb, h)])
                        if j + 3 < NS:
                            dma(S_in[j % 3][:], st_in[j + 3].rearrange("h k v -> k h v"), "sin%d" % (j % 3),
                                writes=[("S_in", j % 3)], eng=ACT)
                        P.op(ACT, lambda e: e.activation(out=Snbf[sb][:].rearrange("p h v -> p (h v)"),
                                                         in_=Snew[sb][:].rearrange("p h v -> p (h v)"), func=AF.Copy),
                             reads=[("Snew", sb, h) for h in range(4)], writes=[("Snbf", sb)])
                        dma(sts[j].rearrange("h k v -> k h v"), Snew[sb][:], "sout%d" % sb,
                            reads=[("Snew", sb, h) for h in range(4)])

                    def s_o(j):
                        sb = j % 2

                        def osm(e):
                            ins = None
                            for h in range(4):
                                ins = e.matmul(bank(2 + h // 2, NS, 256, (h % 2) * 256), Qm[:, h, j, :], Snbf[sb][:, h, :],
                                               start=(j == 0 and h % 2 == 0), stop=(j == NS - 1), skip_group_check=True)
                            return ins
                        P.op(PE, osm, reads=[("Qm", h) for h in range(4)] + [("Snbf", sb)], writes=[("ps", 2), ("ps", 3)])

                    for j0 in range(3):
                        dma(S_in[j0][:], st_in[j0].rearrange("h k v -> k h v"), "sin%d" % j0, writes=[("S_in", j0)], eng=ACT)
                    s_pre(0)
                    for j in range(NS):
                        if j + 1 < NS:
                            s_pre(j + 1)
                        s_outer(j)
                        if j >= 1:
                            s_o(j - 1)
                    s_o(NS - 1)
                    onorm(t, NS, 0)
                    trO(t, NS, 0)
                    wout_mm(t, NS, 0)
                    norm_out(t, NS, 0, gbi_mix)
                    hi = norm_in_A(t, NS)
                    carryB.append((t, NS, hi, li * 2 + 1))

            if li == 1:
                for w_ in Wo2:
                    wrelease(w_)
            P.set_barrier()
            gbi_ffn = load_gb(li, 3)
            if li == 1 and p + 1 < NPASS:
                for t_ in range(TPP):
                    r0_ = (p + 1) * PTOK + t_ * 128
                    dma(xnext[:, t_, :], xp[r0_:r0_ + 128, :], "xn%d" % t_, writes=[("xnext", t_)])
            w13 = ffn_w13[li]
            fc = [0]

            def ab_unit(j, Wa, Wb, ci, t0, t1, tl):
                n = t1 - t0
                ba = (fc[0] % 3) * 2
                sb = fc[0] % 2
                fc[0] += 1
                rk = [("hT", t) for t in tl]
                P.op(PE, mm_group(bank(ba, 128, n), [(Wa[0][:, kc, ci * 128:(ci + 1) * 128], hT[:, kc, t0:t1])
                                                    for kc in range(8)]),
                     reads=rk + [Wa[1]], writes=[("ps", ba)])
                P.op(PE, mm_group(bank(ba + 1, 128, n), [(Wb[0][:, kc, ci * 128:(ci + 1) * 128], hT[:, kc, t0:t1])
                                                        for kc in range(8)]),
                     reads=rk + [Wb[1]], writes=[("ps", ba + 1)])
                P.op(ACT, lambda e: e.activation(out=sab[sb][:, 0:n], in_=bank(ba, 128, n), func=AF.Silu),
                     reads=[("ps", ba)], writes=[("sab", sb)])
                P.op(DVE, lambda e: e.tensor_tensor(
                    out=gTf[:, j, t0:t1], in0=sab[sb][:, 0:n], in1=bank(ba + 1, 128, n), op=ALU.mult),
                    reads=[("sab", sb), ("ps", ba + 1)], writes=[("gTf", j, 0 if t0 < PTOK else PTOK)])

            NSPLIT = 4
            for jj in range(6):
                w = 512 if jj < 5 else 256
                Wa = wslab("ffn_w13", li, 0, 8, jj * 512, w)
                Wb = wslab("ffn_w13", li, 0, 8, DFF + jj * 512, w)
                deferred = []
                for ci in range(w // 128):
                    j = 4 * jj + ci
                    for (t0, t1) in groups:
                        tl = list(range(TPP)) if t0 == 0 else [TPP]
                        if jj == 0 and ci < NSPLIT and t0 == 0:
                            ab_unit(j, Wa, Wb, ci, 0, 384, [0, 1, 2])
                            deferred.append((j, Wa, Wb, ci, 384, 512, [3]))
                        elif jj == 0 and ci < NSPLIT:
                            deferred.append((j, Wa, Wb, ci, t0, t1, tl))
                        else:
                            ab_unit(j, Wa, Wb, ci, t0, t1, tl)
                    if jj == 0 and ci == NSPLIT - 1:
                        pop_carry()
                        for d_ in deferred:
                            ab_unit(*d_)
                        deferred = []
                wrelease(Wa)
                wrelease(Wb)
            W2 = {}
            for nn in range(2):
                for kg in range(3):
                    nk = 8 if kg < 2 else 6
                    W2[(kg, nn)] = wslab("ffn_w2", li, kg * 1024, nk, nn * 512, 512)
            final = (li == 1)
            pendq = []
            chained = []
            split = not last
            prefx = final and (p + 1 < NPASS)
            if split:
                pq = []
                for (t, np_) in tiles:
                    cols = slice(t * 128, t * 128 + np_)
                    P.op(PE, mm_group(bank(t, np_), [(gTf[:, kc, cols], W2[(kc // 8, 0)][0][:, kc % 8, :])
                                                     for kc in range(NFF)]),
                         reads=[("gTf", kc, 0) for kc in range(NFF)] + [W2[(kg, 0)][1] for kg in range(3)],
                         writes=[("ps", t)])
                    if prefx:
                        if pq:
                            norm_in_B(*pq.pop(0))
                        hi = norm_in_A(t, np_, xi=t, nx=True)
                        pq.append((t, np_, hi, 0))
                while prefx and pq:
                    norm_in_B(*pq.pop(0))
                for kg in range(3):
                    wrelease(W2[(kg, 0)])
            for i, (t, np_) in enumerate(tiles):
                b0 = (i % 3) * 2
                cols = slice(t * 128, t * 128 + np_)
                t0k = 0 if np_ == 128 else PTOK
                lagn = (len(tiles) + 1) if final else (2 if split else 1)
                if split:
                    bB = 4 + i % 2
                    P.op(PE, mm_group(bank(bB, np_), [(gTf[:, kc, cols], W2[(kc // 8, 1)][0][:, kc % 8, :])
                                                      for kc in range(NFF)]),
                         reads=[("gTf", kc, t0k) for kc in range(NFF)] + [W2[(kg, 1)][1] for kg in range(3)],
                         writes=[("ps", bB)])
                    while len(pendq) >= lagn:
                        norm_in_B(*pendq.pop(0))
                    norm_out2(t, np_, t, bB, gbi_ffn, ys=(final and np_ == 128))
                else:
                    for nn in range(2):
                        P.op(PE, mm_group(bank(b0 + nn, np_), [(gTf[:, kc, cols], W2[(kc // 8, nn)][0][:, kc % 8, :])
                                                               for kc in range(NFF)]),
                             reads=[("gTf", kc, t0k) for kc in range(NFF)] + [W2[(kg, nn)][1] for kg in range(3)],
                             writes=[("ps", b0 + nn)])
                    while len(pendq) >= lagn:
                        norm_in_B(*pendq.pop(0))
                    norm_out(t, np_, b0, gbi_ffn, ys=(final and np_ == 128))
                if final:
                    if np_ == 128:
                        r0 = p * PTOK + t * 128
                        dma(yp[r0:r0 + 128, :], ystage[:, t, :], "ys%d" % t, reads=[("ystage", t)])
                        if p + 1 < NPASS:
                            P.op(POOL, lambda e, t=t: e.tensor_copy(out=xs[:, t, :], in_=xnext[:, t, :]),
                                 reads=[("xnext", t)], writes=[("xs", t)])
                    else:
                        dma(ysam[:, :], xs[0:NS, t, :], "ys4", reads=[("xs", t)])
                else:
                    hi = norm_in_A(t, np_)
                    pendq.append((t, np_, hi, 2))
            while pendq:
                if len(pendq) == 1:
                    carryB.append(pendq.pop(0))
                else:
                    norm_in_B(*pendq.pop(0))
            for (t, np_) in chained:
                hi = norm_in_A(t, np_, xi=t)
                carryB.append((t, np_, hi, 0))
            for (kg_, nn_), w_ in W2.items():
                if not (split and nn_ == 0):
                    wrelease(w_)

    if schedule is None:
        return specs
    ops = P.ops
    needed = [False] * len(ops)
    for i, o in enumerate(ops):
        for d in o["deps"]:
            od = ops[d]
            if od["dma"] is None and o["dma"] is None and od["eng"] == PE and o["eng"] == PE:
                continue
            needed[d] = True
    streams = {}
    ev = [None] * len(ops)
    for i, o in enumerate(ops):
        s = P._stream(o["eng"], o["dma"])
        if o["dma"] is not None:
            streams[s] = streams.get(s, 0) + 16
            ev[i] = (s, streams[s])
        elif needed[i]:
            streams[s] = streams.get(s, 0) + 1
            ev[i] = (s, streams[s])
        else:
            streams.setdefault(s, 0)
    for e_ in (PE, ACT, DVE, POOL):
        streams.setdefault(e_, 0)
    sem_names = list(streams.keys())
    import contextlib
    with contextlib.ExitStack() as es:
        sems = {}
        for k, s in enumerate(sem_names):
            sems[s] = es.enter_context(nc.semaphore("s%d" % k))
        block = es.enter_context(nc.Block())

        def emit(engname):
            def body(e):
                waited = {}
                for i, o in enumerate(ops):
                    if o["eng"] != engname:
                        continue
                    want = {}
                    for d in o["deps"]:
                        od = ops[d]
                        if od["dma"] is None and o["dma"] is None and od["eng"] == PE and engname == PE:
                            continue
                        s, c = ev[d]
                        if c > want.get(s, 0):
                            want[s] = c
                    for s, c in want.items():
                        if waited.get(s, 0) >= c:
                            continue
                        e.wait_ge(sems[s], c)
                        waited[s] = c
                    ins = o["fn"](e)
                    if ev[i] is not None:
                        s, c = ev[i]
                        ins.then_inc(sems[s], 16 if o["dma"] is not None else 1)
                if engname == SP:
                    for s, c in streams.items():
                        if isinstance(s, tuple) and c > 0:
                            e.wait_ge(sems[s], c)
            return body

        block.sync(emit(SP))
        block.gpsimd(emit(POOL))
        block.vector(emit(DVE))
        block.scalar(emit(ACT))
        block.tensor(emit(PE))
    return nc


_CACHE = {}


def kernel(x_prompt, x_sample, state_gla, norm_g, ffn_w13, ffn_w2, sg_w_in, sg_ln_g, sg_ln_b, sg_w_s,
           sg_b_s, sg_w_out, gla_w_in, gla_w_gate_up, gla_b_gate, gla_out_norm_g, gla_w_out):
    f = lambda a: np.ascontiguousarray(np.asarray(a, dtype=np.float32))
    if "nc" not in _CACHE:
        _CACHE["nc"] = build_program()
    nc = _CACHE["nc"]
    shared = dict(norm_g=f(norm_g), ffn_w13=f(ffn_w13), ffn_w2=f(ffn_w2), sg_w_in=f(sg_w_in), sg_ln_g=f(sg_ln_g),
                  sg_ln_b=f(sg_ln_b), sg_w_s=f(sg_w_s), sg_b_s=f(sg_b_s), sg_w_out=f(sg_w_out), gla_w_in=f(gla_w_in),
                  gla_w_gate_up=f(gla_w_gate_up), gla_b_gate=f(gla_b_gate), gla_out_norm_g=f(gla_out_norm_g),
                  gla_w_out=f(gla_w_out))
    xpr = f(x_prompt)
    xsa = f(x_sample).reshape(128, D)
    stg = f(state_gla)[0]
    in_maps = []
    for c in range(NCORE):
        m = dict(shared)
        m["xp"] = xpr[c]
        m["xsam"] = np.ascontiguousarray(xsa[c * NS:(c + 1) * NS])
        m["st_in"] = np.ascontiguousarray(stg[c * NS:(c + 1) * NS])
        in_maps.append(m)
    res = run_bass_kernel_spmd(nc, in_maps, core_ids=list(range(NCORE)))
    R = res.results
    y_prompt = np.stack([R[c]["yp"] for c in range(NCORE)], 0).astype(np.float32)
    y_sample = np.concatenate([R[c]["ysam"] for c in range(NCORE)], 0).reshape(128, 1, D).astype(np.float32)
    sgv_p = np.stack([R[c]["sgvp"] for c in range(NCORE)], 0)[None].astype(np.float32)
    sgv_s = np.concatenate([R[c]["sgvs"] for c in range(NCORE)], 0).reshape(1, 128, 1, 2048).astype(np.float32)
    st_p = np.stack([R[c]["stp"] for c in range(NCORE)], 0)[None].astype(np.float32)
    st_s = np.concatenate([R[c]["sts"] for c in range(NCORE)], 0)[None].astype(np.float32)
    return (y_prompt, y_sample, sgv_p, sgv_s, st_p, st_s)
```
